# Optimizing a Trainium2 kernel written in Bass

```python
import jax, jax.numpy as jnp
from jax import lax
import numpy as np

D_MODEL = 1024
BATCH = 8
SEQ = 2048
DEPTH = 2

CHUNK = 64
N_MIXERS = 2
N_LAYERS_A = (DEPTH + 1) // 2
N_LAYERS_B = DEPTH // 2
RMS_EPS = 1e-6

GLA_HEADS = 4
KEY_DIM = D_MODEL // 2
VALUE_DIM = D_MODEL
HEAD_K = KEY_DIM // GLA_HEADS
HEAD_V = VALUE_DIM // GLA_HEADS
GATE_RANK = 16
GATE_NORMALIZER = 16.0
PROJ_A = 2 * KEY_DIM + 2 * VALUE_DIM + GATE_RANK
SPLITS_A = (KEY_DIM, 2 * KEY_DIM, 2 * KEY_DIM + VALUE_DIM, 2 * KEY_DIM + 2 * VALUE_DIM)

CONV_DIM = D_MODEL
CONV_WIDTH = 3

D_FF = 2816
FFN_CONV_WIDTH = 3

kernel_name = 'hybrid_gla_shortconv_convffn_trunk'


def _rmsnorm(x, g):
    xf = x.astype(jnp.float32)
    y = xf * lax.rsqrt(jnp.mean(xf * xf, axis=-1, keepdims=True) + RMS_EPS)
    return (y * g.astype(jnp.float32)).astype(x.dtype)


def _causal_dwconv(z, w):
    width, ch = w.shape
    return lax.conv_general_dilated(
        z, w[:, None, :].astype(z.dtype), window_strides=(1,),
        padding=[(width - 1, 0)], dimension_numbers=('NWC', 'WIO', 'NWC'),
        feature_group_count=ch)


def _gla_mixer(x, w_in, w_gate_up, b_gate, gn, w_out):
    bsz, seq, _ = x.shape
    n_chunks = seq // CHUNK
    proj = x @ w_in
    q, k, v, r, gl = jnp.split(proj, SPLITS_A, axis=-1)
    log_a = jax.nn.log_sigmoid((gl @ w_gate_up + b_gate).astype(jnp.float32)) / GATE_NORMALIZER

    def chunked(t, d):
        return t.astype(jnp.float32).reshape(bsz, n_chunks, CHUNK, GLA_HEADS, d)

    q = chunked(q, HEAD_K) * (HEAD_K ** -0.5)
    k = chunked(k, HEAD_K)
    v = chunked(v, HEAD_V)
    log_a = chunked(log_a, HEAD_K)
    cum = jnp.cumsum(log_a, axis=2)
    tot = cum[:, :, -1]
    k_dec = k * jnp.exp(tot[:, :, None] - cum)
    upd = jnp.einsum('bnlhk,bnlhv->nbhkv', k_dec, v)
    q_n = jnp.moveaxis(q, 1, 0)
    decay = jnp.moveaxis(jnp.exp(tot), 1, 0)

    def step(state, inp):
        q_c, d_c, u_c = inp
        state = d_c[..., None] * state + u_c
        return state, jnp.einsum('blhk,bhkv->blhv', q_c, state)

    s0 = jnp.zeros((bsz, GLA_HEADS, HEAD_K, HEAD_V), jnp.float32)
    _, o = lax.scan(step, s0, (q_n, decay, upd))
    o = jnp.moveaxis(o, 0, 1).reshape(bsz, seq, GLA_HEADS, HEAD_V)
    o = o * lax.rsqrt(jnp.mean(o * o, axis=-1, keepdims=True) + RMS_EPS)
    o = o.reshape(bsz, seq, VALUE_DIM) * gn.astype(jnp.float32) * jax.nn.silu(r.astype(jnp.float32))
    return o.astype(x.dtype) @ w_out


def _short_conv_mixer(x, w_in, conv_w, w_out):
    b_gate, c_gate, h = jnp.split(x @ w_in, 3, axis=-1)
    return (b_gate * _causal_dwconv(c_gate * h, conv_w)) @ w_out


def _conv_ffn(x, w_up, conv_w, w_down):
    g, u = jnp.split(x @ w_up, 2, axis=-1)
    return (jax.nn.silu(_causal_dwconv(g, conv_w)) * u) @ w_down


def setup_inputs(seed: int = 0) -> dict:
    key = jax.random.key(seed)
    ks = jax.random.split(key, 20)
    nrm = jax.random.normal
    f32 = jnp.float32
    d = D_MODEL
    return {
        'x': nrm(ks[0], (BATCH, SEQ, d), f32),
        'a_norm': 1.0 + 0.05 * nrm(ks[1], (N_LAYERS_A, d), f32),
        'a_w_in': nrm(ks[2], (N_LAYERS_A, d, PROJ_A), f32) * d ** -0.5,
        'a_w_gate_up': nrm(ks[3], (N_LAYERS_A, GATE_RANK, KEY_DIM), f32) * GATE_RANK ** -0.5,
        'a_b_gate': 0.1 * nrm(ks[4], (N_LAYERS_A, KEY_DIM), f32),
        'a_gn': 1.0 + 0.05 * nrm(ks[5], (N_LAYERS_A, VALUE_DIM), f32),
        'a_w_out': nrm(ks[6], (N_LAYERS_A, VALUE_DIM, d), f32) * VALUE_DIM ** -0.5,
        'b_norm': 1.0 + 0.05 * nrm(ks[7], (N_LAYERS_B, d), f32),
        'b_w_in': nrm(ks[8], (N_LAYERS_B, d, 3 * CONV_DIM), f32) * d ** -0.5,
        'b_conv': nrm(ks[9], (N_LAYERS_B, CONV_WIDTH, CONV_DIM), f32) * CONV_WIDTH ** -0.5,
        'b_w_out': nrm(ks[10], (N_LAYERS_B, CONV_DIM, d), f32) * CONV_DIM ** -0.5,
        'f_norm': 1.0 + 0.05 * nrm(ks[11], (DEPTH, d), f32),
        'f_w_up': nrm(ks[12], (DEPTH, d, 2 * D_FF), f32) * d ** -0.5,
        'f_conv': nrm(ks[13], (DEPTH, FFN_CONV_WIDTH, D_FF), f32) * FFN_CONV_WIDTH ** -0.5,
        'f_w_down': nrm(ks[14], (DEPTH, D_FF, d), f32) * D_FF ** -0.5,
        'final_norm': 1.0 + 0.05 * nrm(ks[15], (d,), f32),
    }


def reference(x, a_norm, a_w_in, a_w_gate_up, a_b_gate, a_gn, a_w_out,
              b_norm, b_w_in, b_conv, b_w_out,
              f_norm, f_w_up, f_conv, f_w_down, final_norm):
    for i in range(DEPTH):
        j = i // N_MIXERS
        if i % N_MIXERS == 0:
            h = _rmsnorm(x, a_norm[j])
            x = x + _gla_mixer(h, a_w_in[j], a_w_gate_up[j], a_b_gate[j], a_gn[j], a_w_out[j])
        else:
            h = _rmsnorm(x, b_norm[j])
            x = x + _short_conv_mixer(h, b_w_in[j], b_conv[j], b_w_out[j])
        h = _rmsnorm(x, f_norm[i])
        x = x + _conv_ffn(h, f_w_up[i], f_conv[i], f_w_down[i])
    return _rmsnorm(x, final_norm)
```

```python
from contextlib import ExitStack

import numpy as np
import concourse.bass as bass
import concourse.mybir as mybir
from concourse.bass_utils import run_bass_kernel_spmd

F32 = mybir.dt.float32
BF16 = mybir.dt.bfloat16
AF = mybir.ActivationFunctionType
ALU = mybir.AluOpType

D = 1024
T = 2048
NCORES = 8
KC = 8
DFF = 2816
NJ = 22
EPS = 1e-6
HEADS = 4
DK = 128
DV = 256
CH = 64
PROJ_A = 3088
NSLOT = 16

VC = {}
_c = 0
for _name, _n in [("a_norm", 8), ("b_norm", 8), ("f_norm0", 8), ("f_norm1", 8), ("final_norm", 8),
                  ("a_gn", 8), ("b_conv0", 8), ("b_conv1", 8), ("b_conv2", 8),
                  ("f_conv00", 22), ("f_conv01", 22), ("f_conv02", 22),
                  ("f_conv10", 22), ("f_conv11", 22), ("f_conv12", 22)]:
    VC[_name] = _c
    _c += _n
NV = _c
NCONST = 128 + 128 + 2


class Tok:
    __slots__ = ("sem", "val")

    def __init__(self, sem, val):
        self.sem = sem
        self.val = val


class Buf:
    __slots__ = ("name", "w", "r")

    def __init__(self, name=""):
        self.name = name
        self.w = None
        self.r = {}


class Q:
    def __init__(self, nc, eng, name, is_pe=False):
        self.eng = eng
        self.sem = nc.alloc_semaphore("q_" + name)
        self.n = 0
        self.seen = {}
        self.is_pe = is_pe
        self.name = name

    def wait(self, tok):
        if tok is None:
            return
        if self.seen.get(tok.sem, 0) >= tok.val:
            return
        self.eng.wait_ge(tok.sem, tok.val)
        self.seen[tok.sem] = tok.val

    def deps(self, reads, writes):
        for b in reads:
            self.wait(b.w)
        for b in writes:
            if b.w is not None and not (self.is_pe and b.w.sem is self.sem):
                self.wait(b.w)
            for s, t in b.r.items():
                if s is self.sem:
                    continue
                self.wait(t)

    def done(self, ins, reads, writes):
        self.n += 1
        ins.then_inc(self.sem, 1)
        tok = Tok(self.sem, self.n)
        for b in reads:
            b.r[self.sem] = tok
        for b in writes:
            b.w = tok
            b.r = {}
        return tok

    def op(self, reads, writes, fn):
        self.deps(reads, writes)
        return self.done(fn(), reads, writes)


class DmaQ:
    def __init__(self, nc, eng, name, nsem):
        self.eng = eng
        self.sems = [nc.alloc_semaphore("d_%s%d" % (name, i)) for i in range(nsem)]
        self.cnt = [0] * nsem
        self.i = 0
        self.seen = {}

    def wait(self, tok):
        if tok is None:
            return
        if self.seen.get(tok.sem, 0) >= tok.val:
            return
        self.eng.wait_ge(tok.sem, tok.val)
        self.seen[tok.sem] = tok.val

    def dma(self, out, in_, reads, writes, extra_waits=()):
        for t in extra_waits:
            self.wait(t)
        for b in reads:
            self.wait(b.w)
        for b in writes:
            self.wait(b.w)
            for t in b.r.values():
                self.wait(t)
        k = self.i % len(self.sems)
        self.i += 1
        self.cnt[k] += 16
        sem = self.sems[k]
        self.eng.dma_start(out=out, in_=in_).then_inc(sem, 16)
        tok = Tok(sem, self.cnt[k])
        for b in reads:
            b.r[sem] = tok
        for b in writes:
            b.w = tok
            b.r = {}
        return tok


class WBlock:
    def __init__(self, ap, buf, slots):
        self.ap = ap
        self.buf = buf
        self.slots = slots


class Ring:
    def __init__(self, nc, dq, nslot):
        self.nc = nc
        self.dq = dq
        self.nslot = nslot
        self.t = nc.alloc_sbuf_tensor("wring", [128, nslot * 1024], BF16)
        self.bufs = [Buf("ring%d" % i) for i in range(nslot)]
        self.head = 0

    def align(self):
        half = self.nslot // 2
        if self.head % half:
            self.head = (self.head // half + 1) * half
        if self.head >= self.nslot:
            self.head = 0

    def load(self, src, nparts, shape_free):
        nel = int(np.prod(shape_free))
        ns = (nel + 1023) // 1024
        assert ns <= self.nslot
        if self.head + ns > self.nslot:
            self.head = 0
        s0 = self.head
        self.head += ns
        slots = self.bufs[s0:s0 + ns]
        flat = self.t[0:nparts, s0 * 1024: s0 * 1024 + nel]
        if len(shape_free) == 2:
            dst = flat.rearrange("p (a b) -> p a b", b=shape_free[1])
        else:
            dst = flat
        self.dq.dma(dst, src, [], slots)
        return WBlock(dst, slots, slots)


class Prog:
    def __init__(self, stages, dbg=None):
        self.stages = stages
        self.dbg = dbg
        self.dumped = set()
        nc = bass.Bass("TRN2", target_bir_lowering=False)
        self.nc = nc
        dt = nc.dram_tensor
        self.x = dt("x", [T, D], F32, kind="ExternalInput").ap()
        self.vecs_d = dt("vecs", [128, NV], F32, kind="ExternalInput").ap()
        self.consts_d = dt("consts", [128, NCONST], F32, kind="ExternalInput").ap()
        self.a_w_in = dt("a_w_in", [D, PROJ_A], F32, kind="ExternalInput").ap()
        self.a_wgu = dt("a_wgu", [17, 512], F32, kind="ExternalInput").ap()
        self.a_w_out = dt("a_w_out", [D, D], F32, kind="ExternalInput").ap()
        self.b_w_in = dt("b_w_in", [D, 3 * D], F32, kind="ExternalInput").ap()
        self.b_w_out = dt("b_w_out", [D, D], F32, kind="ExternalInput").ap()
        self.f_w_up = dt("f_w_up", [2, D, 2 * DFF], F32, kind="ExternalInput").ap()
        self.f_w_down = dt("f_w_down", [2, DFF, D], F32, kind="ExternalInput").ap()
        self.out = dt("out", [T, D], F32, kind="ExternalOutput").ap()

        self.pe = Q(nc, nc.tensor, "pe", is_pe=True)
        self.act = Q(nc, nc.scalar, "act")
        self.dve = Q(nc, nc.vector, "dve")
        self.gp = Q(nc, nc.gpsimd, "gp")
        self.sp = DmaQ(nc, nc.sync, "sp", 4)
        self.pool = DmaQ(nc, nc.gpsimd, "pool", NSLOT)
        self.ring = Ring(nc, self.pool, NSLOT)

        self.XT = nc.alloc_sbuf_tensor("XT", [128, KC, T], F32)
        self.bXT = [[Buf("XT%d_%d" % (k, t)) for t in range(4)] for k in range(KC)]
        self.vecs = nc.alloc_sbuf_tensor("vecs_sb", [128, NV], F32)
        self.consts = nc.alloc_sbuf_tensor("consts_sb", [128, NCONST], F32)
        self.onesD = nc.alloc_sbuf_tensor("onesD", [128, 128], BF16)
        self.onesV = nc.alloc_sbuf_tensor("onesV", [128, 128], BF16)
        self.bconst = Buf("const")
        self.cst = nc.alloc_sbuf_tensor("cst", [128, 2], F32)
        self.PS = nc.alloc_psum_tensor("PS", [128, 8, 512], F32)
        self.bPS = [Buf("ps%d" % i) for i in range(8)]
        self.ident = self.consts[:, 0:128]
        self.M1 = self.consts[:, 128:256]
        self.M2 = self.consts[:, 256:258]

        self.build()

    def sbt(self, name, shape, dtype):
        self._uid = getattr(self, "_uid", 0) + 1
        return self.nc.sbuf_tensor("%s_u%d" % (name, self._uid), shape, dtype)

    def dump(self, name, ap, shape, dtype, bufs):
        if not self.dbg or name in self.dumped:
            return
        self.dumped.add(name)
        d = self.nc.dram_tensor("dbg_" + name, list(shape), dtype, kind="ExternalOutput").ap()
        self.barrier()
        for q in (self.pe, self.act, self.dve):
            if q.n > 0:
                self.sp.wait(Tok(q.sem, q.n))
        t = self.sp.dma(d, ap, bufs, [])
        for q in (self.pe, self.act, self.dve):
            q.wait(t)

    def vcol(self, name, j=0):
        c = VC[name] + j
        return self.vecs[:, c:c + 1]

    def barrier(self):
        qs = [self.pe, self.act, self.dve]
        for q in qs:
            for p in qs + [self.gp]:
                if p is not q and p.n > 0:
                    q.wait(Tok(p.sem, p.n))

    def evac_alt(self, idx):
        return self.act if idx % 2 == 0 else self.dve

    def copy_on(self, q, out, in_, reads, writes):
        if q is self.act:
            return q.op(reads, writes, lambda: self.nc.scalar.copy(out, in_))
        return q.op(reads, writes, lambda: self.nc.vector.tensor_copy(out, in_))

    def build(self):
        nc = self.nc
        st = self.stages
        self.sp.dma(self.vecs[:], self.vecs_d, [], [self.bconst])
        self.sp.dma(self.consts[:], self.consts_d, [], [self.bconst])
        self.dve.op([], [self.bconst], lambda: nc.vector.memset(self.onesD[:], 1.0 / D))
        self.dve.op([], [self.bconst], lambda: nc.vector.memset(self.onesV[:], 1.0 / DV))
        self.dve.op([], [self.bconst], lambda: nc.vector.memset(self.cst[:, 0:1], EPS))
        self.dve.op([], [self.bconst], lambda: nc.vector.memset(self.cst[:, 1:2], 1.0))
        self.load_x()
        self.pool.wait(self.x_toks[2])
        if "gla" in st:
            self.gla()
        with ExitStack() as eso:
            self.XN = eso.enter_context(self.sbt("xn_shared", [128, KC, T], BF16))
            self.bXNs = [[Buf("xn") for _ in range(4)] for _ in range(KC)]
            self.scr = self.norm_scratch(eso)
            self.ostg = [eso.enter_context(self.sbt("ostage%d" % i, [128, D], F32)) for i in range(2)]
            self.bost = [Buf("ost%d" % i) for i in range(2)]
            self.out_toks = []
            phases = [p for p in ("ffn0", "sconv", "ffn1") if p in st]
            gname = {"ffn0": "f_norm0", "sconv": "b_norm", "ffn1": "f_norm1"}
            do_fnorm = "fnorm" in st

            def norm_tile_for(ph):
                return lambda tt: self.norm_tile_to(self.scr, self.XN, self.bXNs, gname[ph], tt, tt)

            def final_tile(tt):
                if do_fnorm:
                    self.final_norm_tile(tt)
                if tt >= 1:
                    self.store_tile(tt - 1)

            if phases:
                for tt in range(4):
                    norm_tile_for(phases[0])(tt)
            for k, ph in enumerate(phases):
                nxt = norm_tile_for(phases[k + 1]) if k + 1 < len(phases) else final_tile
                if ph == "sconv":
                    self.sconv(nxt)
                else:
                    self.ffn(int(ph[-1]), nxt)
            if not phases:
                for tt in range(4):
                    final_tile(tt)
            self.store_tile(3)
            for t in self.out_toks[-4:]:
                self.sp.wait(t)
            for t in self.out_toks[-4:]:
                self.pe.wait(t)

    def load_x(self):
        nc = self.nc
        with ExitStack() as es:
            stg = [es.enter_context(self.sbt("xstage%d" % i, [128, 4, D], F32)) for i in range(2)]
            bst = [Buf("xst0"), Buf("xst1")]
            xv = self.x.rearrange("(g i p) d -> g p i d", i=4, p=128)
            self.x_toks = []
            n = 0
            for tg in range(4):
                s = tg % 2
                self.x_toks.append(self.sp.dma(stg[s][:], xv[tg], [], [bst[s]]))
                for kc in range(KC):
                    b = n % 8
                    n += 1

                    def tr():
                        ins = None
                        for i in range(4):
                            ins = nc.tensor.transpose(self.PS[:, b, i * 128:(i + 1) * 128],
                                                      stg[s][:, i, kc * 128:(kc + 1) * 128], self.ident)
                        return ins
                    self.pe.op([bst[s], self.bconst], [self.bPS[b]], tr)
                    self.copy_on(self.evac_alt(n), self.XT[:, kc, tg * 512:(tg + 1) * 512], self.PS[:, b, :],
                                 [self.bPS[b]], [self.bXT[kc][tg]])
            self.barrier()

    def norm_scratch(self, es):
        SQ = [es.enter_context(self.sbt("sq%d" % i, [128, KC, 512], BF16)) for i in range(2)]
        RST = [es.enter_context(self.sbt("rstd%d" % i, [128, 512], F32)) for i in range(2)]
        return (SQ, [Buf("sq0"), Buf("sq1")], RST, [Buf("rs0"), Buf("rs1")])

    def norm_stats_tile(self, scr, tt):
        nc = self.nc
        SQ, bSQ, RST, bRS = scr
        s = tt % 2
        c0 = tt * 512
        bank = 6 + s
        self.act.op([self.bXT[kc][tt] for kc in range(KC)], [bSQ[s]],
                    lambda: nc.scalar.activation(SQ[s][:], self.XT[:, :, c0:c0 + 512], AF.Square))

        def mm():
            ins = None
            for kc in range(KC):
                ins = nc.tensor.matmul(self.PS[:, bank, :], self.onesD[:], SQ[s][:, kc, :],
                                       start=(kc == 0), stop=(kc == KC - 1))
            return ins
        self.pe.op([bSQ[s], self.bconst], [self.bPS[bank]], mm)
        self.rstd_from_psum(RST[s][:], self.PS[:, bank, :], [self.bPS[bank]], bRS[s])
        return RST[s], bRS[s]

    def norm_tile_to(self, scr, XN, bXN, gname, t, tt):
        nc = self.nc
        rst, brs = self.norm_stats_tile(scr, tt)
        c0 = tt * 512
        for kc in range(KC):
            self.dve.op([brs, self.bXT[kc][tt]], [bXN[kc][t]],
                        lambda: nc.vector.scalar_tensor_tensor(
                            XN[:, kc, t * 512:(t + 1) * 512], self.XT[:, kc, c0:c0 + 512], self.vcol(gname, kc),
                            rst[:], ALU.mult, ALU.mult))

    def final_norm_tile(self, tt):
        nc = self.nc
        rst, brs = self.norm_stats_tile(self.scr, tt)
        c0 = tt * 512
        for kc in range(KC):
            self.dve.op([brs, self.bXT[kc][tt]], [self.bXT[kc][tt]],
                        lambda: nc.vector.scalar_tensor_tensor(
                            self.XT[:, kc, c0:c0 + 512], self.XT[:, kc, c0:c0 + 512],
                            self.vcol("final_norm", kc), rst[:], ALU.mult, ALU.mult))

    def store_tile(self, tt):
        nc = self.nc
        ov = self.out.rearrange("(i p) d -> i p d", p=128)
        for i in range(tt * 4, tt * 4 + 4):
            s = i % 2
            pb = (i % 3) * 2

            def tr():
                ins = None
                for kc in range(KC):
                    ins = nc.tensor.transpose(self.PS[:, pb + kc // 4, (kc % 4) * 128:(kc % 4 + 1) * 128],
                                              self.XT[:, kc, i * 128:(i + 1) * 128], self.ident)
                return ins
            self.pe.op([self.bXT[kc][tt] for kc in range(KC)] + [self.bconst],
                       [self.bPS[pb], self.bPS[pb + 1]], tr)
            self.copy_on(self.evac_alt(i), self.ostg[s][:], self.PS[:, pb:pb + 2, :],
                         [self.bPS[pb], self.bPS[pb + 1]], [self.bost[s]])
            self.out_toks.append(self.sp.dma(ov[i], self.ostg[s][:], [self.bost[s]], []))

    def rstd_from_psum(self, out, ps, bps, bout):
        nc = self.nc
        self.act.op(bps + [self.bconst], [bout],
                    lambda: nc.scalar.activation(out, ps, AF.Ln, bias=self.cst[:, 0:1]))
        self.act.op([bout], [bout], lambda: nc.scalar.activation(out, out, AF.Exp, scale=-0.5))

    def rmsnorm_to(self, es_tmp, XN, bXN, gname, t0=0, nt=4):
        scr = self.norm_scratch(es_tmp)
        for t in range(nt):
            self.norm_tile_to(scr, XN, bXN, gname, t, t0 + t)

    def ffn(self, l, next_tile):
        nc = self.nc
        groups = [(0, 8), (8, 7), (15, 7)]
        GMAX = 8
        with ExitStack() as es:
            XN = self.XN
            bXN = self.bXNs
            H = es.enter_context(self.sbt("ffn_h", [128, GMAX, T], BF16))
            bH = [[Buf("h%d_%d" % (j, t)) for t in range(2)] for j in range(GMAX)]
            Gb = [es.enter_context(self.sbt("ffn_g%d" % i, [128, 2 + 1024], F32)) for i in range(2)]
            A = [es.enter_context(self.sbt("ffn_a%d" % i, [128, 1024], F32)) for i in range(2)]
            bG = [Buf("g0"), Buf("g1")]
            bA = [Buf("a0"), Buf("a1")]
            self.dve.op([], [bG[0]], lambda: nc.vector.memset(Gb[0][:, 0:2], 0.0))
            wup = self.f_w_up[l].rearrange("(kc p) c -> p kc c", p=128)
            wdn = self.f_w_down[l].rearrange("(j p) c -> j p c", p=128)
            cw = lambda tap, j: self.vcol("f_conv%d%d" % (l, tap), j)
            unit = 0
            nbank = 0
            for (g0, gn) in groups:
                jj = 0
                while jj < gn:
                    nb = min(4, gn - jj)
                    j0 = g0 + jj
                    self.ring.align()
                    wg = self.ring.load(wup[:, :, j0 * 128:(j0 + nb) * 128], 128, [KC, nb * 128])
                    wu = self.ring.load(wup[:, :, DFF + j0 * 128:DFF + (j0 + nb) * 128], 128, [KC, nb * 128])
                    for jb in range(nb):
                        j = j0 + jb
                        for hh in range(2):
                            pb = (unit % 2) * 4
                            s = hh
                            unit += 1

                            def mm(w, bank0):
                                ins = None
                                for kc in range(KC):
                                    for t in range(2):
                                        ins = nc.tensor.matmul(
                                            self.PS[:, bank0 + t, :], w.ap[:, kc, jb * 128:(jb + 1) * 128],
                                            XN[:, kc, (hh * 2 + t) * 512:(hh * 2 + t + 1) * 512],
                                            start=(kc == 0), stop=(kc == KC - 1))
                                return ins
                            rxn = [bXN[kc][hh * 2 + t] for kc in range(KC) for t in range(2)]
                            self.pe.op(rxn + wg.buf, [self.bPS[pb], self.bPS[pb + 1]], lambda: mm(wg, pb))
                            self.pe.op(rxn + wu.buf, [self.bPS[pb + 2], self.bPS[pb + 3]], lambda: mm(wu, pb + 2))
                            psg = self.PS[:, pb:pb + 2, :]
                            psu = self.PS[:, pb + 2:pb + 4, :]
                            bg = [self.bPS[pb], self.bPS[pb + 1]]
                            bu = [self.bPS[pb + 2], self.bPS[pb + 3]]
                            self.act.op(bg, [bA[s]], lambda: nc.scalar.activation(
                                A[s][:], psg, AF.Copy, scale=cw(2, j)))
                            self.act.op(bg, [bG[s]], lambda: nc.scalar.copy(Gb[s][:, 2:2 + 1024], psg))
                            if hh == 1:
                                self.act.op([bG[0]], [bG[1]], lambda: nc.scalar.copy(
                                    Gb[1][:, 0:2], Gb[0][:, 1024:1026]))
                            self.dve.op([bG[s], bA[s]], [bA[s]], lambda: nc.vector.scalar_tensor_tensor(
                                A[s][:], Gb[s][:, 1:1025], cw(1, j), A[s][:], ALU.mult, ALU.add))
                            self.dve.op([bG[s], bA[s]], [bA[s]], lambda: nc.vector.scalar_tensor_tensor(
                                A[s][:], Gb[s][:, 0:1024], cw(0, j), A[s][:], ALU.mult, ALU.add))
                            self.act.op([bA[s]], [bA[s]], lambda: nc.scalar.activation(A[s][:], A[s][:], AF.Silu))
                            self.dve.op([bA[s]] + bu, [bH[jj + jb][hh]], lambda: nc.vector.tensor_tensor(
                                H[:, jj + jb, hh * 1024:(hh + 1) * 1024], A[s][:], psu, ALU.mult))
                    jj += nb
                self.ring.align()
                wd = [self.ring.load(wdn[g0 + k], 128, [D]) for k in range(gn)]
                for tt in range(4):
                    for m in range(KC):
                        b = nbank % 8
                        nbank += 1

                        def mmd():
                            ins = None
                            for k in range(gn):
                                ins = nc.tensor.matmul(self.PS[:, b, :], wd[k].ap[:, m * 128:(m + 1) * 128],
                                                       H[:, k, tt * 512:(tt + 1) * 512],
                                                       start=(k == 0), stop=(k == gn - 1))
                            return ins
                        rd = [bH[k][tt // 2] for k in range(gn)]
                        for k in range(gn):
                            rd += wd[k].buf
                        self.pe.op(rd, [self.bPS[b]], mmd)
                        self.dve.op([self.bPS[b], self.bXT[m][tt]], [self.bXT[m][tt]],
                                    lambda: nc.vector.tensor_tensor(
                                        self.XT[:, m, tt * 512:(tt + 1) * 512],
                                        self.XT[:, m, tt * 512:(tt + 1) * 512], self.PS[:, b, :], ALU.add))
                    if g0 == groups[-1][0] and tt >= 1:
                        next_tile(tt - 1)
            next_tile(3)

    def sconv(self, next_tile):
        nc = self.nc
        with ExitStack() as es:
            XN = self.XN
            bXN = self.bXNs
            Y = es.enter_context(self.sbt("sc_y", [128, KC, T], BF16))
            bY = [[Buf("y%d_%d" % (j, t)) for t in range(4)] for j in range(KC)]
            Cs = [es.enter_context(self.sbt("sc_c%d" % i, [128, 512], F32)) for i in range(2)]
            Zb = [es.enter_context(self.sbt("sc_z%d" % i, [128, 2 + 512], F32)) for i in range(2)]
            A = [es.enter_context(self.sbt("sc_a%d" % i, [128, 512], F32)) for i in range(2)]
            Bs = [es.enter_context(self.sbt("sc_b%d" % i, [128, 512], F32)) for i in range(2)]
            bB = [Buf("b0"), Buf("b1")]
            bC = [Buf("c0"), Buf("c1")]
            bZ = [Buf("z0"), Buf("z1")]
            bA = [Buf("a0"), Buf("a1")]
            win = self.b_w_in.rearrange("(kc p) c -> p kc c", p=128)
            wout = self.b_w_out.rearrange("(j p) c -> j p c", p=128)
            cw = lambda tap, j: self.vcol("b_conv%d" % tap, j)
            NB = 2
            for j0 in range(0, KC, NB):
                self.ring.align()
                ws = [self.ring.load(win[:, :, sg * D + j0 * 128: sg * D + (j0 + NB) * 128], 128, [KC, NB * 128])
                      for sg in range(3)]
                for jb in range(NB):
                    j = j0 + jb
                    for tt in range(4):
                        s = tt % 2
                        pb = s * 3

                        def mm(w, bank):
                            ins = None
                            for kc in range(KC):
                                ins = nc.tensor.matmul(self.PS[:, bank, :], w.ap[:, kc, jb * 128:(jb + 1) * 128],
                                                       XN[:, kc, tt * 512:(tt + 1) * 512],
                                                       start=(kc == 0), stop=(kc == KC - 1))
                            return ins
                        for sg in range(3):
                            self.pe.op([bXN[kc][tt] for kc in range(KC)] + ws[sg].buf, [self.bPS[pb + sg]],
                                       lambda: mm(ws[sg], pb + sg))
                        self.act.op([self.bPS[pb + 1]], [bC[s]], lambda: nc.scalar.copy(Cs[s][:], self.PS[:, pb + 1, :]))
                        self.act.op([self.bPS[pb]], [bB[s]], lambda: nc.scalar.copy(Bs[s][:], self.PS[:, pb, :]))
                        if tt == 0:
                            self.dve.op([], [bZ[s]], lambda: nc.vector.memset(Zb[s][:, 0:2], 0.0))
                        else:
                            self.act.op([bZ[1 - s]], [bZ[s]], lambda: nc.scalar.copy(
                                Zb[s][:, 0:2], Zb[1 - s][:, 512:514]))
                        self.dve.op([bC[s], self.bPS[pb + 2]], [bZ[s]], lambda: nc.vector.tensor_tensor(
                            Zb[s][:, 2:514], Cs[s][:], self.PS[:, pb + 2, :], ALU.mult))
                        self.act.op([bZ[s]], [bA[s]], lambda: nc.scalar.activation(
                            A[s][:], Zb[s][:, 2:514], AF.Copy, scale=cw(2, j)))
                        self.dve.op([bZ[s], bA[s]], [bA[s]], lambda: nc.vector.scalar_tensor_tensor(
                            A[s][:], Zb[s][:, 1:513], cw(1, j), A[s][:], ALU.mult, ALU.add))
                        self.dve.op([bZ[s], bA[s]], [bA[s]], lambda: nc.vector.scalar_tensor_tensor(
                            A[s][:], Zb[s][:, 0:512], cw(0, j), A[s][:], ALU.mult, ALU.add))
                        self.dve.op([bA[s], bB[s]], [bY[j][tt]], lambda: nc.vector.tensor_tensor(
                            Y[:, j, tt * 512:(tt + 1) * 512], A[s][:], Bs[s][:], ALU.mult))
            self.out_proj(wout, Y, lambda kc, tg: [bY[kc][tg]], range(4), lambda tg: tg, next_tile)
            next_tile(3)

    def out_proj(self, wout, Y, ybufs, tiles, gtile, next_tile=None):
        nc = self.nc
        self.ring.align()
        wo = [self.ring.load(wout[kc], 128, [D]) for kc in range(KC)]
        nb = 0
        for t in tiles:
            tg = gtile(t)
            for m in range(KC):
                b = 6 + nb % 2
                nb += 1

                def mmo():
                    ins = None
                    for kc in range(KC):
                        ins = nc.tensor.matmul(self.PS[:, b, :], wo[kc].ap[:, m * 128:(m + 1) * 128],
                                               Y[:, kc, t * 512:(t + 1) * 512],
                                               start=(kc == 0), stop=(kc == KC - 1))
                    return ins
                rd = []
                for kc in range(KC):
                    rd += ybufs(kc, t) + wo[kc].buf
                self.pe.op(rd, [self.bPS[b]], mmo)
                self.dve.op([self.bPS[b], self.bXT[m][tg]], [self.bXT[m][tg]],
                            lambda: nc.vector.tensor_tensor(
                                self.XT[:, m, tg * 512:(tg + 1) * 512],
                                self.XT[:, m, tg * 512:(tg + 1) * 512], self.PS[:, b, :], ALU.add))
            if next_tile is not None and t >= 1:
                next_tile(t - 1)

    def gla(self):
        nc = self.nc
        PS = self.PS
        bPS = self.bPS
        win = self.a_w_in.rearrange("(kc p) c -> p kc c", p=128)
        wout = self.a_w_out.rearrange("(j p) c -> j p c", p=128)
        nbank = [0]

        def nextbank():
            b = nbank[0] % 8
            nbank[0] += 1
            return b

        with ExitStack() as es:
            S = [es.enter_context(self.sbt("gl_s%d" % i, [128, HEADS, DV], F32)) for i in range(2)]
            bS = [[Buf("s") for _ in range(HEADS)] for _ in range(2)]
            WGU = es.enter_context(self.sbt("gl_wgu", [32, 512], BF16))
            bWGU = Buf("wgu")
            self.dve.op([], [bWGU], lambda: nc.vector.memset(WGU[:], 0.0))
            self.pool.dma(WGU[0:17, :], self.a_wgu, [], [bWGU])
            self.dve.op([], bS[1], lambda: nc.vector.memset(S[1][:], 0.0))
            for hf in range(2):
                with ExitStack() as esh:
                    QT = esh.enter_context(self.sbt("gl_qt", [128, HEADS, 1024], BF16))
                    bQT = [[Buf("qt") for _ in range(2)] for _ in range(HEADS)]
                    GL = esh.enter_context(self.sbt("gl_gl", [32, 1024], BF16))
                    bGL = Buf("gl")
                    KD = esh.enter_context(self.sbt("gl_kd", [128, 8, 512], BF16))
                    bKD = [Buf("kd") for _ in range(8)]
                    V = esh.enter_context(self.sbt("gl_v", [128, 8, 1024], BF16))
                    bV = [Buf("v") for _ in range(8)]
                    SR = esh.enter_context(self.sbt("gl_sr", [128, KC, 1024], BF16))
                    bSR = [[Buf("sr") for _ in range(2)] for _ in range(KC)]
                    DEC = esh.enter_context(self.sbt("gl_dec", [128, HEADS, 16], F32))
                    bDEC = [Buf("dec") for _ in range(8)]
                    with ExitStack() as esa:
                        XN = esa.enter_context(self.sbt("gl_xn", [128, KC, 1024], BF16))
                        bXN = [[Buf("xn") for _ in range(2)] for _ in range(KC)]
                        SPt = [esa.enter_context(self.sbt("gl_sp%d" % i, [128, 512], F32)) for i in range(2)]
                        E = [esa.enter_context(self.sbt("gl_e%d" % i, [128, 512], F32)) for i in range(2)]
                        bSP = [Buf("sp0"), Buf("sp1")]
                        bE = [Buf("e0"), Buf("e1")]
                        with ExitStack() as es2:
                            self.rmsnorm_to(es2, XN, bXN, "a_norm", t0=hf * 2, nt=2)
                        self.dve.op([], [bGL], lambda: nc.vector.memset(GL[:], 0.0))
                        self.dve.op([], [bGL], lambda: nc.vector.memset(GL[0:17, :], 1.0))
                        self.ring.align()
                        wgl = self.ring.load(win[:, :, 3072:3088], 128, [KC, 16])
                        for tt in range(2):
                            b = nextbank()

                            def mmg():
                                ins = None
                                for kc in range(KC):
                                    ins = nc.tensor.matmul(PS[0:16, b, :], wgl.ap[:, kc, :],
                                                           XN[:, kc, tt * 512:(tt + 1) * 512],
                                                           start=(kc == 0), stop=(kc == KC - 1))
                                return ins
                            self.pe.op([bXN[kc][tt] for kc in range(KC)] + wgl.buf, [bPS[b]], mmg)
                            self.act.op([bPS[b]], [bGL], lambda: nc.scalar.copy(
                                GL[0:16, tt * 512:(tt + 1) * 512], PS[0:16, b, :]))
                        wq = self.ring.load(win[:, :, 0:512], 128, [KC, 512])
                        for j in range(HEADS):
                            for tt in range(2):
                                b = nextbank()

                                def mmq():
                                    ins = None
                                    for kc in range(KC):
                                        ins = nc.tensor.matmul(PS[:, b, :], wq.ap[:, kc, j * 128:(j + 1) * 128],
                                                               XN[:, kc, tt * 512:(tt + 1) * 512],
                                                               start=(kc == 0), stop=(kc == KC - 1))
                                    return ins
                                self.pe.op([bXN[kc][tt] for kc in range(KC)] + wq.buf, [bPS[b]], mmq)
                                self.act.op([bPS[b]], [bQT[j][tt]], lambda: nc.scalar.activation(
                                    QT[:, j, tt * 512:(tt + 1) * 512], PS[:, b, :], AF.Copy, scale=float(DK) ** -0.5))
                        self.ring.align()
                        wk = self.ring.load(win[:, :, 512:1024], 128, [KC, 512])

                        def gate_pre(i):
                            s = i % 2
                            b1 = nextbank()
                            self.pe.op([bGL, bWGU], [bPS[b1]], lambda: nc.tensor.matmul(
                                PS[:, b1, :], GL[:, i * 128:(i + 1) * 128], WGU[:, :], start=True, stop=True))
                            self.act.op([bPS[b1]], [bSP[s]], lambda: nc.scalar.activation(
                                SPt[s][:], PS[:, b1, :], AF.Exp, scale=-1.0))
                            self.act.op([bSP[s], self.bconst], [bSP[s]], lambda: nc.scalar.activation(
                                SPt[s][:], SPt[s][:], AF.Ln, bias=self.cst[:, 1:2]))

                        gate_pre(0)
                        for i in range(8):
                            s = i % 2
                            if i + 1 < 8:
                                gate_pre(i + 1)
                            b4 = nextbank()

                            def mmk():
                                ins = None
                                for kc in range(KC):
                                    ins = nc.tensor.matmul(PS[:, b4, :], XN[:, kc, i * 128:(i + 1) * 128], wk.ap[:, kc, :],
                                                           start=(kc == 0), stop=(kc == KC - 1))
                                return ins
                            self.pe.op([bXN[kc][i // 4] for kc in range(KC)] + wk.buf, [bPS[b4]], mmk)
                            b2 = nextbank()
                            self.pe.op([bSP[s], self.bconst], [bPS[b2]], lambda: nc.tensor.matmul(
                                PS[:, b2, :], self.M1, SPt[s][:], start=True, stop=True))
                            b3 = nextbank()

                            def mmt():
                                ins = None
                                for h in range(HEADS):
                                    ins = nc.tensor.matmul(PS[:, b3, h * 2:h * 2 + 2], SPt[s][:, h * 128:(h + 1) * 128],
                                                           self.M2, start=True, stop=True)
                                return ins
                            self.pe.op([bSP[s], self.bconst], [bPS[b3]], mmt)
                            self.act.op([bPS[b2]], [bE[s]], lambda: nc.scalar.activation(E[s][:], PS[:, b2, :], AF.Exp))
                            self.act.op([bPS[b3]], [bDEC[i]], lambda: nc.scalar.activation(
                                DEC[:, :, 2 * i:2 * i + 2],
                                PS[:, b3, 0:8].rearrange("p (h c) -> p h c", c=2), AF.Exp))
                            self.dve.op([bPS[b4], bE[s]], [bKD[i]], lambda: nc.vector.tensor_tensor(
                                KD[:, i, :], PS[:, b4, :], E[s][:], ALU.mult))
                        self.ring.align()
                        for vb in range(2):
                            wv = self.ring.load(win[:, :, 1024 + vb * 512:1024 + (vb + 1) * 512], 128, [KC, 512])
                            for i in range(8):
                                b = nextbank()

                                def mmv():
                                    ins = None
                                    for kc in range(KC):
                                        ins = nc.tensor.matmul(PS[:, b, :], XN[:, kc, i * 128:(i + 1) * 128],
                                                               wv.ap[:, kc, :], start=(kc == 0), stop=(kc == KC - 1))
                                    return ins
                                self.pe.op([bXN[kc][i // 4] for kc in range(KC)] + wv.buf, [bPS[b]], mmv)
                                self.copy_on(self.evac_alt(i), V[:, i, vb * 512:(vb + 1) * 512], PS[:, b, :],
                                             [bPS[b]], [bV[i]])
                        self.ring.align()
                        for rb in range(2):
                            wr = self.ring.load(win[:, :, 2048 + rb * 512:2048 + (rb + 1) * 512], 128, [KC, 512])
                            for jb in range(4):
                                j = rb * 4 + jb
                                for tt in range(2):
                                    b = nextbank()

                                    def mmr():
                                        ins = None
                                        for kc in range(KC):
                                            ins = nc.tensor.matmul(PS[:, b, :], wr.ap[:, kc, jb * 128:(jb + 1) * 128],
                                                                   XN[:, kc, tt * 512:(tt + 1) * 512],
                                                                   start=(kc == 0), stop=(kc == KC - 1))
                                        return ins
                                    self.pe.op([bXN[kc][tt] for kc in range(KC)] + wr.buf, [bPS[b]], mmr)
                                    self.act.op([bPS[b]], [bSR[j][tt]], lambda: nc.scalar.activation(
                                        SR[:, j, tt * 512:(tt + 1) * 512], PS[:, b, :], AF.Silu))
                    self.dump("GL", GL[:], [32, 1024], BF16, [])
                    self.dump("QT", QT[:], [128, HEADS, 1024], BF16, [])
                    self.dump("KD", KD[:], [128, 8, 512], BF16, [])
                    self.dump("V", V[:], [128, 8, 1024], BF16, [])
                    self.dump("SR", SR[:], [128, KC, 1024], BF16, [])
                    self.dump("DEC", DEC[:], [128, HEADS, 16], F32, [])
                    self.barrier()
                    with ExitStack() as esb:
                        TQ = 4
                        TW = TQ * CH
                        SB = [esb.enter_context(self.sbt("gl_sb%d" % i, [128, HEADS, DV], BF16)) for i in range(2)]
                        bSB = [Buf("sb0"), Buf("sb1")]
                        OT2 = [esb.enter_context(self.sbt("gl_ot%d" % i, [128, KC, TW], F32)) for i in range(2)]
                        OSQ2 = [esb.enter_context(self.sbt("gl_osq%d" % i, [128, KC, TW], BF16)) for i in range(2)]
                        RS = esb.enter_context(self.sbt("gl_rs", [128, HEADS, TW], F32))
                        OF2 = [esb.enter_context(self.sbt("gl_of%d" % i, [128, KC, TW], BF16)) for i in range(2)]
                        TMP = [esb.enter_context(self.sbt("gl_tmp%d" % i, [128, TW], F32)) for i in range(2)]
                        bTMP = [Buf("tmp0"), Buf("tmp1")]
                        bOF2 = [[Buf("of") for _ in range(KC)] for _ in range(2)]
                        bOT2 = [[Buf("ot") for _ in range(TQ)] for _ in range(2)]
                        bOSQ2 = [[Buf("osq") for _ in range(TQ)] for _ in range(2)]
                        bRS = [Buf("rs") for _ in range(HEADS)]

                        def emit_upd(c):
                            i = c // 2
                            part = (c % 2) * 64
                            su = (c % 2) * 2

                            def mmu():
                                ins = None
                                for h in range(HEADS):
                                    ins = nc.tensor.matmul(
                                        PS[:, su + h // 2, (h % 2) * 256:(h % 2 + 1) * 256],
                                        KD[part:part + 64, i, h * 128:(h + 1) * 128],
                                        V[part:part + 64, i, h * 256:(h + 1) * 256], start=True, stop=True)
                                return ins
                            self.pe.op([bKD[i], bV[i]], [bPS[su], bPS[su + 1]], mmu)

                        def emit_state(c):
                            i = c // 2
                            su = (c % 2) * 2
                            sn = (hf * 16 + c) % 2
                            so = 1 - sn
                            for h in range(HEADS):
                                self.dve.op([bS[so][h], bDEC[i], bPS[su + h // 2]], [bS[sn][h]],
                                            lambda: nc.vector.scalar_tensor_tensor(
                                                S[sn][:, h, :], S[so][:, h, :], DEC[:, h, c:c + 1],
                                                PS[:, su + h // 2, (h % 2) * 256:(h % 2 + 1) * 256],
                                                ALU.mult, ALU.add))
                            self.act.op(bS[sn], [bSB[c % 2]], lambda: nc.scalar.copy(SB[c % 2][:], S[sn][:]))

                        def emit_o(c):
                            tq = c // TQ
                            cl = c % TQ
                            bo = 4 + c % 2

                            def mmo():
                                ins = None
                                for j in range(KC):
                                    h, hv = j // 2, j % 2
                                    ins = nc.tensor.matmul(PS[:, bo, j * 64:(j + 1) * 64],
                                                           SB[c % 2][:, h, hv * 128:(hv + 1) * 128],
                                                           QT[:, h, c * 64:(c + 1) * 64], start=True, stop=True)
                                return ins
                            self.pe.op([bSB[c % 2]] + [bQT[h][c // 8] for h in range(HEADS)], [bPS[bo]], mmo)
                            pso = PS[:, bo, :].rearrange("p (j l) -> p j l", l=64)
                            self.act.op([bPS[bo]], [bOT2[tq % 2][cl]], lambda: nc.scalar.copy(
                                OT2[tq % 2][:, :, cl * 64:(cl + 1) * 64], pso))
                            self.act.op([bPS[bo]], [bOSQ2[tq % 2][cl]], lambda: nc.scalar.activation(
                                OSQ2[tq % 2][:, :, cl * 64:(cl + 1) * 64], pso, AF.Square))

                        def emit_post(tq):
                            ot, osq, of = OT2[tq % 2], OSQ2[tq % 2], OF2[tq % 2]
                            bot, bosq, bof = bOT2[tq % 2], bOSQ2[tq % 2], bOF2[tq % 2]
                            for h in range(HEADS):
                                bn = 6 + h % 2

                                def mmn():
                                    nc.tensor.matmul(PS[:, bn, 0:TW], self.onesV[:], osq[:, 2 * h, :], start=True, stop=False)
                                    return nc.tensor.matmul(PS[:, bn, 0:TW], self.onesV[:], osq[:, 2 * h + 1, :],
                                                            start=False, stop=True)
                                self.pe.op(bosq + [self.bconst], [bPS[bn]], mmn)
                                self.rstd_from_psum(RS[:, h, :], PS[:, bn, 0:TW], [bPS[bn]], bRS[h])
                            self.ring.align()
                            wo = [self.ring.load(wout[kc], 128, [D]) for kc in range(KC)]
                            for j in range(KC):
                                self.dve.op(bot + [bRS[j // 2]], [bTMP[j % 2]], lambda: nc.vector.scalar_tensor_tensor(
                                    TMP[j % 2][:], ot[:, j, :], self.vcol("a_gn", j), RS[:, j // 2, :],
                                    ALU.mult, ALU.mult))
                                self.dve.op([bTMP[j % 2], bSR[j][tq // 2]], [bof[j]], lambda: nc.vector.tensor_tensor(
                                    of[:, j, :], TMP[j % 2][:], SR[:, j, tq * TW:(tq + 1) * TW], ALU.mult))
                            tg = hf * 2 + tq // 2
                            c0 = tg * 512 + (tq % 2) * TW
                            groups = []
                            for m in range(KC):
                                def grp(m=m):
                                    b = 6 + m % 2

                                    def mmo2():
                                        ins = None
                                        for kc in range(KC):
                                            ins = nc.tensor.matmul(PS[:, b, 0:TW], wo[kc].ap[:, m * 128:(m + 1) * 128],
                                                                   of[:, kc, :], start=(kc == 0), stop=(kc == KC - 1))
                                        return ins
                                    rd = list(bof)
                                    for kc in range(KC):
                                        rd += wo[kc].buf
                                    self.pe.op(rd, [bPS[b]], mmo2)
                                    self.dve.op([bPS[b], self.bXT[m][tg]], [self.bXT[m][tg]],
                                                lambda: nc.vector.tensor_tensor(
                                                    self.XT[:, m, c0:c0 + TW],
                                                    self.XT[:, m, c0:c0 + TW], PS[:, b, 0:TW], ALU.add))
                                groups.append(grp)
                            return groups

                        pending = []
                        emit_upd(0)
                        emit_state(0)
                        for c in range(16):
                            if c + 1 < 16:
                                emit_upd(c + 1)
                                emit_state(c + 1)
                            emit_o(c)
                            for _ in range(2):
                                if pending:
                                    pending.pop(0)()
                            if c % TQ == TQ - 1:
                                while pending:
                                    pending.pop(0)()
                                pending = emit_post(c // TQ)
                        while pending:
                            pending.pop(0)()
                    self.barrier()


def _chunkcols(v):
    v = np.asarray(v, dtype=np.float32)
    return np.ascontiguousarray(v.reshape(-1, 128).T)


def make_consts():
    c = np.zeros((128, NCONST), dtype=np.float32)
    c[:, 0:128] = np.eye(128, dtype=np.float32)
    lp = np.arange(128)[:, None]
    l = np.arange(128)[None, :]
    c[:, 128:256] = np.where((lp > l) & (lp // CH == l // CH), -1.0 / 16.0, 0.0)
    c[:, 256:258] = np.where(lp // CH == np.arange(2)[None, :], -1.0 / 16.0, 0.0)
    return c


def make_in_maps(inp, ncores=NCORES):
    vec = np.zeros((128, NV), dtype=np.float32)

    def put(name, v):
        a = _chunkcols(v)
        vec[:, VC[name]:VC[name] + a.shape[1]] = a
    put("a_norm", inp["a_norm"][0])
    put("b_norm", inp["b_norm"][0])
    put("f_norm0", inp["f_norm"][0])
    put("f_norm1", inp["f_norm"][1])
    put("final_norm", inp["final_norm"])
    put("a_gn", inp["a_gn"][0])
    for t in range(3):
        put("b_conv%d" % t, inp["b_conv"][0][t])
        put("f_conv0%d" % t, inp["f_conv"][0][t])
        put("f_conv1%d" % t, inp["f_conv"][1][t])
    wgu = np.ascontiguousarray(np.concatenate(
        [np.asarray(inp["a_w_gate_up"][0], dtype=np.float32),
         np.asarray(inp["a_b_gate"][0], dtype=np.float32)[None, :]], axis=0))
    shared = {
        "vecs": vec,
        "consts": make_consts(),
        "a_w_in": np.ascontiguousarray(inp["a_w_in"][0], dtype=np.float32),
        "a_wgu": wgu,
        "a_w_out": np.ascontiguousarray(inp["a_w_out"][0], dtype=np.float32),
        "b_w_in": np.ascontiguousarray(inp["b_w_in"][0], dtype=np.float32),
        "b_w_out": np.ascontiguousarray(inp["b_w_out"][0], dtype=np.float32),
        "f_w_up": np.ascontiguousarray(inp["f_w_up"], dtype=np.float32),
        "f_w_down": np.ascontiguousarray(inp["f_w_down"], dtype=np.float32),
    }
    x = np.asarray(inp["x"], dtype=np.float32)
    maps = []
    for c in range(ncores):
        m = dict(shared)
        m["x"] = np.ascontiguousarray(x[c])
        maps.append(m)
    return maps


ALL_STAGES = ("gla", "ffn0", "sconv", "ffn1", "fnorm")
_PROG_CACHE = {}


def get_prog(stages=ALL_STAGES):
    key = tuple(stages)
    if key not in _PROG_CACHE:
        _PROG_CACHE[key] = Prog(key)
    return _PROG_CACHE[key]


def kernel(**inputs):
    prog = get_prog(ALL_STAGES)
    in_maps = make_in_maps(inputs)
    res = run_bass_kernel_spmd(prog.nc, in_maps, core_ids=list(range(NCORES)))
    return np.stack([np.asarray(r["out"], dtype=np.float32) for r in res.results], axis=0)
```

```python
from contextlib import ExitStack

import numpy as np
import concourse.bass as bass
import concourse.mybir as mybir
from concourse.bass_utils import run_bass_kernel_spmd

F32 = mybir.dt.float32
BF16 = mybir.dt.bfloat16
AF = mybir.ActivationFunctionType
ALU = mybir.AluOpType

D = 1024
T = 2048
NCORES = 8
KC = 8
DFF = 2816
NJ = 22
EPS = 1e-6
HEADS = 4
DK = 128
DV = 256
CH = 64
PROJ_A = 3088
NSLOT = 16

VC = {}
_c = 0
for _name, _n in [("a_norm", 8), ("b_norm", 8), ("f_norm0", 8), ("f_norm1", 8), ("final_norm", 8),
                  ("a_gn", 8), ("b_conv0", 8), ("b_conv1", 8), ("b_conv2", 8),
                  ("f_conv00", 22), ("f_conv01", 22), ("f_conv02", 22),
                  ("f_conv10", 22), ("f_conv11", 22), ("f_conv12", 22)]:
    VC[_name] = _c
    _c += _n
NV = _c
NCONST = 128 + 128 + 2


class Tok:
    __slots__ = ("sem", "val")

    def __init__(self, sem, val):
        self.sem = sem
        self.val = val


class Buf:
    __slots__ = ("name", "w", "r")

    def __init__(self, name=""):
        self.name = name
        self.w = None
        self.r = {}


class Q:
    def __init__(self, nc, eng, name, is_pe=False):
        self.eng = eng
        self.sem = nc.alloc_semaphore("q_" + name)
        self.n = 0
        self.seen = {}
        self.is_pe = is_pe
        self.name = name

    def wait(self, tok):
        if tok is None:
            return
        if self.seen.get(tok.sem, 0) >= tok.val:
            return
        self.eng.wait_ge(tok.sem, tok.val)
        self.seen[tok.sem] = tok.val

    def deps(self, reads, writes):
        for b in reads:
            self.wait(b.w)
        for b in writes:
            if b.w is not None and not (self.is_pe and b.w.sem is self.sem):
                self.wait(b.w)
            for s, t in b.r.items():
                if s is self.sem:
                    continue
                self.wait(t)

    def done(self, ins, reads, writes):
        self.n += 1
        ins.then_inc(self.sem, 1)
        tok = Tok(self.sem, self.n)
        for b in reads:
            b.r[self.sem] = tok
        for b in writes:
            b.w = tok
            b.r = {}
        return tok

    def op(self, reads, writes, fn):
        self.deps(reads, writes)
        return self.done(fn(), reads, writes)


class DmaQ:
    def __init__(self, nc, eng, name, nsem):
        self.eng = eng
        self.sems = [nc.alloc_semaphore("d_%s%d" % (name, i)) for i in range(nsem)]
        self.cnt = [0] * nsem
        self.i = 0
        self.seen = {}

    def wait(self, tok):
        if tok is None:
            return
        if self.seen.get(tok.sem, 0) >= tok.val:
            return
        self.eng.wait_ge(tok.sem, tok.val)
        self.seen[tok.sem] = tok.val

    def dma(self, out, in_, reads, writes, extra_waits=()):
        for t in extra_waits:
            self.wait(t)
        for b in reads:
            self.wait(b.w)
        for b in writes:
            self.wait(b.w)
            for t in b.r.values():
                self.wait(t)
        k = self.i % len(self.sems)
        self.i += 1
        self.cnt[k] += 16
        sem = self.sems[k]
        self.eng.dma_start(out=out, in_=in_).then_inc(sem, 16)
        tok = Tok(sem, self.cnt[k])
        for b in reads:
            b.r[sem] = tok
        for b in writes:
            b.w = tok
            b.r = {}
        return tok


class WBlock:
    def __init__(self, ap, buf, slots):
        self.ap = ap
        self.buf = buf
        self.slots = slots


class Ring:
    def __init__(self, nc, dq, nslot):
        self.nc = nc
        self.dq = dq
        self.nslot = nslot
        self.t = nc.alloc_sbuf_tensor("wring", [128, nslot * 1024], BF16)
        self.bufs = [Buf("ring%d" % i) for i in range(nslot)]
        self.head = 0

    def align(self):
        half = self.nslot // 2
        if self.head % half:
            self.head = (self.head // half + 1) * half
        if self.head >= self.nslot:
            self.head = 0

    def load(self, src, nparts, shape_free):
        nel = int(np.prod(shape_free))
        ns = (nel + 1023) // 1024
        assert ns <= self.nslot
        if self.head + ns > self.nslot:
            self.head = 0
        s0 = self.head
        self.head += ns
        slots = self.bufs[s0:s0 + ns]
        flat = self.t[0:nparts, s0 * 1024: s0 * 1024 + nel]
        if len(shape_free) == 2:
            dst = flat.rearrange("p (a b) -> p a b", b=shape_free[1])
        else:
            dst = flat
        self.dq.dma(dst, src, [], slots)
        return WBlock(dst, slots, slots)


class Prog:
    def __init__(self, stages, dbg=None):
        self.stages = stages
        self.dbg = dbg
        self.dumped = set()
        nc = bass.Bass("TRN2", target_bir_lowering=False)
        self.nc = nc
        dt = nc.dram_tensor
        self.x = dt("x", [T, D], F32, kind="ExternalInput").ap()
        self.vecs_d = dt("vecs", [128, NV], F32, kind="ExternalInput").ap()
        self.consts_d = dt("consts", [128, NCONST], F32, kind="ExternalInput").ap()
        self.a_w_in = dt("a_w_in", [D, PROJ_A], F32, kind="ExternalInput").ap()
        self.a_wgu = dt("a_wgu", [17, 512], F32, kind="ExternalInput").ap()
        self.a_w_out = dt("a_w_out", [D, D], F32, kind="ExternalInput").ap()
        self.b_w_in = dt("b_w_in", [D, 3 * D], F32, kind="ExternalInput").ap()
        self.b_w_out = dt("b_w_out", [D, D], F32, kind="ExternalInput").ap()
        self.f_w_up = dt("f_w_up", [2, D, 2 * DFF], F32, kind="ExternalInput").ap()
        self.f_w_down = dt("f_w_down", [2, DFF, D], F32, kind="ExternalInput").ap()
        self.out = dt("out", [T, D], F32, kind="ExternalOutput").ap()

        self.pe = Q(nc, nc.tensor, "pe", is_pe=True)
        self.act = Q(nc, nc.scalar, "act")
        self.dve = Q(nc, nc.vector, "dve")
        self.gp = Q(nc, nc.gpsimd, "gp")
        self.sp = DmaQ(nc, nc.sync, "sp", 4)
        self.pool = DmaQ(nc, nc.gpsimd, "pool", NSLOT)
        self.ring = Ring(nc, self.pool, NSLOT)

        self.XT = nc.alloc_sbuf_tensor("XT", [128, KC, T], F32)
        self.bXT = [[Buf("XT%d_%d" % (k, t)) for t in range(4)] for k in range(KC)]
        self.vecs = nc.alloc_sbuf_tensor("vecs_sb", [128, NV], F32)
        self.consts = nc.alloc_sbuf_tensor("consts_sb", [128, NCONST], F32)
        self.onesD = nc.alloc_sbuf_tensor("onesD", [128, 128], BF16)
        self.onesV = nc.alloc_sbuf_tensor("onesV", [128, 128], BF16)
        self.bconst = Buf("const")
        self.cst = nc.alloc_sbuf_tensor("cst", [128, 2], F32)
        self.PS = nc.alloc_psum_tensor("PS", [128, 8, 512], F32)
        self.bPS = [Buf("ps%d" % i) for i in range(8)]
        self.ident = self.consts[:, 0:128]
        self.M1 = self.consts[:, 128:256]
        self.M2 = self.consts[:, 256:258]

        self.build()

    def sbt(self, name, shape, dtype):
        self._uid = getattr(self, "_uid", 0) + 1
        return self.nc.sbuf_tensor("%s_u%d" % (name, self._uid), shape, dtype)

    def dump(self, name, ap, shape, dtype, bufs):
        if not self.dbg or name in self.dumped:
            return
        self.dumped.add(name)
        d = self.nc.dram_tensor("dbg_" + name, list(shape), dtype, kind="ExternalOutput").ap()
        self.barrier()
        for q in (self.pe, self.act, self.dve):
            if q.n > 0:
                self.sp.wait(Tok(q.sem, q.n))
        t = self.sp.dma(d, ap, bufs, [])
        for q in (self.pe, self.act, self.dve):
            q.wait(t)

    def vcol(self, name, j=0):
        c = VC[name] + j
        return self.vecs[:, c:c + 1]

    def barrier(self):
        qs = [self.pe, self.act, self.dve]
        for q in qs:
            for p in qs + [self.gp]:
                if p is not q and p.n > 0:
                    q.wait(Tok(p.sem, p.n))

    def evac_alt(self, idx):
        return self.act if idx % 2 == 0 else self.dve

    def copy_on(self, q, out, in_, reads, writes):
        if q is self.act:
            return q.op(reads, writes, lambda: self.nc.scalar.copy(out, in_))
        return q.op(reads, writes, lambda: self.nc.vector.tensor_copy(out, in_))

    def build(self):
        nc = self.nc
        st = self.stages
        self.sp.dma(self.vecs[:], self.vecs_d, [], [self.bconst])
        self.sp.dma(self.consts[:], self.consts_d, [], [self.bconst])
        self.dve.op([], [self.bconst], lambda: nc.vector.memset(self.onesD[:], 1.0 / D))
        self.dve.op([], [self.bconst], lambda: nc.vector.memset(self.onesV[:], 1.0 / DV))
        self.dve.op([], [self.bconst], lambda: nc.vector.memset(self.cst[:, 0:1], EPS))
        self.dve.op([], [self.bconst], lambda: nc.vector.memset(self.cst[:, 1:2], 1.0))
        self.load_x()
        if "gla" in st:
            self.gla()
        with ExitStack() as eso:
            self.XN = eso.enter_context(self.sbt("xn_shared", [128, KC, T], BF16))
            self.bXNs = [[Buf("xn") for _ in range(4)] for _ in range(KC)]
            self.scr = self.norm_scratch(eso)
            self.ostg = [eso.enter_context(self.sbt("ostage%d" % i, [128, D], F32)) for i in range(2)]
            self.bost = [Buf("ost%d" % i) for i in range(2)]
            self.out_toks = []
            phases = [p for p in ("ffn0", "sconv", "ffn1") if p in st]
            gname = {"ffn0": "f_norm0", "sconv": "b_norm", "ffn1": "f_norm1"}
            do_fnorm = "fnorm" in st

            def norm_tile_for(ph):
                return lambda tt: self.norm_tile_to(self.scr, self.XN, self.bXNs, gname[ph], tt, tt)

            def final_tile(tt):
                if do_fnorm:
                    self.final_norm_tile(tt)
                if tt >= 1:
                    self.store_tile(tt - 1)

            if phases:
                for tt in range(4):
                    norm_tile_for(phases[0])(tt)
            for k, ph in enumerate(phases):
                nxt = norm_tile_for(phases[k + 1]) if k + 1 < len(phases) else final_tile
                if ph == "sconv":
                    self.sconv(nxt)
                else:
                    self.ffn(int(ph[-1]), nxt)
            if not phases:
                for tt in range(4):
                    final_tile(tt)
            self.store_tile(3)
            for t in self.out_toks[-4:]:
                self.sp.wait(t)
            for t in self.out_toks[-4:]:
                self.pe.wait(t)

    def load_x(self):
        nc = self.nc
        with ExitStack() as es:
            stg = [es.enter_context(self.sbt("xstage%d" % i, [128, 4, D], F32)) for i in range(2)]
            bst = [Buf("xst0"), Buf("xst1")]
            xv = self.x.rearrange("(g i p) d -> g p i d", i=4, p=128)
            n = 0
            for tg in range(4):
                s = tg % 2
                self.sp.dma(stg[s][:], xv[tg], [], [bst[s]])
                for kc in range(KC):
                    b = n % 8
                    n += 1

                    def tr():
                        ins = None
                        for i in range(4):
                            ins = nc.tensor.transpose(self.PS[:, b, i * 128:(i + 1) * 128],
                                                      stg[s][:, i, kc * 128:(kc + 1) * 128], self.ident)
                        return ins
                    self.pe.op([bst[s], self.bconst], [self.bPS[b]], tr)
                    self.copy_on(self.evac_alt(n), self.XT[:, kc, tg * 512:(tg + 1) * 512], self.PS[:, b, :],
                                 [self.bPS[b]], [self.bXT[kc][tg]])
            self.barrier()

    def norm_scratch(self, es):
        SQ = [es.enter_context(self.sbt("sq%d" % i, [128, KC, 512], BF16)) for i in range(2)]
        RST = [es.enter_context(self.sbt("rstd%d" % i, [128, 512], F32)) for i in range(2)]
        return (SQ, [Buf("sq0"), Buf("sq1")], RST, [Buf("rs0"), Buf("rs1")])

    def norm_stats_tile(self, scr, tt):
        nc = self.nc
        SQ, bSQ, RST, bRS = scr
        s = tt % 2
        c0 = tt * 512
        bank = 6 + s
        self.act.op([self.bXT[kc][tt] for kc in range(KC)], [bSQ[s]],
                    lambda: nc.scalar.activation(SQ[s][:], self.XT[:, :, c0:c0 + 512], AF.Square))

        def mm():
            ins = None
            for kc in range(KC):
                ins = nc.tensor.matmul(self.PS[:, bank, :], self.onesD[:], SQ[s][:, kc, :],
                                       start=(kc == 0), stop=(kc == KC - 1))
            return ins
        self.pe.op([bSQ[s], self.bconst], [self.bPS[bank]], mm)
        self.rstd_from_psum(RST[s][:], self.PS[:, bank, :], [self.bPS[bank]], bRS[s])
        return RST[s], bRS[s]

    def norm_tile_to(self, scr, XN, bXN, gname, t, tt):
        nc = self.nc
        rst, brs = self.norm_stats_tile(scr, tt)
        c0 = tt * 512
        for kc in range(KC):
            self.dve.op([brs, self.bXT[kc][tt]], [bXN[kc][t]],
                        lambda: nc.vector.scalar_tensor_tensor(
                            XN[:, kc, t * 512:(t + 1) * 512], self.XT[:, kc, c0:c0 + 512], self.vcol(gname, kc),
                            rst[:], ALU.mult, ALU.mult))

    def final_norm_tile(self, tt):
        nc = self.nc
        rst, brs = self.norm_stats_tile(self.scr, tt)
        c0 = tt * 512
        for kc in range(KC):
            self.dve.op([brs, self.bXT[kc][tt]], [self.bXT[kc][tt]],
                        lambda: nc.vector.scalar_tensor_tensor(
                            self.XT[:, kc, c0:c0 + 512], self.XT[:, kc, c0:c0 + 512],
                            self.vcol("final_norm", kc), rst[:], ALU.mult, ALU.mult))

    def store_tile(self, tt):
        nc = self.nc
        ov = self.out.rearrange("(i p) d -> i p d", p=128)
        for i in range(tt * 4, tt * 4 + 4):
            s = i % 2
            pb = (i % 3) * 2

            def tr():
                ins = None
                for kc in range(KC):
                    ins = nc.tensor.transpose(self.PS[:, pb + kc // 4, (kc % 4) * 128:(kc % 4 + 1) * 128],
                                              self.XT[:, kc, i * 128:(i + 1) * 128], self.ident)
                return ins
            self.pe.op([self.bXT[kc][tt] for kc in range(KC)] + [self.bconst],
                       [self.bPS[pb], self.bPS[pb + 1]], tr)
            self.copy_on(self.evac_alt(i), self.ostg[s][:], self.PS[:, pb:pb + 2, :],
                         [self.bPS[pb], self.bPS[pb + 1]], [self.bost[s]])
            self.out_toks.append(self.sp.dma(ov[i], self.ostg[s][:], [self.bost[s]], []))

    def rstd_from_psum(self, out, ps, bps, bout):
        nc = self.nc
        self.act.op(bps + [self.bconst], [bout],
                    lambda: nc.scalar.activation(out, ps, AF.Ln, bias=self.cst[:, 0:1]))
        self.act.op([bout], [bout], lambda: nc.scalar.activation(out, out, AF.Exp, scale=-0.5))

    def rmsnorm_to(self, es_tmp, XN, bXN, gname, t0=0, nt=4):
        scr = self.norm_scratch(es_tmp)
        for t in range(nt):
            self.norm_tile_to(scr, XN, bXN, gname, t, t0 + t)

    def ffn(self, l, next_tile):
        nc = self.nc
        groups = [(0, 8), (8, 7), (15, 7)]
        GMAX = 8
        with ExitStack() as es:
            XN = self.XN
            bXN = self.bXNs
            H = es.enter_context(self.sbt("ffn_h", [128, GMAX, T], BF16))
            bH = [[Buf("h%d_%d" % (j, t)) for t in range(2)] for j in range(GMAX)]
            Gb = [es.enter_context(self.sbt("ffn_g%d" % i, [128, 2 + 1024], F32)) for i in range(2)]
            A = [es.enter_context(self.sbt("ffn_a%d" % i, [128, 1024], F32)) for i in range(2)]
            bG = [Buf("g0"), Buf("g1")]
            bA = [Buf("a0"), Buf("a1")]
            self.dve.op([], [bG[0]], lambda: nc.vector.memset(Gb[0][:, 0:2], 0.0))
            wup = self.f_w_up[l].rearrange("(kc p) c -> p kc c", p=128)
            wdn = self.f_w_down[l].rearrange("(j p) c -> j p c", p=128)
            cw = lambda tap, j: self.vcol("f_conv%d%d" % (l, tap), j)
            unit = 0
            nbank = 0
            for (g0, gn) in groups:
                jj = 0
                while jj < gn:
                    nb = min(4, gn - jj)
                    j0 = g0 + jj
                    self.ring.align()
                    wg = self.ring.load(wup[:, :, j0 * 128:(j0 + nb) * 128], 128, [KC, nb * 128])
                    wu = self.ring.load(wup[:, :, DFF + j0 * 128:DFF + (j0 + nb) * 128], 128, [KC, nb * 128])
                    for jb in range(nb):
                        j = j0 + jb
                        for hh in range(2):
                            pb = (unit % 2) * 4
                            s = hh
                            unit += 1

                            def mm(w, bank0):
                                ins = None
                                for kc in range(KC):
                                    for t in range(2):
                                        ins = nc.tensor.matmul(
                                            self.PS[:, bank0 + t, :], w.ap[:, kc, jb * 128:(jb + 1) * 128],
                                            XN[:, kc, (hh * 2 + t) * 512:(hh * 2 + t + 1) * 512],
                                            start=(kc == 0), stop=(kc == KC - 1))
                                return ins
                            rxn = [bXN[kc][hh * 2 + t] for kc in range(KC) for t in range(2)]
                            self.pe.op(rxn + wg.buf, [self.bPS[pb], self.bPS[pb + 1]], lambda: mm(wg, pb))
                            self.pe.op(rxn + wu.buf, [self.bPS[pb + 2], self.bPS[pb + 3]], lambda: mm(wu, pb + 2))
                            psg = self.PS[:, pb:pb + 2, :]
                            psu = self.PS[:, pb + 2:pb + 4, :]
                            bg = [self.bPS[pb], self.bPS[pb + 1]]
                            bu = [self.bPS[pb + 2], self.bPS[pb + 3]]
                            self.act.op(bg, [bA[s]], lambda: nc.scalar.activation(
                                A[s][:], psg, AF.Copy, scale=cw(2, j)))
                            self.act.op(bg, [bG[s]], lambda: nc.scalar.copy(Gb[s][:, 2:2 + 1024], psg))
                            if hh == 1:
                                self.act.op([bG[0]], [bG[1]], lambda: nc.scalar.copy(
                                    Gb[1][:, 0:2], Gb[0][:, 1024:1026]))
                            self.dve.op([bG[s], bA[s]], [bA[s]], lambda: nc.vector.scalar_tensor_tensor(
                                A[s][:], Gb[s][:, 1:1025], cw(1, j), A[s][:], ALU.mult, ALU.add))
                            self.dve.op([bG[s], bA[s]], [bA[s]], lambda: nc.vector.scalar_tensor_tensor(
                                A[s][:], Gb[s][:, 0:1024], cw(0, j), A[s][:], ALU.mult, ALU.add))
                            self.act.op([bA[s]], [bA[s]], lambda: nc.scalar.activation(A[s][:], A[s][:], AF.Silu))
                            self.dve.op([bA[s]] + bu, [bH[jj + jb][hh]], lambda: nc.vector.tensor_tensor(
                                H[:, jj + jb, hh * 1024:(hh + 1) * 1024], A[s][:], psu, ALU.mult))
                    jj += nb
                self.ring.align()
                wd = [self.ring.load(wdn[g0 + k], 128, [D]) for k in range(gn)]
                for tt in range(4):
                    for m in range(KC):
                        b = nbank % 8
                        nbank += 1

                        def mmd():
                            ins = None
                            for k in range(gn):
                                ins = nc.tensor.matmul(self.PS[:, b, :], wd[k].ap[:, m * 128:(m + 1) * 128],
                                                       H[:, k, tt * 512:(tt + 1) * 512],
                                                       start=(k == 0), stop=(k == gn - 1))
                            return ins
                        rd = [bH[k][tt // 2] for k in range(gn)]
                        for k in range(gn):
                            rd += wd[k].buf
                        self.pe.op(rd, [self.bPS[b]], mmd)
                        self.dve.op([self.bPS[b], self.bXT[m][tt]], [self.bXT[m][tt]],
                                    lambda: nc.vector.tensor_tensor(
                                        self.XT[:, m, tt * 512:(tt + 1) * 512],
                                        self.XT[:, m, tt * 512:(tt + 1) * 512], self.PS[:, b, :], ALU.add))
                    if g0 == groups[-1][0] and tt >= 1:
                        next_tile(tt - 1)
            next_tile(3)

    def sconv(self, next_tile):
        nc = self.nc
        with ExitStack() as es:
            XN = self.XN
            bXN = self.bXNs
            Y = es.enter_context(self.sbt("sc_y", [128, KC, T], BF16))
            bY = [[Buf("y%d_%d" % (j, t)) for t in range(4)] for j in range(KC)]
            Cs = [es.enter_context(self.sbt("sc_c%d" % i, [128, 512], F32)) for i in range(2)]
            Zb = [es.enter_context(self.sbt("sc_z%d" % i, [128, 2 + 512], F32)) for i in range(2)]
            A = [es.enter_context(self.sbt("sc_a%d" % i, [128, 512], F32)) for i in range(2)]
            Bs = [es.enter_context(self.sbt("sc_b%d" % i, [128, 512], F32)) for i in range(2)]
            bB = [Buf("b0"), Buf("b1")]
            bC = [Buf("c0"), Buf("c1")]
            bZ = [Buf("z0"), Buf("z1")]
            bA = [Buf("a0"), Buf("a1")]
            win = self.b_w_in.rearrange("(kc p) c -> p kc c", p=128)
            wout = self.b_w_out.rearrange("(j p) c -> j p c", p=128)
            cw = lambda tap, j: self.vcol("b_conv%d" % tap, j)
            NB = 2
            for j0 in range(0, KC, NB):
                self.ring.align()
                ws = [self.ring.load(win[:, :, sg * D + j0 * 128: sg * D + (j0 + NB) * 128], 128, [KC, NB * 128])
                      for sg in range(3)]
                for jb in range(NB):
                    j = j0 + jb
                    for tt in range(4):
                        s = tt % 2
                        pb = s * 3

                        def mm(w, bank):
                            ins = None
                            for kc in range(KC):
                                ins = nc.tensor.matmul(self.PS[:, bank, :], w.ap[:, kc, jb * 128:(jb + 1) * 128],
                                                       XN[:, kc, tt * 512:(tt + 1) * 512],
                                                       start=(kc == 0), stop=(kc == KC - 1))
                            return ins
                        for sg in range(3):
                            self.pe.op([bXN[kc][tt] for kc in range(KC)] + ws[sg].buf, [self.bPS[pb + sg]],
                                       lambda: mm(ws[sg], pb + sg))
                        self.act.op([self.bPS[pb + 1]], [bC[s]], lambda: nc.scalar.copy(Cs[s][:], self.PS[:, pb + 1, :]))
                        self.act.op([self.bPS[pb]], [bB[s]], lambda: nc.scalar.copy(Bs[s][:], self.PS[:, pb, :]))
                        if tt == 0:
                            self.dve.op([], [bZ[s]], lambda: nc.vector.memset(Zb[s][:, 0:2], 0.0))
                        else:
                            self.act.op([bZ[1 - s]], [bZ[s]], lambda: nc.scalar.copy(
                                Zb[s][:, 0:2], Zb[1 - s][:, 512:514]))
                        self.dve.op([bC[s], self.bPS[pb + 2]], [bZ[s]], lambda: nc.vector.tensor_tensor(
                            Zb[s][:, 2:514], Cs[s][:], self.PS[:, pb + 2, :], ALU.mult))
                        self.act.op([bZ[s]], [bA[s]], lambda: nc.scalar.activation(
                            A[s][:], Zb[s][:, 2:514], AF.Copy, scale=cw(2, j)))
                        self.dve.op([bZ[s], bA[s]], [bA[s]], lambda: nc.vector.scalar_tensor_tensor(
                            A[s][:], Zb[s][:, 1:513], cw(1, j), A[s][:], ALU.mult, ALU.add))
                        self.dve.op([bZ[s], bA[s]], [bA[s]], lambda: nc.vector.scalar_tensor_tensor(
                            A[s][:], Zb[s][:, 0:512], cw(0, j), A[s][:], ALU.mult, ALU.add))
                        self.dve.op([bA[s], bB[s]], [bY[j][tt]], lambda: nc.vector.tensor_tensor(
                            Y[:, j, tt * 512:(tt + 1) * 512], A[s][:], Bs[s][:], ALU.mult))
            self.out_proj(wout, Y, lambda kc, tg: [bY[kc][tg]], range(4), lambda tg: tg, next_tile)
            next_tile(3)

    def out_proj(self, wout, Y, ybufs, tiles, gtile, next_tile=None):
        nc = self.nc
        self.ring.align()
        wo = [self.ring.load(wout[kc], 128, [D]) for kc in range(KC)]
        nb = 0
        for t in tiles:
            tg = gtile(t)
            for m in range(KC):
                b = 6 + nb % 2
                nb += 1

                def mmo():
                    ins = None
                    for kc in range(KC):
                        ins = nc.tensor.matmul(self.PS[:, b, :], wo[kc].ap[:, m * 128:(m + 1) * 128],
                                               Y[:, kc, t * 512:(t + 1) * 512],
                                               start=(kc == 0), stop=(kc == KC - 1))
                    return ins
                rd = []
                for kc in range(KC):
                    rd += ybufs(kc, t) + wo[kc].buf
                self.pe.op(rd, [self.bPS[b]], mmo)
                self.dve.op([self.bPS[b], self.bXT[m][tg]], [self.bXT[m][tg]],
                            lambda: nc.vector.tensor_tensor(
                                self.XT[:, m, tg * 512:(tg + 1) * 512],
                                self.XT[:, m, tg * 512:(tg + 1) * 512], self.PS[:, b, :], ALU.add))
            if next_tile is not None and t >= 1:
                next_tile(t - 1)

    def gla(self):
        nc = self.nc
        PS = self.PS
        bPS = self.bPS
        win = self.a_w_in.rearrange("(kc p) c -> p kc c", p=128)
        wout = self.a_w_out.rearrange("(j p) c -> j p c", p=128)
        nbank = [0]

        def nextbank():
            b = nbank[0] % 8
            nbank[0] += 1
            return b

        with ExitStack() as es:
            S = [es.enter_context(self.sbt("gl_s%d" % i, [128, HEADS, DV], F32)) for i in range(2)]
            bS = [[Buf("s") for _ in range(HEADS)] for _ in range(2)]
            WGU = es.enter_context(self.sbt("gl_wgu", [32, 512], BF16))
            bWGU = Buf("wgu")
            self.dve.op([], [bWGU], lambda: nc.vector.memset(WGU[:], 0.0))
            self.pool.dma(WGU[0:17, :], self.a_wgu, [], [bWGU])
            self.dve.op([], bS[1], lambda: nc.vector.memset(S[1][:], 0.0))
            for hf in range(2):
                with ExitStack() as esh:
                    QT = esh.enter_context(self.sbt("gl_qt", [128, HEADS, 1024], BF16))
                    bQT = [[Buf("qt") for _ in range(2)] for _ in range(HEADS)]
                    XN = esh.enter_context(self.sbt("gl_xn", [128, KC, 1024], BF16))
                    bXN = [[Buf("xn") for _ in range(2)] for _ in range(KC)]
                    KD = esh.enter_context(self.sbt("gl_kd", [128, 8, 512], BF16))
                    bKD = [Buf("kd") for _ in range(8)]
                    V = esh.enter_context(self.sbt("gl_v", [128, 8, 1024], BF16))
                    bV = [Buf("v") for _ in range(8)]
                    SR = esh.enter_context(self.sbt("gl_sr", [128, KC, 1024], BF16))
                    bSR = [[Buf("sr") for _ in range(2)] for _ in range(KC)]
                    DEC = esh.enter_context(self.sbt("gl_dec", [128, HEADS, 16], F32))
                    bDEC = [Buf("dec") for _ in range(8)]
                    with ExitStack() as esa:
                        GL = esa.enter_context(self.sbt("gl_gl", [32, 1024], BF16))
                        bGL = Buf("gl")
                        SPt = [esa.enter_context(self.sbt("gl_sp%d" % i, [128, 512], F32)) for i in range(2)]
                        E = [esa.enter_context(self.sbt("gl_e%d" % i, [128, 512], F32)) for i in range(2)]
                        bSP = [Buf("sp0"), Buf("sp1")]
                        bE = [Buf("e0"), Buf("e1")]
                        with ExitStack() as es2:
                            self.rmsnorm_to(es2, XN, bXN, "a_norm", t0=hf * 2, nt=2)
                        self.dve.op([], [bGL], lambda: nc.vector.memset(GL[:], 0.0))
                        self.dve.op([], [bGL], lambda: nc.vector.memset(GL[0:17, :], 1.0))
                        self.ring.align()
                        wgl = self.ring.load(win[:, :, 3072:3088], 128, [KC, 16])
                        for tt in range(2):
                            b = nextbank()

                            def mmg():
                                ins = None
                                for kc in range(KC):
                                    ins = nc.tensor.matmul(PS[0:16, b, :], wgl.ap[:, kc, :],
                                                           XN[:, kc, tt * 512:(tt + 1) * 512],
                                                           start=(kc == 0), stop=(kc == KC - 1))
                                return ins
                            self.pe.op([bXN[kc][tt] for kc in range(KC)] + wgl.buf, [bPS[b]], mmg)
                            self.act.op([bPS[b]], [bGL], lambda: nc.scalar.copy(
                                GL[0:16, tt * 512:(tt + 1) * 512], PS[0:16, b, :]))
                        wq = self.ring.load(win[:, :, 0:512], 128, [KC, 512])
                        for j in range(HEADS):
                            for tt in range(2):
                                b = nextbank()

                                def mmq():
                                    ins = None
                                    for kc in range(KC):
                                        ins = nc.tensor.matmul(PS[:, b, :], wq.ap[:, kc, j * 128:(j + 1) * 128],
                                                               XN[:, kc, tt * 512:(tt + 1) * 512],
                                                               start=(kc == 0), stop=(kc == KC - 1))
                                    return ins
                                self.pe.op([bXN[kc][tt] for kc in range(KC)] + wq.buf, [bPS[b]], mmq)
                                self.act.op([bPS[b]], [bQT[j][tt]], lambda: nc.scalar.activation(
                                    QT[:, j, tt * 512:(tt + 1) * 512], PS[:, b, :], AF.Copy, scale=float(DK) ** -0.5))
                        self.ring.align()
                        wk = self.ring.load(win[:, :, 512:1024], 128, [KC, 512])

                        def gate_pre(i):
                            s = i % 2
                            b1 = nextbank()
                            self.pe.op([bGL, bWGU], [bPS[b1]], lambda: nc.tensor.matmul(
                                PS[:, b1, :], GL[:, i * 128:(i + 1) * 128], WGU[:, :], start=True, stop=True))
                            self.act.op([bPS[b1]], [bSP[s]], lambda: nc.scalar.activation(
                                SPt[s][:], PS[:, b1, :], AF.Exp, scale=-1.0))
                            self.act.op([bSP[s], self.bconst], [bSP[s]], lambda: nc.scalar.activation(
                                SPt[s][:], SPt[s][:], AF.Ln, bias=self.cst[:, 1:2]))

                        gate_pre(0)
                        for i in range(8):
                            s = i % 2
                            if i + 1 < 8:
                                gate_pre(i + 1)
                            b4 = nextbank()

                            def mmk():
                                ins = None
                                for kc in range(KC):
                                    ins = nc.tensor.matmul(PS[:, b4, :], XN[:, kc, i * 128:(i + 1) * 128], wk.ap[:, kc, :],
                                                           start=(kc == 0), stop=(kc == KC - 1))
                                return ins
                            self.pe.op([bXN[kc][i // 4] for kc in range(KC)] + wk.buf, [bPS[b4]], mmk)
                            b2 = nextbank()
                            self.pe.op([bSP[s], self.bconst], [bPS[b2]], lambda: nc.tensor.matmul(
                                PS[:, b2, :], self.M1, SPt[s][:], start=True, stop=True))
                            b3 = nextbank()

                            def mmt():
                                ins = None
                                for h in range(HEADS):
                                    ins = nc.tensor.matmul(PS[:, b3, h * 2:h * 2 + 2], SPt[s][:, h * 128:(h + 1) * 128],
                                                           self.M2, start=True, stop=True)
                                return ins
                            self.pe.op([bSP[s], self.bconst], [bPS[b3]], mmt)
                            self.act.op([bPS[b2]], [bE[s]], lambda: nc.scalar.activation(E[s][:], PS[:, b2, :], AF.Exp))
                            self.act.op([bPS[b3]], [bDEC[i]], lambda: nc.scalar.activation(
                                DEC[:, :, 2 * i:2 * i + 2],
                                PS[:, b3, 0:8].rearrange("p (h c) -> p h c", c=2), AF.Exp))
                            self.dve.op([bPS[b4], bE[s]], [bKD[i]], lambda: nc.vector.tensor_tensor(
                                KD[:, i, :], PS[:, b4, :], E[s][:], ALU.mult))
                        self.ring.align()
                        for vb in range(2):
                            wv = self.ring.load(win[:, :, 1024 + vb * 512:1024 + (vb + 1) * 512], 128, [KC, 512])
                            for i in range(8):
                                b = nextbank()

                                def mmv():
                                    ins = None
                                    for kc in range(KC):
                                        ins = nc.tensor.matmul(PS[:, b, :], XN[:, kc, i * 128:(i + 1) * 128],
                                                               wv.ap[:, kc, :], start=(kc == 0), stop=(kc == KC - 1))
                                    return ins
                                self.pe.op([bXN[kc][i // 4] for kc in range(KC)] + wv.buf, [bPS[b]], mmv)
                                self.copy_on(self.evac_alt(i), V[:, i, vb * 512:(vb + 1) * 512], PS[:, b, :],
                                             [bPS[b]], [bV[i]])
                        self.ring.align()
                        for rb in range(2):
                            wr = self.ring.load(win[:, :, 2048 + rb * 512:2048 + (rb + 1) * 512], 128, [KC, 512])
                            for jb in range(4):
                                j = rb * 4 + jb
                                for tt in range(2):
                                    b = nextbank()

                                    def mmr():
                                        ins = None
                                        for kc in range(KC):
                                            ins = nc.tensor.matmul(PS[:, b, :], wr.ap[:, kc, jb * 128:(jb + 1) * 128],
                                                                   XN[:, kc, tt * 512:(tt + 1) * 512],
                                                                   start=(kc == 0), stop=(kc == KC - 1))
                                        return ins
                                    self.pe.op([bXN[kc][tt] for kc in range(KC)] + wr.buf, [bPS[b]], mmr)
                                    self.act.op([bPS[b]], [bSR[j][tt]], lambda: nc.scalar.activation(
                                        SR[:, j, tt * 512:(tt + 1) * 512], PS[:, b, :], AF.Silu))
                    self.dump("QT", QT[:], [128, HEADS, 1024], BF16, [])
                    self.dump("KD", KD[:], [128, 8, 512], BF16, [])
                    self.dump("V", V[:], [128, 8, 1024], BF16, [])
                    self.dump("DEC", DEC[:], [128, HEADS, 16], F32, [])
                    self.barrier()
                    with ExitStack() as esb:
                        TQ = 4
                        TW = TQ * CH
                        SB = [esb.enter_context(self.sbt("gl_sb%d" % i, [128, HEADS, DV], BF16)) for i in range(2)]
                        bSB = [Buf("sb0"), Buf("sb1")]
                        OT2 = [esb.enter_context(self.sbt("gl_ot%d" % i, [128, KC, TW], F32)) for i in range(2)]
                        SQh = [esb.enter_context(self.sbt("gl_sqh%d" % i, [128, 2, TW], BF16)) for i in range(2)]
                        bSQh = [Buf("sqh0"), Buf("sqh1")]
                        RS = esb.enter_context(self.sbt("gl_rs", [128, 2, TW], F32))
                        OF2 = [esb.enter_context(self.sbt("gl_of%d" % i, [128, KC, TW], BF16)) for i in range(2)]
                        TMP = [esb.enter_context(self.sbt("gl_tmp%d" % i, [128, TW], F32)) for i in range(2)]
                        bTMP = [Buf("tmp0"), Buf("tmp1")]
                        bOF2 = [[Buf("of") for _ in range(KC)] for _ in range(2)]
                        bOT2 = [[Buf("ot") for _ in range(TQ)] for _ in range(2)]
                        bRS = [Buf("rs") for _ in range(2)]

                        def emit_upd(c):
                            i = c // 2
                            part = (c % 2) * 64
                            su = (c % 2) * 2

                            def mmu():
                                ins = None
                                for h in range(HEADS):
                                    ins = nc.tensor.matmul(
                                        PS[:, su + h // 2, (h % 2) * 256:(h % 2 + 1) * 256],
                                        KD[part:part + 64, i, h * 128:(h + 1) * 128],
                                        V[part:part + 64, i, h * 256:(h + 1) * 256], start=True, stop=True)
                                return ins
                            self.pe.op([bKD[i], bV[i]], [bPS[su], bPS[su + 1]], mmu)

                        def emit_state(c):
                            i = c // 2
                            su = (c % 2) * 2
                            sn = (hf * 16 + c) % 2
                            so = 1 - sn
                            for h in range(HEADS):
                                self.dve.op([bS[so][h], bDEC[i], bPS[su + h // 2]], [bS[sn][h]],
                                            lambda: nc.vector.scalar_tensor_tensor(
                                                S[sn][:, h, :], S[so][:, h, :], DEC[:, h, c:c + 1],
                                                PS[:, su + h // 2, (h % 2) * 256:(h % 2 + 1) * 256],
                                                ALU.mult, ALU.add))
                            self.act.op(bS[sn], [bSB[c % 2]], lambda: nc.scalar.copy(SB[c % 2][:], S[sn][:]))

                        def emit_o(c):
                            tq = c // TQ
                            cl = c % TQ
                            bo = 4 + c % 2

                            def mmo():
                                ins = None
                                for j in range(KC):
                                    h, hv = j // 2, j % 2
                                    ins = nc.tensor.matmul(PS[:, bo, j * 64:(j + 1) * 64],
                                                           SB[c % 2][:, h, hv * 128:(hv + 1) * 128],
                                                           QT[:, h, c * 64:(c + 1) * 64], start=True, stop=True)
                                return ins
                            self.pe.op([bSB[c % 2]] + [bQT[h][c // 8] for h in range(HEADS)], [bPS[bo]], mmo)
                            pso = PS[:, bo, :].rearrange("p (j l) -> p j l", l=64)
                            self.act.op([bPS[bo]], [bOT2[tq % 2][cl]], lambda: nc.scalar.copy(
                                OT2[tq % 2][:, :, cl * 64:(cl + 1) * 64], pso))

                        def act_part(tq, h):
                            ot = OT2[tq % 2]
                            s2 = h % 2
                            bn = 6 + s2
                            self.act.op(bOT2[tq % 2], [bSQh[s2]], lambda: nc.scalar.activation(
                                SQh[s2][:], ot[:, 2 * h:2 * h + 2, :], AF.Square))

                            def mmn():
                                nc.tensor.matmul(PS[:, bn, 0:TW], self.onesV[:], SQh[s2][:, 0, :], start=True, stop=False)
                                return nc.tensor.matmul(PS[:, bn, 0:TW], self.onesV[:], SQh[s2][:, 1, :],
                                                        start=False, stop=True)
                            self.pe.op([bSQh[s2], self.bconst], [bPS[bn]], mmn)
                            self.rstd_from_psum(RS[:, s2, :], PS[:, bn, 0:TW], [bPS[bn]], bRS[s2])

                        def dve_part(tq, h):
                            ot, of = OT2[tq % 2], OF2[tq % 2]
                            for j in (2 * h, 2 * h + 1):
                                self.dve.op(bOT2[tq % 2] + [bRS[h % 2]], [bTMP[j % 2]],
                                            lambda: nc.vector.scalar_tensor_tensor(
                                                TMP[j % 2][:], ot[:, j, :], self.vcol("a_gn", j), RS[:, h % 2, :],
                                                ALU.mult, ALU.mult))
                                self.dve.op([bTMP[j % 2], bSR[j][tq // 2]], [bOF2[tq % 2][j]],
                                            lambda: nc.vector.tensor_tensor(
                                                of[:, j, :], TMP[j % 2][:], SR[:, j, tq * TW:(tq + 1) * TW], ALU.mult))

                        wos = {}

                        def load_wo(tq):
                            self.ring.align()
                            wos[tq] = [self.ring.load(wout[kc], 128, [D]) for kc in range(KC)]

                        def outproj_group(tq, m):
                            wo = wos[tq]
                            of, bof = OF2[tq % 2], bOF2[tq % 2]
                            tg = hf * 2 + tq // 2
                            c0 = tg * 512 + (tq % 2) * TW
                            b = 6 + m % 2

                            def mmo2():
                                ins = None
                                for kc in range(KC):
                                    ins = nc.tensor.matmul(PS[:, b, 0:TW], wo[kc].ap[:, m * 128:(m + 1) * 128],
                                                           of[:, kc, :], start=(kc == 0), stop=(kc == KC - 1))
                                return ins
                            rd = list(bof)
                            for kc in range(KC):
                                rd += wo[kc].buf
                            self.pe.op(rd, [bPS[b]], mmo2)
                            self.dve.op([bPS[b], self.bXT[m][tg]], [self.bXT[m][tg]],
                                        lambda: nc.vector.tensor_tensor(
                                            self.XT[:, m, c0:c0 + TW],
                                            self.XT[:, m, c0:c0 + TW], PS[:, b, 0:TW], ALU.add))

                        NST = 16
                        sched = {}

                        def at(c, fn):
                            sched.setdefault(c, []).append(fn)
                        for tq in range(4):
                            b0 = 4 * tq + 4
                            at(b0, lambda tq=tq: act_part(tq, 0))
                            at(b0, lambda tq=tq: act_part(tq, 1))
                            at(b0 + 1, lambda tq=tq: dve_part(tq, 0))
                            at(b0 + 1, lambda tq=tq: act_part(tq, 2))
                            at(b0 + 2, lambda tq=tq: dve_part(tq, 1))
                            at(b0 + 2, lambda tq=tq: act_part(tq, 3))
                            at(b0 + 3, lambda tq=tq: dve_part(tq, 2))
                            at(b0 + 3, lambda tq=tq: dve_part(tq, 3))
                            at(b0 + 3, lambda tq=tq: load_wo(tq))
                            for k in range(4):
                                at(b0 + 4 + k, lambda tq=tq, k=k: outproj_group(tq, 2 * k))
                                at(b0 + 4 + k, lambda tq=tq, k=k: outproj_group(tq, 2 * k + 1))
                        emit_upd(0)
                        emit_state(0)
                        for c in range(NST):
                            if c + 1 < NST:
                                emit_upd(c + 1)
                                emit_state(c + 1)
                            emit_o(c)
                            for fn in sched.get(c, []):
                                fn()
                        for c in sorted(k for k in sched if k >= NST):
                            for fn in sched[c]:
                                fn()
                    self.barrier()


def _chunkcols(v):
    v = np.asarray(v, dtype=np.float32)
    return np.ascontiguousarray(v.reshape(-1, 128).T)


def make_consts():
    c = np.zeros((128, NCONST), dtype=np.float32)
    c[:, 0:128] = np.eye(128, dtype=np.float32)
    lp = np.arange(128)[:, None]
    l = np.arange(128)[None, :]
    c[:, 128:256] = np.where((lp > l) & (lp // CH == l // CH), -1.0 / 16.0, 0.0)
    c[:, 256:258] = np.where(lp // CH == np.arange(2)[None, :], -1.0 / 16.0, 0.0)
    return c


def make_in_maps(inp, ncores=NCORES):
    vec = np.zeros((128, NV), dtype=np.float32)

    def put(name, v):
        a = _chunkcols(v)
        vec[:, VC[name]:VC[name] + a.shape[1]] = a
    put("a_norm", inp["a_norm"][0])
    put("b_norm", inp["b_norm"][0])
    put("f_norm0", inp["f_norm"][0])
    put("f_norm1", inp["f_norm"][1])
    put("final_norm", inp["final_norm"])
    put("a_gn", inp["a_gn"][0])
    for t in range(3):
        put("b_conv%d" % t, inp["b_conv"][0][t])
        put("f_conv0%d" % t, inp["f_conv"][0][t])
        put("f_conv1%d" % t, inp["f_conv"][1][t])
    wgu = np.ascontiguousarray(np.concatenate(
        [np.asarray(inp["a_w_gate_up"][0], dtype=np.float32),
         np.asarray(inp["a_b_gate"][0], dtype=np.float32)[None, :]], axis=0))
    shared = {
        "vecs": vec,
        "consts": make_consts(),
        "a_w_in": np.ascontiguousarray(inp["a_w_in"][0], dtype=np.float32),
        "a_wgu": wgu,
        "a_w_out": np.ascontiguousarray(inp["a_w_out"][0], dtype=np.float32),
        "b_w_in": np.ascontiguousarray(inp["b_w_in"][0], dtype=np.float32),
        "b_w_out": np.ascontiguousarray(inp["b_w_out"][0], dtype=np.float32),
        "f_w_up": np.ascontiguousarray(inp["f_w_up"], dtype=np.float32),
        "f_w_down": np.ascontiguousarray(inp["f_w_down"], dtype=np.float32),
    }
    x = np.asarray(inp["x"], dtype=np.float32)
    maps = []
    for c in range(ncores):
        m = dict(shared)
        m["x"] = np.ascontiguousarray(x[c])
        maps.append(m)
    return maps


ALL_STAGES = ("gla", "ffn0", "sconv", "ffn1", "fnorm")
_PROG_CACHE = {}


def get_prog(stages=ALL_STAGES):
    key = tuple(stages)
    if key not in _PROG_CACHE:
        _PROG_CACHE[key] = Prog(key)
    return _PROG_CACHE[key]


def kernel(**inputs):
    prog = get_prog(ALL_STAGES)
    in_maps = make_in_maps(inputs)
    res = run_bass_kernel_spmd(prog.nc, in_maps, core_ids=list(range(NCORES)))
    return np.stack([np.asarray(r["out"], dtype=np.float32) for r in res.results], axis=0)
```

```python
from contextlib import ExitStack

import numpy as np
import concourse.bass as bass
import concourse.mybir as mybir
from concourse.bass_utils import run_bass_kernel_spmd

F32 = mybir.dt.float32
BF16 = mybir.dt.bfloat16
AF = mybir.ActivationFunctionType
ALU = mybir.AluOpType

D = 1024
T = 2048
NCORES = 8
KC = 8
DFF = 2816
NJ = 22
EPS = 1e-6
HEADS = 4
DK = 128
DV = 256
CH = 64
PROJ_A = 3088
NSLOT = 16

VC = {}
_c = 0
for _name, _n in [("a_norm", 8), ("b_norm", 8), ("f_norm0", 8), ("f_norm1", 8), ("final_norm", 8),
                  ("a_gn", 8), ("b_conv0", 8), ("b_conv1", 8), ("b_conv2", 8),
                  ("f_conv00", 22), ("f_conv01", 22), ("f_conv02", 22),
                  ("f_conv10", 22), ("f_conv11", 22), ("f_conv12", 22)]:
    VC[_name] = _c
    _c += _n
NV = _c
NCONST = 128 + 128 + 2


class Tok:
    __slots__ = ("sem", "val")

    def __init__(self, sem, val):
        self.sem = sem
        self.val = val


class Buf:
    __slots__ = ("name", "w", "r")

    def __init__(self, name=""):
        self.name = name
        self.w = None
        self.r = {}


class Q:
    def __init__(self, nc, eng, name, is_pe=False):
        self.eng = eng
        self.sem = nc.alloc_semaphore("q_" + name)
        self.n = 0
        self.seen = {}
        self.is_pe = is_pe
        self.name = name

    def wait(self, tok):
        if tok is None:
            return
        if self.seen.get(tok.sem, 0) >= tok.val:
            return
        self.eng.wait_ge(tok.sem, tok.val)
        self.seen[tok.sem] = tok.val

    def deps(self, reads, writes):
        for b in reads:
            self.wait(b.w)
        for b in writes:
            if b.w is not None and not (self.is_pe and b.w.sem is self.sem):
                self.wait(b.w)
            for s, t in b.r.items():
                if s is self.sem:
                    continue
                self.wait(t)

    def done(self, ins, reads, writes):
        self.n += 1
        ins.then_inc(self.sem, 1)
        tok = Tok(self.sem, self.n)
        for b in reads:
            b.r[self.sem] = tok
        for b in writes:
            b.w = tok
            b.r = {}
        return tok

    def op(self, reads, writes, fn):
        self.deps(reads, writes)
        return self.done(fn(), reads, writes)


class DmaQ:
    def __init__(self, nc, eng, name, nsem):
        self.eng = eng
        self.sems = [nc.alloc_semaphore("d_%s%d" % (name, i)) for i in range(nsem)]
        self.cnt = [0] * nsem
        self.i = 0
        self.seen = {}

    def wait(self, tok):
        if tok is None:
            return
        if self.seen.get(tok.sem, 0) >= tok.val:
            return
        self.eng.wait_ge(tok.sem, tok.val)
        self.seen[tok.sem] = tok.val

    def dma(self, out, in_, reads, writes, extra_waits=()):
        for t in extra_waits:
            self.wait(t)
        for b in reads:
            self.wait(b.w)
        for b in writes:
            self.wait(b.w)
            for t in b.r.values():
                self.wait(t)
        k = self.i % len(self.sems)
        self.i += 1
        self.cnt[k] += 16
        sem = self.sems[k]
        self.eng.dma_start(out=out, in_=in_).then_inc(sem, 16)
        tok = Tok(sem, self.cnt[k])
        for b in reads:
            b.r[sem] = tok
        for b in writes:
            b.w = tok
            b.r = {}
        return tok


class WBlock:
    def __init__(self, ap, buf, slots):
        self.ap = ap
        self.buf = buf
        self.slots = slots


class Ring:
    def __init__(self, nc, dq, nslot):
        self.nc = nc
        self.dq = dq
        self.nslot = nslot
        self.t = nc.alloc_sbuf_tensor("wring", [128, nslot * 1024], BF16)
        self.bufs = [Buf("ring%d" % i) for i in range(nslot)]
        self.head = 0

    def align(self):
        half = self.nslot // 2
        if self.head % half:
            self.head = (self.head // half + 1) * half
        if self.head >= self.nslot:
            self.head = 0

    def load(self, src, nparts, shape_free):
        nel = int(np.prod(shape_free))
        ns = (nel + 1023) // 1024
        assert ns <= self.nslot
        if self.head + ns > self.nslot:
            self.head = 0
        s0 = self.head
        self.head += ns
        slots = self.bufs[s0:s0 + ns]
        flat = self.t[0:nparts, s0 * 1024: s0 * 1024 + nel]
        if len(shape_free) == 2:
            dst = flat.rearrange("p (a b) -> p a b", b=shape_free[1])
        else:
            dst = flat
        self.dq.dma(dst, src, [], slots)
        return WBlock(dst, slots, slots)


class Prog:
    def __init__(self, stages, dbg=None):
        self.stages = stages
        self.dbg = dbg
        self.dumped = set()
        nc = bass.Bass("TRN2", target_bir_lowering=False)
        self.nc = nc
        dt = nc.dram_tensor
        self.x = dt("x", [T, D], F32, kind="ExternalInput").ap()
        self.vecs_d = dt("vecs", [128, NV], F32, kind="ExternalInput").ap()
        self.consts_d = dt("consts", [128, NCONST], F32, kind="ExternalInput").ap()
        self.a_w_in = dt("a_w_in", [D, PROJ_A], F32, kind="ExternalInput").ap()
        self.a_wgu = dt("a_wgu", [17, 512], F32, kind="ExternalInput").ap()
        self.a_w_out = dt("a_w_out", [D, D], F32, kind="ExternalInput").ap()
        self.b_w_in = dt("b_w_in", [D, 3 * D], F32, kind="ExternalInput").ap()
        self.b_w_out = dt("b_w_out", [D, D], F32, kind="ExternalInput").ap()
        self.f_w_up = dt("f_w_up", [2, D, 2 * DFF], F32, kind="ExternalInput").ap()
        self.f_w_down = dt("f_w_down", [2, DFF, D], F32, kind="ExternalInput").ap()
        self.out = dt("out", [T, D], F32, kind="ExternalOutput").ap()

        self.pe = Q(nc, nc.tensor, "pe", is_pe=True)
        self.act = Q(nc, nc.scalar, "act")
        self.dve = Q(nc, nc.vector, "dve")
        self.gp = Q(nc, nc.gpsimd, "gp")
        self.sp = DmaQ(nc, nc.sync, "sp", 4)
        self.pool = DmaQ(nc, nc.gpsimd, "pool", NSLOT)
        self.ring = Ring(nc, self.pool, NSLOT)

        self.XT = nc.alloc_sbuf_tensor("XT", [128, KC, T], F32)
        self.bXT = [[Buf("XT%d_%d" % (k, t)) for t in range(4)] for k in range(KC)]
        self.vecs = nc.alloc_sbuf_tensor("vecs_sb", [128, NV], F32)
        self.consts = nc.alloc_sbuf_tensor("consts_sb", [128, NCONST], F32)
        self.onesD = nc.alloc_sbuf_tensor("onesD", [128, 128], BF16)
        self.onesV = nc.alloc_sbuf_tensor("onesV", [128, 128], BF16)
        self.bconst = Buf("const")
        self.cst = nc.alloc_sbuf_tensor("cst", [128, 2], F32)
        self.PS = nc.alloc_psum_tensor("PS", [128, 8, 512], F32)
        self.bPS = [Buf("ps%d" % i) for i in range(8)]
        self.ident = self.consts[:, 0:128]
        self.M1 = self.consts[:, 128:256]
        self.M2 = self.consts[:, 256:258]

        self.build()

    def sbt(self, name, shape, dtype):
        self._uid = getattr(self, "_uid", 0) + 1
        return self.nc.sbuf_tensor("%s_u%d" % (name, self._uid), shape, dtype)

    def dump(self, name, ap, shape, dtype, bufs):
        if not self.dbg or name in self.dumped:
            return
        self.dumped.add(name)
        d = self.nc.dram_tensor("dbg_" + name, list(shape), dtype, kind="ExternalOutput").ap()
        self.barrier()
        for q in (self.pe, self.act, self.dve):
            if q.n > 0:
                self.sp.wait(Tok(q.sem, q.n))
        t = self.sp.dma(d, ap, bufs, [])
        for q in (self.pe, self.act, self.dve):
            q.wait(t)

    def vcol(self, name, j=0):
        c = VC[name] + j
        return self.vecs[:, c:c + 1]

    def barrier(self):
        qs = [self.pe, self.act, self.dve]
        for q in qs:
            for p in qs + [self.gp]:
                if p is not q and p.n > 0:
                    q.wait(Tok(p.sem, p.n))

    def evac_alt(self, idx):
        return self.act if idx % 2 == 0 else self.dve

    def copy_on(self, q, out, in_, reads, writes):
        if q is self.act:
            return q.op(reads, writes, lambda: self.nc.scalar.copy(out, in_))
        return q.op(reads, writes, lambda: self.nc.vector.tensor_copy(out, in_))

    def build(self):
        nc = self.nc
        st = self.stages
        self.sp.dma(self.vecs[:], self.vecs_d, [], [self.bconst])
        self.sp.dma(self.consts[:], self.consts_d, [], [self.bconst])
        self.dve.op([], [self.bconst], lambda: nc.vector.memset(self.onesD[:], 1.0 / D))
        self.dve.op([], [self.bconst], lambda: nc.vector.memset(self.onesV[:], 1.0 / DV))
        self.dve.op([], [self.bconst], lambda: nc.vector.memset(self.cst[:, 0:1], EPS))
        self.dve.op([], [self.bconst], lambda: nc.vector.memset(self.cst[:, 1:2], 1.0))
        self.load_x()
        if "gla" in st:
            self.gla()
        with ExitStack() as eso:
            self.XN = eso.enter_context(self.sbt("xn_shared", [128, KC, T], BF16))
            self.bXNs = [[Buf("xn") for _ in range(4)] for _ in range(KC)]
            self.scr = self.norm_scratch(eso)
            self.ostg = [eso.enter_context(self.sbt("ostage%d" % i, [128, D], F32)) for i in range(2)]
            self.bost = [Buf("ost%d" % i) for i in range(2)]
            self.out_toks = []
            phases = [p for p in ("ffn0", "sconv", "ffn1") if p in st]
            gname = {"ffn0": "f_norm0", "sconv": "b_norm", "ffn1": "f_norm1"}
            do_fnorm = "fnorm" in st

            def norm_tile_for(ph):
                return lambda tt: self.norm_tile_to(self.scr, self.XN, self.bXNs, gname[ph], tt, tt)

            def final_tile(tt):
                if do_fnorm:
                    self.final_norm_tile(tt)
                if tt >= 1:
                    self.store_tile(tt - 1)

            if phases:
                for tt in range(4):
                    norm_tile_for(phases[0])(tt)
            for k, ph in enumerate(phases):
                nxt = norm_tile_for(phases[k + 1]) if k + 1 < len(phases) else final_tile
                if ph == "sconv":
                    self.sconv(nxt)
                else:
                    self.ffn(int(ph[-1]), nxt)
            if not phases:
                for tt in range(4):
                    final_tile(tt)
            self.store_tile(3)
            for t in self.out_toks[-4:]:
                self.sp.wait(t)
            for t in self.out_toks[-4:]:
                self.pe.wait(t)

    def load_x(self):
        nc = self.nc
        with ExitStack() as es:
            stg = [es.enter_context(self.sbt("xstage%d" % i, [128, 4, D], F32)) for i in range(2)]
            bst = [Buf("xst0"), Buf("xst1")]
            xv = self.x.rearrange("(g i p) d -> g p i d", i=4, p=128)
            n = 0
            for tg in range(4):
                s = tg % 2
                self.sp.dma(stg[s][:], xv[tg], [], [bst[s]])
                for kc in range(KC):
                    b = n % 8
                    n += 1

                    def tr():
                        ins = None
                        for i in range(4):
                            ins = nc.tensor.transpose(self.PS[:, b, i * 128:(i + 1) * 128],
                                                      stg[s][:, i, kc * 128:(kc + 1) * 128], self.ident)
                        return ins
                    self.pe.op([bst[s], self.bconst], [self.bPS[b]], tr)
                    self.copy_on(self.evac_alt(n), self.XT[:, kc, tg * 512:(tg + 1) * 512], self.PS[:, b, :],
                                 [self.bPS[b]], [self.bXT[kc][tg]])
            self.barrier()

    def norm_scratch(self, es):
        SQ = [es.enter_context(self.sbt("sq%d" % i, [128, KC, 512], BF16)) for i in range(2)]
        RST = [es.enter_context(self.sbt("rstd%d" % i, [128, 512], F32)) for i in range(2)]
        return (SQ, [Buf("sq0"), Buf("sq1")], RST, [Buf("rs0"), Buf("rs1")])

    def norm_stats_tile(self, scr, tt):
        nc = self.nc
        SQ, bSQ, RST, bRS = scr
        s = tt % 2
        c0 = tt * 512
        bank = 6 + s
        self.act.op([self.bXT[kc][tt] for kc in range(KC)], [bSQ[s]],
                    lambda: nc.scalar.activation(SQ[s][:], self.XT[:, :, c0:c0 + 512], AF.Square))

        def mm():
            ins = None
            for kc in range(KC):
                ins = nc.tensor.matmul(self.PS[:, bank, :], self.onesD[:], SQ[s][:, kc, :],
                                       start=(kc == 0), stop=(kc == KC - 1))
            return ins
        self.pe.op([bSQ[s], self.bconst], [self.bPS[bank]], mm)
        self.rstd_from_psum(RST[s][:], self.PS[:, bank, :], [self.bPS[bank]], bRS[s])
        return RST[s], bRS[s]

    def norm_tile_to(self, scr, XN, bXN, gname, t, tt):
        nc = self.nc
        rst, brs = self.norm_stats_tile(scr, tt)
        c0 = tt * 512
        for kc in range(KC):
            self.dve.op([brs, self.bXT[kc][tt]], [bXN[kc][t]],
                        lambda: nc.vector.scalar_tensor_tensor(
                            XN[:, kc, t * 512:(t + 1) * 512], self.XT[:, kc, c0:c0 + 512], self.vcol(gname, kc),
                            rst[:], ALU.mult, ALU.mult))

    def final_norm_tile(self, tt):
        nc = self.nc
        rst, brs = self.norm_stats_tile(self.scr, tt)
        c0 = tt * 512
        for kc in range(KC):
            self.dve.op([brs, self.bXT[kc][tt]], [self.bXT[kc][tt]],
                        lambda: nc.vector.scalar_tensor_tensor(
                            self.XT[:, kc, c0:c0 + 512], self.XT[:, kc, c0:c0 + 512],
                            self.vcol("final_norm", kc), rst[:], ALU.mult, ALU.mult))

    def store_tile(self, tt):
        nc = self.nc
        ov = self.out.rearrange("(i p) d -> i p d", p=128)
        for i in range(tt * 4, tt * 4 + 4):
            s = i % 2
            pb = (i % 3) * 2

            def tr():
                ins = None
                for kc in range(KC):
                    ins = nc.tensor.transpose(self.PS[:, pb + kc // 4, (kc % 4) * 128:(kc % 4 + 1) * 128],
                                              self.XT[:, kc, i * 128:(i + 1) * 128], self.ident)
                return ins
            self.pe.op([self.bXT[kc][tt] for kc in range(KC)] + [self.bconst],
                       [self.bPS[pb], self.bPS[pb + 1]], tr)
            self.copy_on(self.evac_alt(i), self.ostg[s][:], self.PS[:, pb:pb + 2, :],
                         [self.bPS[pb], self.bPS[pb + 1]], [self.bost[s]])
            self.out_toks.append(self.sp.dma(ov[i], self.ostg[s][:], [self.bost[s]], []))

    def rstd_from_psum(self, out, ps, bps, bout):
        nc = self.nc
        self.act.op(bps + [self.bconst], [bout],
                    lambda: nc.scalar.activation(out, ps, AF.Ln, bias=self.cst[:, 0:1]))
        self.act.op([bout], [bout], lambda: nc.scalar.activation(out, out, AF.Exp, scale=-0.5))

    def rmsnorm_to(self, es_tmp, XN, bXN, gname, t0=0, nt=4):
        scr = self.norm_scratch(es_tmp)
        for t in range(nt):
            self.norm_tile_to(scr, XN, bXN, gname, t, t0 + t)

    def ffn(self, l, next_tile):
        nc = self.nc
        groups = [(0, 8), (8, 7), (15, 7)]
        GMAX = 8
        with ExitStack() as es:
            XN = self.XN
            bXN = self.bXNs
            H = es.enter_context(self.sbt("ffn_h", [128, GMAX, T], BF16))
            bH = [[Buf("h%d_%d" % (j, t)) for t in range(2)] for j in range(GMAX)]
            Gb = [es.enter_context(self.sbt("ffn_g%d" % i, [128, 2 + 1024], F32)) for i in range(2)]
            A = [es.enter_context(self.sbt("ffn_a%d" % i, [128, 1024], F32)) for i in range(2)]
            bG = [Buf("g0"), Buf("g1")]
            bA = [Buf("a0"), Buf("a1")]
            self.dve.op([], [bG[0]], lambda: nc.vector.memset(Gb[0][:, 0:2], 0.0))
            wup = self.f_w_up[l].rearrange("(kc p) c -> p kc c", p=128)
            wdn = self.f_w_down[l].rearrange("(j p) c -> j p c", p=128)
            cw = lambda tap, j: self.vcol("f_conv%d%d" % (l, tap), j)
            unit = 0
            nbank = 0
            for (g0, gn) in groups:
                jj = 0
                while jj < gn:
                    nb = min(4, gn - jj)
                    j0 = g0 + jj
                    self.ring.align()
                    wg = self.ring.load(wup[:, :, j0 * 128:(j0 + nb) * 128], 128, [KC, nb * 128])
                    wu = self.ring.load(wup[:, :, DFF + j0 * 128:DFF + (j0 + nb) * 128], 128, [KC, nb * 128])
                    for jb in range(nb):
                        j = j0 + jb
                        for hh in range(2):
                            pb = (unit % 2) * 4
                            s = hh
                            unit += 1

                            def mm(w, bank0):
                                ins = None
                                for kc in range(KC):
                                    for t in range(2):
                                        ins = nc.tensor.matmul(
                                            self.PS[:, bank0 + t, :], w.ap[:, kc, jb * 128:(jb + 1) * 128],
                                            XN[:, kc, (hh * 2 + t) * 512:(hh * 2 + t + 1) * 512],
                                            start=(kc == 0), stop=(kc == KC - 1))
                                return ins
                            rxn = [bXN[kc][hh * 2 + t] for kc in range(KC) for t in range(2)]
                            self.pe.op(rxn + wg.buf, [self.bPS[pb], self.bPS[pb + 1]], lambda: mm(wg, pb))
                            self.pe.op(rxn + wu.buf, [self.bPS[pb + 2], self.bPS[pb + 3]], lambda: mm(wu, pb + 2))
                            psg = self.PS[:, pb:pb + 2, :]
                            psu = self.PS[:, pb + 2:pb + 4, :]
                            bg = [self.bPS[pb], self.bPS[pb + 1]]
                            bu = [self.bPS[pb + 2], self.bPS[pb + 3]]
                            self.act.op(bg, [bA[s]], lambda: nc.scalar.activation(
                                A[s][:], psg, AF.Copy, scale=cw(2, j)))
                            self.act.op(bg, [bG[s]], lambda: nc.scalar.copy(Gb[s][:, 2:2 + 1024], psg))
                            if hh == 1:
                                self.act.op([bG[0]], [bG[1]], lambda: nc.scalar.copy(
                                    Gb[1][:, 0:2], Gb[0][:, 1024:1026]))
                            self.dve.op([bG[s], bA[s]], [bA[s]], lambda: nc.vector.scalar_tensor_tensor(
                                A[s][:], Gb[s][:, 1:1025], cw(1, j), A[s][:], ALU.mult, ALU.add))
                            self.dve.op([bG[s], bA[s]], [bA[s]], lambda: nc.vector.scalar_tensor_tensor(
                                A[s][:], Gb[s][:, 0:1024], cw(0, j), A[s][:], ALU.mult, ALU.add))
                            self.act.op([bA[s]], [bA[s]], lambda: nc.scalar.activation(A[s][:], A[s][:], AF.Silu))
                            self.dve.op([bA[s]] + bu, [bH[jj + jb][hh]], lambda: nc.vector.tensor_tensor(
                                H[:, jj + jb, hh * 1024:(hh + 1) * 1024], A[s][:], psu, ALU.mult))
                    jj += nb
                self.ring.align()
                wd = [self.ring.load(wdn[g0 + k], 128, [D]) for k in range(gn)]
                for tt in range(4):
                    for m in range(KC):
                        b = nbank % 8
                        nbank += 1

                        def mmd():
                            ins = None
                            for k in range(gn):
                                ins = nc.tensor.matmul(self.PS[:, b, :], wd[k].ap[:, m * 128:(m + 1) * 128],
                                                       H[:, k, tt * 512:(tt + 1) * 512],
                                                       start=(k == 0), stop=(k == gn - 1))
                            return ins
                        rd = [bH[k][tt // 2] for k in range(gn)]
                        for k in range(gn):
                            rd += wd[k].buf
                        self.pe.op(rd, [self.bPS[b]], mmd)
                        self.dve.op([self.bPS[b], self.bXT[m][tt]], [self.bXT[m][tt]],
                                    lambda: nc.vector.tensor_tensor(
                                        self.XT[:, m, tt * 512:(tt + 1) * 512],
                                        self.XT[:, m, tt * 512:(tt + 1) * 512], self.PS[:, b, :], ALU.add))
                    if g0 == groups[-1][0] and tt >= 1:
                        next_tile(tt - 1)
            next_tile(3)

    def sconv(self, next_tile):
        nc = self.nc
        with ExitStack() as es:
            XN = self.XN
            bXN = self.bXNs
            Y = es.enter_context(self.sbt("sc_y", [128, KC, T], BF16))
            bY = [[Buf("y%d_%d" % (j, t)) for t in range(4)] for j in range(KC)]
            Cs = [es.enter_context(self.sbt("sc_c%d" % i, [128, 512], F32)) for i in range(2)]
            Zb = [es.enter_context(self.sbt("sc_z%d" % i, [128, 2 + 512], F32)) for i in range(2)]
            A = [es.enter_context(self.sbt("sc_a%d" % i, [128, 512], F32)) for i in range(2)]
            Bs = [es.enter_context(self.sbt("sc_b%d" % i, [128, 512], F32)) for i in range(2)]
            bB = [Buf("b0"), Buf("b1")]
            bC = [Buf("c0"), Buf("c1")]
            bZ = [Buf("z0"), Buf("z1")]
            bA = [Buf("a0"), Buf("a1")]
            win = self.b_w_in.rearrange("(kc p) c -> p kc c", p=128)
            wout = self.b_w_out.rearrange("(j p) c -> j p c", p=128)
            cw = lambda tap, j: self.vcol("b_conv%d" % tap, j)
            NB = 2
            for j0 in range(0, KC, NB):
                self.ring.align()
                ws = [self.ring.load(win[:, :, sg * D + j0 * 128: sg * D + (j0 + NB) * 128], 128, [KC, NB * 128])
                      for sg in range(3)]
                for jb in range(NB):
                    j = j0 + jb
                    for tt in range(4):
                        s = tt % 2
                        pb = s * 3

                        def mm(w, bank):
                            ins = None
                            for kc in range(KC):
                                ins = nc.tensor.matmul(self.PS[:, bank, :], w.ap[:, kc, jb * 128:(jb + 1) * 128],
                                                       XN[:, kc, tt * 512:(tt + 1) * 512],
                                                       start=(kc == 0), stop=(kc == KC - 1))
                            return ins
                        for sg in range(3):
                            self.pe.op([bXN[kc][tt] for kc in range(KC)] + ws[sg].buf, [self.bPS[pb + sg]],
                                       lambda: mm(ws[sg], pb + sg))
                        self.act.op([self.bPS[pb + 1]], [bC[s]], lambda: nc.scalar.copy(Cs[s][:], self.PS[:, pb + 1, :]))
                        self.act.op([self.bPS[pb]], [bB[s]], lambda: nc.scalar.copy(Bs[s][:], self.PS[:, pb, :]))
                        if tt == 0:
                            self.dve.op([], [bZ[s]], lambda: nc.vector.memset(Zb[s][:, 0:2], 0.0))
                        else:
                            self.act.op([bZ[1 - s]], [bZ[s]], lambda: nc.scalar.copy(
                                Zb[s][:, 0:2], Zb[1 - s][:, 512:514]))
                        self.dve.op([bC[s], self.bPS[pb + 2]], [bZ[s]], lambda: nc.vector.tensor_tensor(
                            Zb[s][:, 2:514], Cs[s][:], self.PS[:, pb + 2, :], ALU.mult))
                        self.act.op([bZ[s]], [bA[s]], lambda: nc.scalar.activation(
                            A[s][:], Zb[s][:, 2:514], AF.Copy, scale=cw(2, j)))
                        self.dve.op([bZ[s], bA[s]], [bA[s]], lambda: nc.vector.scalar_tensor_tensor(
                            A[s][:], Zb[s][:, 1:513], cw(1, j), A[s][:], ALU.mult, ALU.add))
                        self.dve.op([bZ[s], bA[s]], [bA[s]], lambda: nc.vector.scalar_tensor_tensor(
                            A[s][:], Zb[s][:, 0:512], cw(0, j), A[s][:], ALU.mult, ALU.add))
                        self.dve.op([bA[s], bB[s]], [bY[j][tt]], lambda: nc.vector.tensor_tensor(
                            Y[:, j, tt * 512:(tt + 1) * 512], A[s][:], Bs[s][:], ALU.mult))
            self.out_proj(wout, Y, lambda kc, tg: [bY[kc][tg]], range(4), lambda tg: tg, next_tile)
            next_tile(3)

    def out_proj(self, wout, Y, ybufs, tiles, gtile, next_tile=None):
        nc = self.nc
        self.ring.align()
        wo = [self.ring.load(wout[kc], 128, [D]) for kc in range(KC)]
        nb = 0
        for t in tiles:
            tg = gtile(t)
            for m in range(KC):
                b = 6 + nb % 2
                nb += 1

                def mmo():
                    ins = None
                    for kc in range(KC):
                        ins = nc.tensor.matmul(self.PS[:, b, :], wo[kc].ap[:, m * 128:(m + 1) * 128],
                                               Y[:, kc, t * 512:(t + 1) * 512],
                                               start=(kc == 0), stop=(kc == KC - 1))
                    return ins
                rd = []
                for kc in range(KC):
                    rd += ybufs(kc, t) + wo[kc].buf
                self.pe.op(rd, [self.bPS[b]], mmo)
                self.dve.op([self.bPS[b], self.bXT[m][tg]], [self.bXT[m][tg]],
                            lambda: nc.vector.tensor_tensor(
                                self.XT[:, m, tg * 512:(tg + 1) * 512],
                                self.XT[:, m, tg * 512:(tg + 1) * 512], self.PS[:, b, :], ALU.add))
            if next_tile is not None and t >= 1:
                next_tile(t - 1)

    def gla(self):
        nc = self.nc
        PS = self.PS
        bPS = self.bPS
        win = self.a_w_in.rearrange("(kc p) c -> p kc c", p=128)
        wout = self.a_w_out.rearrange("(j p) c -> j p c", p=128)
        nbank = [0]

        def nextbank():
            b = nbank[0] % 8
            nbank[0] += 1
            return b

        with ExitStack() as es:
            S = [es.enter_context(self.sbt("gl_s%d" % i, [128, HEADS, DV], F32)) for i in range(2)]
            bS = [[Buf("s") for _ in range(HEADS)] for _ in range(2)]
            WGU = es.enter_context(self.sbt("gl_wgu", [32, 512], BF16))
            bWGU = Buf("wgu")
            self.dve.op([], [bWGU], lambda: nc.vector.memset(WGU[:], 0.0))
            self.pool.dma(WGU[0:17, :], self.a_wgu, [], [bWGU])
            self.dve.op([], bS[1], lambda: nc.vector.memset(S[1][:], 0.0))
            for hf in range(2):
                with ExitStack() as esh:
                    QT = esh.enter_context(self.sbt("gl_qt", [128, HEADS, 1024], BF16))
                    bQT = [[Buf("qt") for _ in range(2)] for _ in range(HEADS)]
                    XN = esh.enter_context(self.sbt("gl_xn", [128, KC, 1024], BF16))
                    bXN = [[Buf("xn") for _ in range(2)] for _ in range(KC)]
                    KD = esh.enter_context(self.sbt("gl_kd", [128, 8, 512], BF16))
                    bKD = [Buf("kd") for _ in range(8)]
                    V = esh.enter_context(self.sbt("gl_v", [128, 8, 1024], BF16))
                    bV = [Buf("v") for _ in range(8)]
                    SR = esh.enter_context(self.sbt("gl_sr", [128, KC, 1024], BF16))
                    bSR = [[Buf("sr") for _ in range(2)] for _ in range(KC)]
                    DEC = esh.enter_context(self.sbt("gl_dec", [128, HEADS, 16], F32))
                    bDEC = [Buf("dec") for _ in range(8)]
                    with ExitStack() as esa:
                        GL = esa.enter_context(self.sbt("gl_gl", [32, 1024], BF16))
                        bGL = Buf("gl")
                        SPt = [esa.enter_context(self.sbt("gl_sp%d" % i, [128, 512], F32)) for i in range(2)]
                        E = [esa.enter_context(self.sbt("gl_e%d" % i, [128, 512], F32)) for i in range(2)]
                        bSP = [Buf("sp0"), Buf("sp1")]
                        bE = [Buf("e0"), Buf("e1")]
                        with ExitStack() as es2:
                            self.rmsnorm_to(es2, XN, bXN, "a_norm", t0=hf * 2, nt=2)
                        self.dve.op([], [bGL], lambda: nc.vector.memset(GL[:], 0.0))
                        self.dve.op([], [bGL], lambda: nc.vector.memset(GL[0:17, :], 1.0))
                        self.ring.align()
                        wgl = self.ring.load(win[:, :, 3072:3088], 128, [KC, 16])
                        for tt in range(2):
                            b = nextbank()

                            def mmg():
                                ins = None
                                for kc in range(KC):
                                    ins = nc.tensor.matmul(PS[0:16, b, :], wgl.ap[:, kc, :],
                                                           XN[:, kc, tt * 512:(tt + 1) * 512],
                                                           start=(kc == 0), stop=(kc == KC - 1))
                                return ins
                            self.pe.op([bXN[kc][tt] for kc in range(KC)] + wgl.buf, [bPS[b]], mmg)
                            self.act.op([bPS[b]], [bGL], lambda: nc.scalar.copy(
                                GL[0:16, tt * 512:(tt + 1) * 512], PS[0:16, b, :]))
                        wq = self.ring.load(win[:, :, 0:512], 128, [KC, 512])
                        for j in range(HEADS):
                            for tt in range(2):
                                b = nextbank()

                                def mmq():
                                    ins = None
                                    for kc in range(KC):
                                        ins = nc.tensor.matmul(PS[:, b, :], wq.ap[:, kc, j * 128:(j + 1) * 128],
                                                               XN[:, kc, tt * 512:(tt + 1) * 512],
                                                               start=(kc == 0), stop=(kc == KC - 1))
                                    return ins
                                self.pe.op([bXN[kc][tt] for kc in range(KC)] + wq.buf, [bPS[b]], mmq)
                                self.act.op([bPS[b]], [bQT[j][tt]], lambda: nc.scalar.activation(
                                    QT[:, j, tt * 512:(tt + 1) * 512], PS[:, b, :], AF.Copy, scale=float(DK) ** -0.5))
                        self.ring.align()
                        wk = self.ring.load(win[:, :, 512:1024], 128, [KC, 512])

                        def gate_pre(i):
                            s = i % 2
                            b1 = nextbank()
                            self.pe.op([bGL, bWGU], [bPS[b1]], lambda: nc.tensor.matmul(
                                PS[:, b1, :], GL[:, i * 128:(i + 1) * 128], WGU[:, :], start=True, stop=True))
                            self.act.op([bPS[b1]], [bSP[s]], lambda: nc.scalar.activation(
                                SPt[s][:], PS[:, b1, :], AF.Exp, scale=-1.0))
                            self.act.op([bSP[s], self.bconst], [bSP[s]], lambda: nc.scalar.activation(
                                SPt[s][:], SPt[s][:], AF.Ln, bias=self.cst[:, 1:2]))

                        gate_pre(0)
                        for i in range(8):
                            s = i % 2
                            if i + 1 < 8:
                                gate_pre(i + 1)
                            b4 = nextbank()

                            def mmk():
                                ins = None
                                for kc in range(KC):
                                    ins = nc.tensor.matmul(PS[:, b4, :], XN[:, kc, i * 128:(i + 1) * 128], wk.ap[:, kc, :],
                                                           start=(kc == 0), stop=(kc == KC - 1))
                                return ins
                            self.pe.op([bXN[kc][i // 4] for kc in range(KC)] + wk.buf, [bPS[b4]], mmk)
                            b2 = nextbank()
                            self.pe.op([bSP[s], self.bconst], [bPS[b2]], lambda: nc.tensor.matmul(
                                PS[:, b2, :], self.M1, SPt[s][:], start=True, stop=True))
                            b3 = nextbank()

                            def mmt():
                                ins = None
                                for h in range(HEADS):
                                    ins = nc.tensor.matmul(PS[:, b3, h * 2:h * 2 + 2], SPt[s][:, h * 128:(h + 1) * 128],
                                                           self.M2, start=True, stop=True)
                                return ins
                            self.pe.op([bSP[s], self.bconst], [bPS[b3]], mmt)
                            self.act.op([bPS[b2]], [bE[s]], lambda: nc.scalar.activation(E[s][:], PS[:, b2, :], AF.Exp))
                            self.act.op([bPS[b3]], [bDEC[i]], lambda: nc.scalar.activation(
                                DEC[:, :, 2 * i:2 * i + 2],
                                PS[:, b3, 0:8].rearrange("p (h c) -> p h c", c=2), AF.Exp))
                            self.dve.op([bPS[b4], bE[s]], [bKD[i]], lambda: nc.vector.tensor_tensor(
                                KD[:, i, :], PS[:, b4, :], E[s][:], ALU.mult))
                        self.ring.align()
                        for vb in range(2):
                            wv = self.ring.load(win[:, :, 1024 + vb * 512:1024 + (vb + 1) * 512], 128, [KC, 512])
                            for i in range(8):
                                b = nextbank()

                                def mmv():
                                    ins = None
                                    for kc in range(KC):
                                        ins = nc.tensor.matmul(PS[:, b, :], XN[:, kc, i * 128:(i + 1) * 128],
                                                               wv.ap[:, kc, :], start=(kc == 0), stop=(kc == KC - 1))
                                    return ins
                                self.pe.op([bXN[kc][i // 4] for kc in range(KC)] + wv.buf, [bPS[b]], mmv)
                                self.copy_on(self.evac_alt(i), V[:, i, vb * 512:(vb + 1) * 512], PS[:, b, :],
                                             [bPS[b]], [bV[i]])
                        self.ring.align()
                        for rb in range(2):
                            wr = self.ring.load(win[:, :, 2048 + rb * 512:2048 + (rb + 1) * 512], 128, [KC, 512])
                            for jb in range(4):
                                j = rb * 4 + jb
                                for tt in range(2):
                                    b = nextbank()

                                    def mmr():
                                        ins = None
                                        for kc in range(KC):
                                            ins = nc.tensor.matmul(PS[:, b, :], wr.ap[:, kc, jb * 128:(jb + 1) * 128],
                                                                   XN[:, kc, tt * 512:(tt + 1) * 512],
                                                                   start=(kc == 0), stop=(kc == KC - 1))
                                        return ins
                                    self.pe.op([bXN[kc][tt] for kc in range(KC)] + wr.buf, [bPS[b]], mmr)
                                    self.act.op([bPS[b]], [bSR[j][tt]], lambda: nc.scalar.activation(
                                        SR[:, j, tt * 512:(tt + 1) * 512], PS[:, b, :], AF.Silu))
                    self.dump("QT", QT[:], [128, HEADS, 1024], BF16, [])
                    self.dump("KD", KD[:], [128, 8, 512], BF16, [])
                    self.dump("V", V[:], [128, 8, 1024], BF16, [])
                    self.dump("DEC", DEC[:], [128, HEADS, 16], F32, [])
                    self.barrier()
                    with ExitStack() as esb:
                        TQ = 4
                        TW = TQ * CH
                        SB = [esb.enter_context(self.sbt("gl_sb%d" % i, [128, HEADS, DV], BF16)) for i in range(2)]
                        bSB = [Buf("sb0"), Buf("sb1")]
                        OT2 = [esb.enter_context(self.sbt("gl_ot%d" % i, [128, KC, TW], F32)) for i in range(2)]
                        SQh = [esb.enter_context(self.sbt("gl_sqh%d" % i, [128, 2, TW], BF16)) for i in range(2)]
                        bSQh = [Buf("sqh0"), Buf("sqh1")]
                        RS = esb.enter_context(self.sbt("gl_rs", [128, 2, TW], F32))
                        OF2 = [esb.enter_context(self.sbt("gl_of%d" % i, [128, KC, TW], BF16)) for i in range(2)]
                        TMP = [esb.enter_context(self.sbt("gl_tmp%d" % i, [128, TW], F32)) for i in range(2)]
                        bTMP = [Buf("tmp0"), Buf("tmp1")]
                        bOF2 = [[Buf("of") for _ in range(KC)] for _ in range(2)]
                        bOT2 = [[Buf("ot") for _ in range(TQ)] for _ in range(2)]
                        bRS = [Buf("rs") for _ in range(2)]

                        def emit_upd(c):
                            i = c // 2
                            part = (c % 2) * 64
                            su = (c % 2) * 2

                            def mmu():
                                ins = None
                                for h in range(HEADS):
                                    ins = nc.tensor.matmul(
                                        PS[:, su + h // 2, (h % 2) * 256:(h % 2 + 1) * 256],
                                        KD[part:part + 64, i, h * 128:(h + 1) * 128],
                                        V[part:part + 64, i, h * 256:(h + 1) * 256], start=True, stop=True)
                                return ins
                            self.pe.op([bKD[i], bV[i]], [bPS[su], bPS[su + 1]], mmu)

                        def emit_state(c):
                            i = c // 2
                            su = (c % 2) * 2
                            sn = (hf * 16 + c) % 2
                            so = 1 - sn
                            for h in range(HEADS):
                                self.dve.op([bS[so][h], bDEC[i], bPS[su + h // 2]], [bS[sn][h]],
                                            lambda: nc.vector.scalar_tensor_tensor(
                                                S[sn][:, h, :], S[so][:, h, :], DEC[:, h, c:c + 1],
                                                PS[:, su + h // 2, (h % 2) * 256:(h % 2 + 1) * 256],
                                                ALU.mult, ALU.add))
                            self.act.op(bS[sn], [bSB[c % 2]], lambda: nc.scalar.copy(SB[c % 2][:], S[sn][:]))

                        def emit_o(c):
                            tq = c // TQ
                            cl = c % TQ
                            bo = 4 + c % 2

                            def mmo():
                                ins = None
                                for j in range(KC):
                                    h, hv = j // 2, j % 2
                                    ins = nc.tensor.matmul(PS[:, bo, j * 64:(j + 1) * 64],
                                                           SB[c % 2][:, h, hv * 128:(hv + 1) * 128],
                                                           QT[:, h, c * 64:(c + 1) * 64], start=True, stop=True)
                                return ins
                            self.pe.op([bSB[c % 2]] + [bQT[h][c // 8] for h in range(HEADS)], [bPS[bo]], mmo)
                            pso = PS[:, bo, :].rearrange("p (j l) -> p j l", l=64)
                            self.act.op([bPS[bo]], [bOT2[tq % 2][cl]], lambda: nc.scalar.copy(
                                OT2[tq % 2][:, :, cl * 64:(cl + 1) * 64], pso))

                        def act_part(tq, h):
                            ot = OT2[tq % 2]
                            s2 = h % 2
                            bn = 6 + s2
                            self.act.op(bOT2[tq % 2], [bSQh[s2]], lambda: nc.scalar.activation(
                                SQh[s2][:], ot[:, 2 * h:2 * h + 2, :], AF.Square))

                            def mmn():
                                nc.tensor.matmul(PS[:, bn, 0:TW], self.onesV[:], SQh[s2][:, 0, :], start=True, stop=False)
                                return nc.tensor.matmul(PS[:, bn, 0:TW], self.onesV[:], SQh[s2][:, 1, :],
                                                        start=False, stop=True)
                            self.pe.op([bSQh[s2], self.bconst], [bPS[bn]], mmn)
                            self.rstd_from_psum(RS[:, s2, :], PS[:, bn, 0:TW], [bPS[bn]], bRS[s2])

                        def dve_part(tq, h):
                            ot, of = OT2[tq % 2], OF2[tq % 2]
                            for j in (2 * h, 2 * h + 1):
                                self.dve.op(bOT2[tq % 2] + [bRS[h % 2]], [bTMP[j % 2]],
                                            lambda: nc.vector.scalar_tensor_tensor(
                                                TMP[j % 2][:], ot[:, j, :], self.vcol("a_gn", j), RS[:, h % 2, :],
                                                ALU.mult, ALU.mult))
                                self.dve.op([bTMP[j % 2], bSR[j][tq // 2]], [bOF2[tq % 2][j]],
                                            lambda: nc.vector.tensor_tensor(
                                                of[:, j, :], TMP[j % 2][:], SR[:, j, tq * TW:(tq + 1) * TW], ALU.mult))

                        wos = {}

                        def load_wo(tq):
                            self.ring.align()
                            wos[tq] = [self.ring.load(wout[kc], 128, [D]) for kc in range(KC)]

                        def outproj_group(tq, m):
                            wo = wos[tq]
                            of, bof = OF2[tq % 2], bOF2[tq % 2]
                            tg = hf * 2 + tq // 2
                            c0 = tg * 512 + (tq % 2) * TW
                            b = 6 + m % 2

                            def mmo2():
                                ins = None
                                for kc in range(KC):
                                    ins = nc.tensor.matmul(PS[:, b, 0:TW], wo[kc].ap[:, m * 128:(m + 1) * 128],
                                                           of[:, kc, :], start=(kc == 0), stop=(kc == KC - 1))
                                return ins
                            rd = list(bof)
                            for kc in range(KC):
                                rd += wo[kc].buf
                            self.pe.op(rd, [bPS[b]], mmo2)
                            self.dve.op([bPS[b], self.bXT[m][tg]], [self.bXT[m][tg]],
                                        lambda: nc.vector.tensor_tensor(
                                            self.XT[:, m, c0:c0 + TW],
                                            self.XT[:, m, c0:c0 + TW], PS[:, b, 0:TW], ALU.add))

                        NST = 16
                        sched = {}

                        def at(c, fn):
                            sched.setdefault(c, []).append(fn)
                        for tq in range(4):
                            b0 = 4 * tq + 4
                            at(b0, lambda tq=tq: act_part(tq, 0))
                            at(b0, lambda tq=tq: act_part(tq, 1))
                            at(b0 + 1, lambda tq=tq: dve_part(tq, 0))
                            at(b0 + 1, lambda tq=tq: act_part(tq, 2))
                            at(b0 + 2, lambda tq=tq: dve_part(tq, 1))
                            at(b0 + 2, lambda tq=tq: act_part(tq, 3))
                            at(b0 + 3, lambda tq=tq: dve_part(tq, 2))
                            at(b0 + 3, lambda tq=tq: dve_part(tq, 3))
                            at(b0 + 3, lambda tq=tq: load_wo(tq))
                            for k in range(4):
                                at(b0 + 4 + k, lambda tq=tq, k=k: outproj_group(tq, 2 * k))
                                at(b0 + 4 + k, lambda tq=tq, k=k: outproj_group(tq, 2 * k + 1))
                        emit_upd(0)
                        emit_state(0)
                        emit_upd(1)
                        for c in range(NST):
                            if c + 2 < NST:
                                emit_upd(c + 2)
                            if c + 1 < NST:
                                emit_state(c + 1)
                            emit_o(c)
                            for fn in sched.get(c, []):
                                fn()
                        for c in sorted(k for k in sched if k >= NST):
                            for fn in sched[c]:
                                fn()
                    self.barrier()


def _chunkcols(v):
    v = np.asarray(v, dtype=np.float32)
    return np.ascontiguousarray(v.reshape(-1, 128).T)


def make_consts():
    c = np.zeros((128, NCONST), dtype=np.float32)
    c[:, 0:128] = np.eye(128, dtype=np.float32)
    lp = np.arange(128)[:, None]
    l = np.arange(128)[None, :]
    c[:, 128:256] = np.where((lp > l) & (lp // CH == l // CH), -1.0 / 16.0, 0.0)
    c[:, 256:258] = np.where(lp // CH == np.arange(2)[None, :], -1.0 / 16.0, 0.0)
    return c


def make_in_maps(inp, ncores=NCORES):
    vec = np.zeros((128, NV), dtype=np.float32)

    def put(name, v):
        a = _chunkcols(v)
        vec[:, VC[name]:VC[name] + a.shape[1]] = a
    put("a_norm", inp["a_norm"][0])
    put("b_norm", inp["b_norm"][0])
    put("f_norm0", inp["f_norm"][0])
    put("f_norm1", inp["f_norm"][1])
    put("final_norm", inp["final_norm"])
    put("a_gn", inp["a_gn"][0])
    for t in range(3):
        put("b_conv%d" % t, inp["b_conv"][0][t])
        put("f_conv0%d" % t, inp["f_conv"][0][t])
        put("f_conv1%d" % t, inp["f_conv"][1][t])
    wgu = np.ascontiguousarray(np.concatenate(
        [np.asarray(inp["a_w_gate_up"][0], dtype=np.float32),
         np.asarray(inp["a_b_gate"][0], dtype=np.float32)[None, :]], axis=0))
    shared = {
        "vecs": vec,
        "consts": make_consts(),
        "a_w_in": np.ascontiguousarray(inp["a_w_in"][0], dtype=np.float32),
        "a_wgu": wgu,
        "a_w_out": np.ascontiguousarray(inp["a_w_out"][0], dtype=np.float32),
        "b_w_in": np.ascontiguousarray(inp["b_w_in"][0], dtype=np.float32),
        "b_w_out": np.ascontiguousarray(inp["b_w_out"][0], dtype=np.float32),
        "f_w_up": np.ascontiguousarray(inp["f_w_up"], dtype=np.float32),
        "f_w_down": np.ascontiguousarray(inp["f_w_down"], dtype=np.float32),
    }
    x = np.asarray(inp["x"], dtype=np.float32)
    maps = []
    for c in range(ncores):
        m = dict(shared)
        m["x"] = np.ascontiguousarray(x[c])
        maps.append(m)
    return maps


ALL_STAGES = ("gla", "ffn0", "sconv", "ffn1", "fnorm")
_PROG_CACHE = {}


def get_prog(stages=ALL_STAGES):
    key = tuple(stages)
    if key not in _PROG_CACHE:
        _PROG_CACHE[key] = Prog(key)
    return _PROG_CACHE[key]


def kernel(**inputs):
    prog = get_prog(ALL_STAGES)
    in_maps = make_in_maps(inputs)
    res = run_bass_kernel_spmd(prog.nc, in_maps, core_ids=list(range(NCORES)))
    return np.stack([np.asarray(r["out"], dtype=np.float32) for r in res.results], axis=0)
```

```python
from contextlib import ExitStack

import numpy as np
import concourse.bass as bass
import concourse.mybir as mybir
from concourse.bass_utils import run_bass_kernel_spmd

F32 = mybir.dt.float32
BF16 = mybir.dt.bfloat16
AF = mybir.ActivationFunctionType
ALU = mybir.AluOpType

D = 1024
T = 2048
NCORES = 8
KC = 8
DFF = 2816
NJ = 22
EPS = 1e-6
HEADS = 4
DK = 128
DV = 256
CH = 64
PROJ_A = 3088
NSLOT = 16

VC = {}
_c = 0
for _name, _n in [("a_norm", 8), ("b_norm", 8), ("f_norm0", 8), ("f_norm1", 8), ("final_norm", 8),
                  ("a_gn", 8), ("b_conv0", 8), ("b_conv1", 8), ("b_conv2", 8),
                  ("f_conv00", 22), ("f_conv01", 22), ("f_conv02", 22),
                  ("f_conv10", 22), ("f_conv11", 22), ("f_conv12", 22)]:
    VC[_name] = _c
    _c += _n
NV = _c
NCONST = 128 + 128 + 2


class Tok:
    __slots__ = ("sem", "val")

    def __init__(self, sem, val):
        self.sem = sem
        self.val = val


class Buf:
    __slots__ = ("name", "w", "r")

    def __init__(self, name=""):
        self.name = name
        self.w = None
        self.r = {}


class Q:
    def __init__(self, nc, eng, name, is_pe=False):
        self.eng = eng
        self.sem = nc.alloc_semaphore("q_" + name)
        self.n = 0
        self.seen = {}
        self.is_pe = is_pe
        self.name = name

    def wait(self, tok):
        if tok is None:
            return
        if self.seen.get(tok.sem, 0) >= tok.val:
            return
        self.eng.wait_ge(tok.sem, tok.val)
        self.seen[tok.sem] = tok.val

    def deps(self, reads, writes):
        for b in reads:
            self.wait(b.w)
        for b in writes:
            if b.w is not None and not (self.is_pe and b.w.sem is self.sem):
                self.wait(b.w)
            for s, t in b.r.items():
                if s is self.sem:
                    continue
                self.wait(t)

    def done(self, ins, reads, writes):
        self.n += 1
        ins.then_inc(self.sem, 1)
        tok = Tok(self.sem, self.n)
        for b in reads:
            b.r[self.sem] = tok
        for b in writes:
            b.w = tok
            b.r = {}
        return tok

    def op(self, reads, writes, fn):
        self.deps(reads, writes)
        return self.done(fn(), reads, writes)


class DmaQ:
    def __init__(self, nc, eng, name, nsem):
        self.eng = eng
        self.sems = [nc.alloc_semaphore("d_%s%d" % (name, i)) for i in range(nsem)]
        self.cnt = [0] * nsem
        self.i = 0
        self.seen = {}

    def wait(self, tok):
        if tok is None:
            return
        if self.seen.get(tok.sem, 0) >= tok.val:
            return
        self.eng.wait_ge(tok.sem, tok.val)
        self.seen[tok.sem] = tok.val

    def dma(self, out, in_, reads, writes, extra_waits=()):
        for t in extra_waits:
            self.wait(t)
        for b in reads:
            self.wait(b.w)
        for b in writes:
            self.wait(b.w)
            for t in b.r.values():
                self.wait(t)
        k = self.i % len(self.sems)
        self.i += 1
        self.cnt[k] += 16
        sem = self.sems[k]
        self.eng.dma_start(out=out, in_=in_).then_inc(sem, 16)
        tok = Tok(sem, self.cnt[k])
        for b in reads:
            b.r[sem] = tok
        for b in writes:
            b.w = tok
            b.r = {}
        return tok


class WBlock:
    def __init__(self, ap, buf, slots):
        self.ap = ap
        self.buf = buf
        self.slots = slots


class Ring:
    def __init__(self, nc, dq, nslot):
        self.nc = nc
        self.dq = dq
        self.nslot = nslot
        self.t = nc.alloc_sbuf_tensor("wring", [128, nslot * 1024], BF16)
        self.bufs = [Buf("ring%d" % i) for i in range(nslot)]
        self.head = 0

    def align(self):
        half = self.nslot // 2
        if self.head % half:
            self.head = (self.head // half + 1) * half
        if self.head >= self.nslot:
            self.head = 0

    def load(self, src, nparts, shape_free):
        nel = int(np.prod(shape_free))
        ns = (nel + 1023) // 1024
        assert ns <= self.nslot
        if self.head + ns > self.nslot:
            self.head = 0
        s0 = self.head
        self.head += ns
        slots = self.bufs[s0:s0 + ns]
        flat = self.t[0:nparts, s0 * 1024: s0 * 1024 + nel]
        if len(shape_free) == 2:
            dst = flat.rearrange("p (a b) -> p a b", b=shape_free[1])
        else:
            dst = flat
        self.dq.dma(dst, src, [], slots)
        return WBlock(dst, slots, slots)


class Prog:
    def __init__(self, stages, dbg=None):
        self.stages = stages
        self.dbg = dbg
        self.dumped = set()
        nc = bass.Bass("TRN2", target_bir_lowering=False)
        self.nc = nc
        dt = nc.dram_tensor
        self.x = dt("x", [T, D], F32, kind="ExternalInput").ap()
        self.vecs_d = dt("vecs", [128, NV], F32, kind="ExternalInput").ap()
        self.consts_d = dt("consts", [128, NCONST], F32, kind="ExternalInput").ap()
        self.a_w_in = dt("a_w_in", [D, PROJ_A], F32, kind="ExternalInput").ap()
        self.a_wgu = dt("a_wgu", [17, 512], F32, kind="ExternalInput").ap()
        self.a_w_out = dt("a_w_out", [D, D], F32, kind="ExternalInput").ap()
        self.b_w_in = dt("b_w_in", [D, 3 * D], F32, kind="ExternalInput").ap()
        self.b_w_out = dt("b_w_out", [D, D], F32, kind="ExternalInput").ap()
        self.f_w_up = dt("f_w_up", [2, D, 2 * DFF], F32, kind="ExternalInput").ap()
        self.f_w_down = dt("f_w_down", [2, DFF, D], F32, kind="ExternalInput").ap()
        self.out = dt("out", [T, D], F32, kind="ExternalOutput").ap()

        self.pe = Q(nc, nc.tensor, "pe", is_pe=True)
        self.act = Q(nc, nc.scalar, "act")
        self.dve = Q(nc, nc.vector, "dve")
        self.gp = Q(nc, nc.gpsimd, "gp")
        self.sp = DmaQ(nc, nc.sync, "sp", 4)
        self.pool = DmaQ(nc, nc.gpsimd, "pool", NSLOT)
        self.ring = Ring(nc, self.pool, NSLOT)

        self.XT = nc.alloc_sbuf_tensor("XT", [128, KC, T], F32)
        self.bXT = [[Buf("XT%d_%d" % (k, t)) for t in range(4)] for k in range(KC)]
        self.vecs = nc.alloc_sbuf_tensor("vecs_sb", [128, NV], F32)
        self.consts = nc.alloc_sbuf_tensor("consts_sb", [128, NCONST], F32)
        self.onesD = nc.alloc_sbuf_tensor("onesD", [128, 128], BF16)
        self.onesV = nc.alloc_sbuf_tensor("onesV", [128, 128], BF16)
        self.bconst = Buf("const")
        self.cst = nc.alloc_sbuf_tensor("cst", [128, 2], F32)
        self.PS = nc.alloc_psum_tensor("PS", [128, 8, 512], F32)
        self.bPS = [Buf("ps%d" % i) for i in range(8)]
        self.ident = self.consts[:, 0:128]
        self.M1 = self.consts[:, 128:256]
        self.M2 = self.consts[:, 256:258]

        self.build()

    def sbt(self, name, shape, dtype):
        self._uid = getattr(self, "_uid", 0) + 1
        return self.nc.sbuf_tensor("%s_u%d" % (name, self._uid), shape, dtype)

    def dump(self, name, ap, shape, dtype, bufs):
        if not self.dbg or name in self.dumped:
            return
        self.dumped.add(name)
        d = self.nc.dram_tensor("dbg_" + name, list(shape), dtype, kind="ExternalOutput").ap()
        self.barrier()
        for q in (self.pe, self.act, self.dve):
            if q.n > 0:
                self.sp.wait(Tok(q.sem, q.n))
        t = self.sp.dma(d, ap, bufs, [])
        for q in (self.pe, self.act, self.dve):
            q.wait(t)

    def vcol(self, name, j=0):
        c = VC[name] + j
        return self.vecs[:, c:c + 1]

    def barrier(self):
        qs = [self.pe, self.act, self.dve]
        for q in qs:
            for p in qs + [self.gp]:
                if p is not q and p.n > 0:
                    q.wait(Tok(p.sem, p.n))

    def evac_alt(self, idx):
        return self.act if idx % 2 == 0 else self.dve

    def copy_on(self, q, out, in_, reads, writes):
        if q is self.act:
            return q.op(reads, writes, lambda: self.nc.scalar.copy(out, in_))
        return q.op(reads, writes, lambda: self.nc.vector.tensor_copy(out, in_))

    def build(self):
        nc = self.nc
        st = self.stages
        self.sp.dma(self.vecs[:], self.vecs_d, [], [self.bconst])
        self.sp.dma(self.consts[:], self.consts_d, [], [self.bconst])
        self.dve.op([], [self.bconst], lambda: nc.vector.memset(self.onesD[:], 1.0 / D))
        self.dve.op([], [self.bconst], lambda: nc.vector.memset(self.onesV[:], 1.0 / DV))
        self.dve.op([], [self.bconst], lambda: nc.vector.memset(self.cst[:, 0:1], EPS))
        self.dve.op([], [self.bconst], lambda: nc.vector.memset(self.cst[:, 1:2], 1.0))
        self.load_x()
        if "gla" in st:
            self.gla()
        with ExitStack() as eso:
            self.XN = eso.enter_context(self.sbt("xn_shared", [128, KC, T], BF16))
            self.bXNs = [[Buf("xn") for _ in range(4)] for _ in range(KC)]
            self.scr = self.norm_scratch(eso)
            self.ostg = [eso.enter_context(self.sbt("ostage%d" % i, [128, D], F32)) for i in range(2)]
            self.bost = [Buf("ost%d" % i) for i in range(2)]
            self.out_toks = []
            phases = [p for p in ("ffn0", "sconv", "ffn1") if p in st]
            gname = {"ffn0": "f_norm0", "sconv": "b_norm", "ffn1": "f_norm1"}
            do_fnorm = "fnorm" in st

            def norm_tile_for(ph):
                return lambda tt: self.norm_tile_to(self.scr, self.XN, self.bXNs, gname[ph], tt, tt)

            def final_tile(tt):
                if do_fnorm:
                    self.final_norm_tile(tt)
                if tt >= 1:
                    self.store_tile(tt - 1)

            if phases:
                for tt in range(4):
                    norm_tile_for(phases[0])(tt)
            for k, ph in enumerate(phases):
                nxt = norm_tile_for(phases[k + 1]) if k + 1 < len(phases) else final_tile
                if ph == "sconv":
                    self.sconv(nxt)
                else:
                    self.ffn(int(ph[-1]), nxt)
            if not phases:
                for tt in range(4):
                    final_tile(tt)
            self.store_tile(3)
            for t in self.out_toks[-4:]:
                self.sp.wait(t)
            for t in self.out_toks[-4:]:
                self.pe.wait(t)

    def load_x(self):
        nc = self.nc
        with ExitStack() as es:
            stg = [es.enter_context(self.sbt("xstage%d" % i, [128, 4, D], F32)) for i in range(2)]
            bst = [Buf("xst0"), Buf("xst1")]
            xv = self.x.rearrange("(g i p) d -> g p i d", i=4, p=128)
            n = 0
            for tg in range(4):
                s = tg % 2
                self.sp.dma(stg[s][:], xv[tg], [], [bst[s]])
                for kc in range(KC):
                    b = n % 8
                    n += 1

                    def tr():
                        ins = None
                        for i in range(4):
                            ins = nc.tensor.transpose(self.PS[:, b, i * 128:(i + 1) * 128],
                                                      stg[s][:, i, kc * 128:(kc + 1) * 128], self.ident)
                        return ins
                    self.pe.op([bst[s], self.bconst], [self.bPS[b]], tr)
                    self.copy_on(self.evac_alt(n), self.XT[:, kc, tg * 512:(tg + 1) * 512], self.PS[:, b, :],
                                 [self.bPS[b]], [self.bXT[kc][tg]])
            self.barrier()

    def norm_scratch(self, es):
        SQ = [es.enter_context(self.sbt("sq%d" % i, [128, KC, 512], BF16)) for i in range(2)]
        RST = [es.enter_context(self.sbt("rstd%d" % i, [128, 512], F32)) for i in range(2)]
        return (SQ, [Buf("sq0"), Buf("sq1")], RST, [Buf("rs0"), Buf("rs1")])

    def norm_stats_tile(self, scr, tt):
        nc = self.nc
        SQ, bSQ, RST, bRS = scr
        s = tt % 2
        c0 = tt * 512
        bank = 6 + s
        self.act.op([self.bXT[kc][tt] for kc in range(KC)], [bSQ[s]],
                    lambda: nc.scalar.activation(SQ[s][:], self.XT[:, :, c0:c0 + 512], AF.Square))

        def mm():
            ins = None
            for kc in range(KC):
                ins = nc.tensor.matmul(self.PS[:, bank, :], self.onesD[:], SQ[s][:, kc, :],
                                       start=(kc == 0), stop=(kc == KC - 1))
            return ins
        self.pe.op([bSQ[s], self.bconst], [self.bPS[bank]], mm)
        self.rstd_from_psum(RST[s][:], self.PS[:, bank, :], [self.bPS[bank]], bRS[s])
        return RST[s], bRS[s]

    def norm_tile_to(self, scr, XN, bXN, gname, t, tt):
        nc = self.nc
        rst, brs = self.norm_stats_tile(scr, tt)
        c0 = tt * 512
        for kc in range(KC):
            self.dve.op([brs, self.bXT[kc][tt]], [bXN[kc][t]],
                        lambda: nc.vector.scalar_tensor_tensor(
                            XN[:, kc, t * 512:(t + 1) * 512], self.XT[:, kc, c0:c0 + 512], self.vcol(gname, kc),
                            rst[:], ALU.mult, ALU.mult))

    def final_norm_tile(self, tt):
        nc = self.nc
        rst, brs = self.norm_stats_tile(self.scr, tt)
        c0 = tt * 512
        for kc in range(KC):
            self.dve.op([brs, self.bXT[kc][tt]], [self.bXT[kc][tt]],
                        lambda: nc.vector.scalar_tensor_tensor(
                            self.XT[:, kc, c0:c0 + 512], self.XT[:, kc, c0:c0 + 512],
                            self.vcol("final_norm", kc), rst[:], ALU.mult, ALU.mult))

    def store_tile(self, tt):
        nc = self.nc
        ov = self.out.rearrange("(i p) d -> i p d", p=128)
        for i in range(tt * 4, tt * 4 + 4):
            s = i % 2
            pb = (i % 3) * 2

            def tr():
                ins = None
                for kc in range(KC):
                    ins = nc.tensor.transpose(self.PS[:, pb + kc // 4, (kc % 4) * 128:(kc % 4 + 1) * 128],
                                              self.XT[:, kc, i * 128:(i + 1) * 128], self.ident)
                return ins
            self.pe.op([self.bXT[kc][tt] for kc in range(KC)] + [self.bconst],
                       [self.bPS[pb], self.bPS[pb + 1]], tr)
            self.copy_on(self.evac_alt(i), self.ostg[s][:], self.PS[:, pb:pb + 2, :],
                         [self.bPS[pb], self.bPS[pb + 1]], [self.bost[s]])
            self.out_toks.append(self.sp.dma(ov[i], self.ostg[s][:], [self.bost[s]], []))

    def rstd_from_psum(self, out, ps, bps, bout):
        nc = self.nc
        self.act.op(bps + [self.bconst], [bout],
                    lambda: nc.scalar.activation(out, ps, AF.Ln, bias=self.cst[:, 0:1]))
        self.act.op([bout], [bout], lambda: nc.scalar.activation(out, out, AF.Exp, scale=-0.5))

    def rmsnorm_to(self, es_tmp, XN, bXN, gname, t0=0, nt=4):
        scr = self.norm_scratch(es_tmp)
        for t in range(nt):
            self.norm_tile_to(scr, XN, bXN, gname, t, t0 + t)

    def ffn(self, l, next_tile):
        nc = self.nc
        groups = [(0, 8), (8, 7), (15, 7)]
        GMAX = 8
        with ExitStack() as es:
            XN = self.XN
            bXN = self.bXNs
            H = es.enter_context(self.sbt("ffn_h", [128, GMAX, T], BF16))
            bH = [[Buf("h%d_%d" % (j, t)) for t in range(2)] for j in range(GMAX)]
            Gb = [es.enter_context(self.sbt("ffn_g%d" % i, [128, 2 + 1024], F32)) for i in range(2)]
            A = [es.enter_context(self.sbt("ffn_a%d" % i, [128, 1024], F32)) for i in range(2)]
            bG = [Buf("g0"), Buf("g1")]
            bA = [Buf("a0"), Buf("a1")]
            self.dve.op([], [bG[0]], lambda: nc.vector.memset(Gb[0][:, 0:2], 0.0))
            wup = self.f_w_up[l].rearrange("(kc p) c -> p kc c", p=128)
            wdn = self.f_w_down[l].rearrange("(j p) c -> j p c", p=128)
            cw = lambda tap, j: self.vcol("f_conv%d%d" % (l, tap), j)
            unit = 0
            nbank = 0
            for (g0, gn) in groups:
                jj = 0
                while jj < gn:
                    nb = min(4, gn - jj)
                    j0 = g0 + jj
                    self.ring.align()
                    wg = self.ring.load(wup[:, :, j0 * 128:(j0 + nb) * 128], 128, [KC, nb * 128])
                    wu = self.ring.load(wup[:, :, DFF + j0 * 128:DFF + (j0 + nb) * 128], 128, [KC, nb * 128])
                    for jb in range(nb):
                        j = j0 + jb
                        for hh in range(2):
                            pb = (unit % 2) * 4
                            s = hh
                            unit += 1

                            def mm(w, bank0):
                                ins = None
                                for kc in range(KC):
                                    for t in range(2):
                                        ins = nc.tensor.matmul(
                                            self.PS[:, bank0 + t, :], w.ap[:, kc, jb * 128:(jb + 1) * 128],
                                            XN[:, kc, (hh * 2 + t) * 512:(hh * 2 + t + 1) * 512],
                                            start=(kc == 0), stop=(kc == KC - 1))
                                return ins
                            rxn = [bXN[kc][hh * 2 + t] for kc in range(KC) for t in range(2)]
                            self.pe.op(rxn + wg.buf, [self.bPS[pb], self.bPS[pb + 1]], lambda: mm(wg, pb))
                            self.pe.op(rxn + wu.buf, [self.bPS[pb + 2], self.bPS[pb + 3]], lambda: mm(wu, pb + 2))
                            psg = self.PS[:, pb:pb + 2, :]
                            psu = self.PS[:, pb + 2:pb + 4, :]
                            bg = [self.bPS[pb], self.bPS[pb + 1]]
                            bu = [self.bPS[pb + 2], self.bPS[pb + 3]]
                            self.act.op(bg, [bA[s]], lambda: nc.scalar.activation(
                                A[s][:], psg, AF.Copy, scale=cw(2, j)))
                            self.act.op(bg, [bG[s]], lambda: nc.scalar.copy(Gb[s][:, 2:2 + 1024], psg))
                            if hh == 1:
                                self.act.op([bG[0]], [bG[1]], lambda: nc.scalar.copy(
                                    Gb[1][:, 0:2], Gb[0][:, 1024:1026]))
                            self.dve.op([bG[s], bA[s]], [bA[s]], lambda: nc.vector.scalar_tensor_tensor(
                                A[s][:], Gb[s][:, 1:1025], cw(1, j), A[s][:], ALU.mult, ALU.add))
                            self.dve.op([bG[s], bA[s]], [bA[s]], lambda: nc.vector.scalar_tensor_tensor(
                                A[s][:], Gb[s][:, 0:1024], cw(0, j), A[s][:], ALU.mult, ALU.add))
                            self.act.op([bA[s]], [bA[s]], lambda: nc.scalar.activation(A[s][:], A[s][:], AF.Silu))
                            self.dve.op([bA[s]] + bu, [bH[jj + jb][hh]], lambda: nc.vector.tensor_tensor(
                                H[:, jj + jb, hh * 1024:(hh + 1) * 1024], A[s][:], psu, ALU.mult))
                    jj += nb
                self.ring.align()
                wd = [self.ring.load(wdn[g0 + k], 128, [D]) for k in range(gn)]
                for tt in range(4):
                    for m in range(KC):
                        b = nbank % 8
                        nbank += 1

                        def mmd():
                            ins = None
                            for k in range(gn):
                                ins = nc.tensor.matmul(self.PS[:, b, :], wd[k].ap[:, m * 128:(m + 1) * 128],
                                                       H[:, k, tt * 512:(tt + 1) * 512],
                                                       start=(k == 0), stop=(k == gn - 1))
                            return ins
                        rd = [bH[k][tt // 2] for k in range(gn)]
                        for k in range(gn):
                            rd += wd[k].buf
                        self.pe.op(rd, [self.bPS[b]], mmd)
                        self.dve.op([self.bPS[b], self.bXT[m][tt]], [self.bXT[m][tt]],
                                    lambda: nc.vector.tensor_tensor(
                                        self.XT[:, m, tt * 512:(tt + 1) * 512],
                                        self.XT[:, m, tt * 512:(tt + 1) * 512], self.PS[:, b, :], ALU.add))
                    if g0 == groups[-1][0] and tt >= 1:
                        next_tile(tt - 1)
            next_tile(3)

    def sconv(self, next_tile):
        nc = self.nc
        with ExitStack() as es:
            XN = self.XN
            bXN = self.bXNs
            Y = es.enter_context(self.sbt("sc_y", [128, KC, T], BF16))
            bY = [[Buf("y%d_%d" % (j, t)) for t in range(4)] for j in range(KC)]
            Cs = [es.enter_context(self.sbt("sc_c%d" % i, [128, 512], F32)) for i in range(2)]
            Zb = [es.enter_context(self.sbt("sc_z%d" % i, [128, 2 + 512], F32)) for i in range(2)]
            A = [es.enter_context(self.sbt("sc_a%d" % i, [128, 512], F32)) for i in range(2)]
            Bs = [es.enter_context(self.sbt("sc_b%d" % i, [128, 512], F32)) for i in range(2)]
            bB = [Buf("b0"), Buf("b1")]
            bC = [Buf("c0"), Buf("c1")]
            bZ = [Buf("z0"), Buf("z1")]
            bA = [Buf("a0"), Buf("a1")]
            win = self.b_w_in.rearrange("(kc p) c -> p kc c", p=128)
            wout = self.b_w_out.rearrange("(j p) c -> j p c", p=128)
            cw = lambda tap, j: self.vcol("b_conv%d" % tap, j)
            NB = 2
            for j0 in range(0, KC, NB):
                self.ring.align()
                ws = [self.ring.load(win[:, :, sg * D + j0 * 128: sg * D + (j0 + NB) * 128], 128, [KC, NB * 128])
                      for sg in range(3)]
                for jb in range(NB):
                    j = j0 + jb
                    for tt in range(4):
                        s = tt % 2
                        pb = s * 3

                        def mm(w, bank):
                            ins = None
                            for kc in range(KC):
                                ins = nc.tensor.matmul(self.PS[:, bank, :], w.ap[:, kc, jb * 128:(jb + 1) * 128],
                                                       XN[:, kc, tt * 512:(tt + 1) * 512],
                                                       start=(kc == 0), stop=(kc == KC - 1))
                            return ins
                        for sg in range(3):
                            self.pe.op([bXN[kc][tt] for kc in range(KC)] + ws[sg].buf, [self.bPS[pb + sg]],
                                       lambda: mm(ws[sg], pb + sg))
                        self.act.op([self.bPS[pb + 1]], [bC[s]], lambda: nc.scalar.copy(Cs[s][:], self.PS[:, pb + 1, :]))
                        self.act.op([self.bPS[pb]], [bB[s]], lambda: nc.scalar.copy(Bs[s][:], self.PS[:, pb, :]))
                        if tt == 0:
                            self.dve.op([], [bZ[s]], lambda: nc.vector.memset(Zb[s][:, 0:2], 0.0))
                        else:
                            self.act.op([bZ[1 - s]], [bZ[s]], lambda: nc.scalar.copy(
                                Zb[s][:, 0:2], Zb[1 - s][:, 512:514]))
                        self.dve.op([bC[s], self.bPS[pb + 2]], [bZ[s]], lambda: nc.vector.tensor_tensor(
                            Zb[s][:, 2:514], Cs[s][:], self.PS[:, pb + 2, :], ALU.mult))
                        self.act.op([bZ[s]], [bA[s]], lambda: nc.scalar.activation(
                            A[s][:], Zb[s][:, 2:514], AF.Copy, scale=cw(2, j)))
                        self.dve.op([bZ[s], bA[s]], [bA[s]], lambda: nc.vector.scalar_tensor_tensor(
                            A[s][:], Zb[s][:, 1:513], cw(1, j), A[s][:], ALU.mult, ALU.add))
                        self.dve.op([bZ[s], bA[s]], [bA[s]], lambda: nc.vector.scalar_tensor_tensor(
                            A[s][:], Zb[s][:, 0:512], cw(0, j), A[s][:], ALU.mult, ALU.add))
                        self.dve.op([bA[s], bB[s]], [bY[j][tt]], lambda: nc.vector.tensor_tensor(
                            Y[:, j, tt * 512:(tt + 1) * 512], A[s][:], Bs[s][:], ALU.mult))
            self.out_proj(wout, Y, lambda kc, tg: [bY[kc][tg]], range(4), lambda tg: tg, next_tile)
            next_tile(3)

    def out_proj(self, wout, Y, ybufs, tiles, gtile, next_tile=None):
        nc = self.nc
        self.ring.align()
        wo = [self.ring.load(wout[kc], 128, [D]) for kc in range(KC)]
        nb = 0
        for t in tiles:
            tg = gtile(t)
            for m in range(KC):
                b = 6 + nb % 2
                nb += 1

                def mmo():
                    ins = None
                    for kc in range(KC):
                        ins = nc.tensor.matmul(self.PS[:, b, :], wo[kc].ap[:, m * 128:(m + 1) * 128],
                                               Y[:, kc, t * 512:(t + 1) * 512],
                                               start=(kc == 0), stop=(kc == KC - 1))
                    return ins
                rd = []
                for kc in range(KC):
                    rd += ybufs(kc, t) + wo[kc].buf
                self.pe.op(rd, [self.bPS[b]], mmo)
                self.dve.op([self.bPS[b], self.bXT[m][tg]], [self.bXT[m][tg]],
                            lambda: nc.vector.tensor_tensor(
                                self.XT[:, m, tg * 512:(tg + 1) * 512],
                                self.XT[:, m, tg * 512:(tg + 1) * 512], self.PS[:, b, :], ALU.add))
            if next_tile is not None and t >= 1:
                next_tile(t - 1)

    def gla(self):
        nc = self.nc
        PS = self.PS
        bPS = self.bPS
        win = self.a_w_in.rearrange("(kc p) c -> p kc c", p=128)
        wout = self.a_w_out.rearrange("(j p) c -> j p c", p=128)
        nbank = [0]

        def nextbank():
            b = nbank[0] % 8
            nbank[0] += 1
            return b

        with ExitStack() as es:
            S = [es.enter_context(self.sbt("gl_s%d" % i, [128, HEADS, DV], F32)) for i in range(2)]
            bS = [[Buf("s") for _ in range(HEADS)] for _ in range(2)]
            WGU = es.enter_context(self.sbt("gl_wgu", [32, 512], BF16))
            bWGU = Buf("wgu")
            self.dve.op([], [bWGU], lambda: nc.vector.memset(WGU[:], 0.0))
            self.pool.dma(WGU[0:17, :], self.a_wgu, [], [bWGU])
            self.dve.op([], bS[1], lambda: nc.vector.memset(S[1][:], 0.0))
            for hf in range(2):
                with ExitStack() as esh:
                    QT = esh.enter_context(self.sbt("gl_qt", [128, HEADS, 1024], BF16))
                    bQT = [[Buf("qt") for _ in range(2)] for _ in range(HEADS)]
                    XN = esh.enter_context(self.sbt("gl_xn", [128, KC, 1024], BF16))
                    bXN = [[Buf("xn") for _ in range(2)] for _ in range(KC)]
                    KD = esh.enter_context(self.sbt("gl_kd", [128, 8, 512], BF16))
                    bKD = [Buf("kd") for _ in range(8)]
                    V = esh.enter_context(self.sbt("gl_v", [128, 8, 1024], BF16))
                    bV = [Buf("v") for _ in range(8)]
                    SR = esh.enter_context(self.sbt("gl_sr", [128, KC, 1024], BF16))
                    bSR = [[Buf("sr") for _ in range(2)] for _ in range(KC)]
                    DEC = esh.enter_context(self.sbt("gl_dec", [128, HEADS, 16], F32))
                    bDEC = [Buf("dec") for _ in range(8)]
                    with ExitStack() as esa:
                        GL = esa.enter_context(self.sbt("gl_gl", [32, 1024], BF16))
                        bGL = Buf("gl")
                        SPt = [esa.enter_context(self.sbt("gl_sp%d" % i, [128, 512], F32)) for i in range(2)]
                        E = [esa.enter_context(self.sbt("gl_e%d" % i, [128, 512], F32)) for i in range(2)]
                        bSP = [Buf("sp0"), Buf("sp1")]
                        bE = [Buf("e0"), Buf("e1")]
                        with ExitStack() as es2:
                            self.rmsnorm_to(es2, XN, bXN, "a_norm", t0=hf * 2, nt=2)
                        self.dve.op([], [bGL], lambda: nc.vector.memset(GL[:], 0.0))
                        self.dve.op([], [bGL], lambda: nc.vector.memset(GL[0:17, :], 1.0))
                        self.ring.align()
                        wgl = self.ring.load(win[:, :, 3072:3088], 128, [KC, 16])
                        for tt in range(2):
                            b = nextbank()

                            def mmg():
                                ins = None
                                for kc in range(KC):
                                    ins = nc.tensor.matmul(PS[0:16, b, :], wgl.ap[:, kc, :],
                                                           XN[:, kc, tt * 512:(tt + 1) * 512],
                                                           start=(kc == 0), stop=(kc == KC - 1))
                                return ins
                            self.pe.op([bXN[kc][tt] for kc in range(KC)] + wgl.buf, [bPS[b]], mmg)
                            self.act.op([bPS[b]], [bGL], lambda: nc.scalar.copy(
                                GL[0:16, tt * 512:(tt + 1) * 512], PS[0:16, b, :]))
                        wq = self.ring.load(win[:, :, 0:512], 128, [KC, 512])
                        for j in range(HEADS):
                            for tt in range(2):
                                b = nextbank()

                                def mmq():
                                    ins = None
                                    for kc in range(KC):
                                        ins = nc.tensor.matmul(PS[:, b, :], wq.ap[:, kc, j * 128:(j + 1) * 128],
                                                               XN[:, kc, tt * 512:(tt + 1) * 512],
                                                               start=(kc == 0), stop=(kc == KC - 1))
                                    return ins
                                self.pe.op([bXN[kc][tt] for kc in range(KC)] + wq.buf, [bPS[b]], mmq)
                                self.act.op([bPS[b]], [bQT[j][tt]], lambda: nc.scalar.activation(
                                    QT[:, j, tt * 512:(tt + 1) * 512], PS[:, b, :], AF.Copy, scale=float(DK) ** -0.5))
                        self.ring.align()
                        wk = self.ring.load(win[:, :, 512:1024], 128, [KC, 512])

                        def gate_pre(i):
                            s = i % 2
                            b1 = nextbank()
                            self.pe.op([bGL, bWGU], [bPS[b1]], lambda: nc.tensor.matmul(
                                PS[:, b1, :], GL[:, i * 128:(i + 1) * 128], WGU[:, :], start=True, stop=True))
                            self.act.op([bPS[b1]], [bSP[s]], lambda: nc.scalar.activation(
                                SPt[s][:], PS[:, b1, :], AF.Exp, scale=-1.0))
                            self.act.op([bSP[s], self.bconst], [bSP[s]], lambda: nc.scalar.activation(
                                SPt[s][:], SPt[s][:], AF.Ln, bias=self.cst[:, 1:2]))

                        gate_pre(0)
                        for i in range(8):
                            s = i % 2
                            if i + 1 < 8:
                                gate_pre(i + 1)
                            b4 = nextbank()

                            def mmk():
                                ins = None
                                for kc in range(KC):
                                    ins = nc.tensor.matmul(PS[:, b4, :], XN[:, kc, i * 128:(i + 1) * 128], wk.ap[:, kc, :],
                                                           start=(kc == 0), stop=(kc == KC - 1))
                                return ins
                            self.pe.op([bXN[kc][i // 4] for kc in range(KC)] + wk.buf, [bPS[b4]], mmk)
                            b2 = nextbank()
                            self.pe.op([bSP[s], self.bconst], [bPS[b2]], lambda: nc.tensor.matmul(
                                PS[:, b2, :], self.M1, SPt[s][:], start=True, stop=True))
                            b3 = nextbank()

                            def mmt():
                                ins = None
                                for h in range(HEADS):
                                    ins = nc.tensor.matmul(PS[:, b3, h * 2:h * 2 + 2], SPt[s][:, h * 128:(h + 1) * 128],
                                                           self.M2, start=True, stop=True)
                                return ins
                            self.pe.op([bSP[s], self.bconst], [bPS[b3]], mmt)
                            self.act.op([bPS[b2]], [bE[s]], lambda: nc.scalar.activation(E[s][:], PS[:, b2, :], AF.Exp))
                            self.act.op([bPS[b3]], [bDEC[i]], lambda: nc.scalar.activation(
                                DEC[:, :, 2 * i:2 * i + 2],
                                PS[:, b3, 0:8].rearrange("p (h c) -> p h c", c=2), AF.Exp))
                            self.dve.op([bPS[b4], bE[s]], [bKD[i]], lambda: nc.vector.tensor_tensor(
                                KD[:, i, :], PS[:, b4, :], E[s][:], ALU.mult))
                        self.ring.align()
                        for vb in range(2):
                            wv = self.ring.load(win[:, :, 1024 + vb * 512:1024 + (vb + 1) * 512], 128, [KC, 512])
                            for i in range(8):
                                b = nextbank()

                                def mmv():
                                    ins = None
                                    for kc in range(KC):
                                        ins = nc.tensor.matmul(PS[:, b, :], XN[:, kc, i * 128:(i + 1) * 128],
                                                               wv.ap[:, kc, :], start=(kc == 0), stop=(kc == KC - 1))
                                    return ins
                                self.pe.op([bXN[kc][i // 4] for kc in range(KC)] + wv.buf, [bPS[b]], mmv)
                                self.copy_on(self.evac_alt(i), V[:, i, vb * 512:(vb + 1) * 512], PS[:, b, :],
                                             [bPS[b]], [bV[i]])
                        self.ring.align()
                        for rb in range(2):
                            wr = self.ring.load(win[:, :, 2048 + rb * 512:2048 + (rb + 1) * 512], 128, [KC, 512])
                            for jb in range(4):
                                j = rb * 4 + jb
                                for tt in range(2):
                                    b = nextbank()

                                    def mmr():
                                        ins = None
                                        for kc in range(KC):
                                            ins = nc.tensor.matmul(PS[:, b, :], wr.ap[:, kc, jb * 128:(jb + 1) * 128],
                                                                   XN[:, kc, tt * 512:(tt + 1) * 512],
                                                                   start=(kc == 0), stop=(kc == KC - 1))
                                        return ins
                                    self.pe.op([bXN[kc][tt] for kc in range(KC)] + wr.buf, [bPS[b]], mmr)
                                    self.act.op([bPS[b]], [bSR[j][tt]], lambda: nc.scalar.activation(
                                        SR[:, j, tt * 512:(tt + 1) * 512], PS[:, b, :], AF.Silu))
                    self.dump("QT", QT[:], [128, HEADS, 1024], BF16, [])
                    self.dump("KD", KD[:], [128, 8, 512], BF16, [])
                    self.dump("V", V[:], [128, 8, 1024], BF16, [])
                    self.dump("DEC", DEC[:], [128, HEADS, 16], F32, [])
                    self.barrier()
                    with ExitStack() as esb:
                        TQ = 4
                        TW = TQ * CH
                        SB = [esb.enter_context(self.sbt("gl_sb%d" % i, [128, HEADS, DV], BF16)) for i in range(2)]
                        bSB = [Buf("sb0"), Buf("sb1")]
                        OT2 = [esb.enter_context(self.sbt("gl_ot%d" % i, [128, KC, TW], F32)) for i in range(2)]
                        SQh = [esb.enter_context(self.sbt("gl_sqh%d" % i, [128, 2, TW], BF16)) for i in range(2)]
                        bSQh = [Buf("sqh0"), Buf("sqh1")]
                        RS = esb.enter_context(self.sbt("gl_rs", [128, 2, TW], F32))
                        OF2 = [esb.enter_context(self.sbt("gl_of%d" % i, [128, KC, TW], BF16)) for i in range(2)]
                        TMP = [esb.enter_context(self.sbt("gl_tmp%d" % i, [128, TW], F32)) for i in range(2)]
                        bTMP = [Buf("tmp0"), Buf("tmp1")]
                        bOF2 = [[Buf("of") for _ in range(KC)] for _ in range(2)]
                        bOT2 = [[Buf("ot") for _ in range(TQ)] for _ in range(2)]
                        bRS = [Buf("rs") for _ in range(2)]

                        def emit_upd(c):
                            i = c // 2
                            part = (c % 2) * 64
                            su = (c % 2) * 2

                            def mmu():
                                ins = None
                                for h in range(HEADS):
                                    ins = nc.tensor.matmul(
                                        PS[:, su + h // 2, (h % 2) * 256:(h % 2 + 1) * 256],
                                        KD[part:part + 64, i, h * 128:(h + 1) * 128],
                                        V[part:part + 64, i, h * 256:(h + 1) * 256], start=True, stop=True)
                                return ins
                            self.pe.op([bKD[i], bV[i]], [bPS[su], bPS[su + 1]], mmu)

                        def emit_state(c):
                            i = c // 2
                            su = (c % 2) * 2
                            sn = (hf * 16 + c) % 2
                            so = 1 - sn
                            for h in range(HEADS):
                                self.dve.op([bS[so][h], bDEC[i], bPS[su + h // 2]], [bS[sn][h]],
                                            lambda: nc.vector.scalar_tensor_tensor(
                                                S[sn][:, h, :], S[so][:, h, :], DEC[:, h, c:c + 1],
                                                PS[:, su + h // 2, (h % 2) * 256:(h % 2 + 1) * 256],
                                                ALU.mult, ALU.add))
                            self.act.op(bS[sn], [bSB[c % 2]], lambda: nc.scalar.copy(SB[c % 2][:], S[sn][:]))

                        def emit_o(c):
                            tq = c // TQ
                            cl = c % TQ
                            bo = 4 + c % 2

                            def mmo():
                                ins = None
                                for j in range(KC):
                                    h, hv = j // 2, j % 2
                                    ins = nc.tensor.matmul(PS[:, bo, j * 64:(j + 1) * 64],
                                                           SB[c % 2][:, h, hv * 128:(hv + 1) * 128],
                                                           QT[:, h, c * 64:(c + 1) * 64], start=True, stop=True)
                                return ins
                            self.pe.op([bSB[c % 2]] + [bQT[h][c // 8] for h in range(HEADS)], [bPS[bo]], mmo)
                            pso = PS[:, bo, :].rearrange("p (j l) -> p j l", l=64)
                            self.act.op([bPS[bo]], [bOT2[tq % 2][cl]], lambda: nc.scalar.copy(
                                OT2[tq % 2][:, :, cl * 64:(cl + 1) * 64], pso))

                        def sq_part(tq, h):
                            ot = OT2[tq % 2]
                            s2 = h % 2
                            bn = 6 + s2
                            self.act.op(bOT2[tq % 2], [bSQh[s2]], lambda: nc.scalar.activation(
                                SQh[s2][:], ot[:, 2 * h:2 * h + 2, :], AF.Square))

                            def mmn():
                                nc.tensor.matmul(PS[:, bn, 0:TW], self.onesV[:], SQh[s2][:, 0, :], start=True, stop=False)
                                return nc.tensor.matmul(PS[:, bn, 0:TW], self.onesV[:], SQh[s2][:, 1, :],
                                                        start=False, stop=True)
                            self.pe.op([bSQh[s2], self.bconst], [bPS[bn]], mmn)

                        def rs_part(tq, h):
                            s2 = h % 2
                            bn = 6 + s2
                            self.rstd_from_psum(RS[:, s2, :], PS[:, bn, 0:TW], [bPS[bn]], bRS[s2])

                        def dve_part(tq, h):
                            ot, of = OT2[tq % 2], OF2[tq % 2]
                            for j in (2 * h, 2 * h + 1):
                                self.dve.op(bOT2[tq % 2] + [bRS[h % 2]], [bTMP[j % 2]],
                                            lambda: nc.vector.scalar_tensor_tensor(
                                                TMP[j % 2][:], ot[:, j, :], self.vcol("a_gn", j), RS[:, h % 2, :],
                                                ALU.mult, ALU.mult))
                                self.dve.op([bTMP[j % 2], bSR[j][tq // 2]], [bOF2[tq % 2][j]],
                                            lambda: nc.vector.tensor_tensor(
                                                of[:, j, :], TMP[j % 2][:], SR[:, j, tq * TW:(tq + 1) * TW], ALU.mult))

                        self.ring.align()
                        wo_once = [self.ring.load(wout[kc], 128, [D]) for kc in range(KC)]

                        def outproj_group(tq, m):
                            wo = wo_once
                            of, bof = OF2[tq % 2], bOF2[tq % 2]
                            tg = hf * 2 + tq // 2
                            c0 = tg * 512 + (tq % 2) * TW
                            b = 6 + m % 2

                            def mmo2():
                                ins = None
                                for kc in range(KC):
                                    ins = nc.tensor.matmul(PS[:, b, 0:TW], wo[kc].ap[:, m * 128:(m + 1) * 128],
                                                           of[:, kc, :], start=(kc == 0), stop=(kc == KC - 1))
                                return ins
                            rd = list(bof)
                            for kc in range(KC):
                                rd += wo[kc].buf
                            self.pe.op(rd, [bPS[b]], mmo2)
                            self.dve.op([bPS[b], self.bXT[m][tg]], [self.bXT[m][tg]],
                                        lambda: nc.vector.tensor_tensor(
                                            self.XT[:, m, c0:c0 + TW],
                                            self.XT[:, m, c0:c0 + TW], PS[:, b, 0:TW], ALU.add))

                        NST = 16
                        sched = {}

                        def at(c, fn):
                            sched.setdefault(c, []).append(fn)
                        for tq in range(4):
                            b0 = 4 * tq + 4
                            at(b0, lambda tq=tq: sq_part(tq, 0))
                            at(b0, lambda tq=tq: sq_part(tq, 1))
                            at(b0, lambda tq=tq: rs_part(tq, 0))
                            at(b0, lambda tq=tq: rs_part(tq, 1))
                            at(b0 + 1, lambda tq=tq: dve_part(tq, 0))
                            at(b0 + 1, lambda tq=tq: sq_part(tq, 2))
                            at(b0 + 1, lambda tq=tq: rs_part(tq, 2))
                            at(b0 + 2, lambda tq=tq: dve_part(tq, 1))
                            at(b0 + 2, lambda tq=tq: sq_part(tq, 3))
                            at(b0 + 2, lambda tq=tq: rs_part(tq, 3))
                            at(b0 + 3, lambda tq=tq: dve_part(tq, 2))
                            at(b0 + 3, lambda tq=tq: dve_part(tq, 3))
                        for tq in range(4):
                            b0 = 4 * tq + 4
                            for k in range(4):
                                at(b0 + 4 + k, lambda tq=tq, k=k: outproj_group(tq, 2 * k))
                                at(b0 + 4 + k, lambda tq=tq, k=k: outproj_group(tq, 2 * k + 1))
                        emit_upd(0)
                        emit_state(0)
                        emit_upd(1)
                        for c in range(NST):
                            if c + 2 < NST:
                                emit_upd(c + 2)
                            if c + 1 < NST:
                                emit_state(c + 1)
                            emit_o(c)
                            for fn in sched.get(c, []):
                                fn()
                        for c in sorted(k for k in sched if k >= NST):
                            for fn in sched[c]:
                                fn()
                    self.barrier()


def _chunkcols(v):
    v = np.asarray(v, dtype=np.float32)
    return np.ascontiguousarray(v.reshape(-1, 128).T)


def make_consts():
    c = np.zeros((128, NCONST), dtype=np.float32)
    c[:, 0:128] = np.eye(128, dtype=np.float32)
    lp = np.arange(128)[:, None]
    l = np.arange(128)[None, :]
    c[:, 128:256] = np.where((lp > l) & (lp // CH == l // CH), -1.0 / 16.0, 0.0)
    c[:, 256:258] = np.where(lp // CH == np.arange(2)[None, :], -1.0 / 16.0, 0.0)
    return c


def make_in_maps(inp, ncores=NCORES):
    vec = np.zeros((128, NV), dtype=np.float32)

    def put(name, v):
        a = _chunkcols(v)
        vec[:, VC[name]:VC[name] + a.shape[1]] = a
    put("a_norm", inp["a_norm"][0])
    put("b_norm", inp["b_norm"][0])
    put("f_norm0", inp["f_norm"][0])
    put("f_norm1", inp["f_norm"][1])
    put("final_norm", inp["final_norm"])
    put("a_gn", inp["a_gn"][0])
    for t in range(3):
        put("b_conv%d" % t, inp["b_conv"][0][t])
        put("f_conv0%d" % t, inp["f_conv"][0][t])
        put("f_conv1%d" % t, inp["f_conv"][1][t])
    wgu = np.ascontiguousarray(np.concatenate(
        [np.asarray(inp["a_w_gate_up"][0], dtype=np.float32),
         np.asarray(inp["a_b_gate"][0], dtype=np.float32)[None, :]], axis=0))
    shared = {
        "vecs": vec,
        "consts": make_consts(),
        "a_w_in": np.ascontiguousarray(inp["a_w_in"][0], dtype=np.float32),
        "a_wgu": wgu,
        "a_w_out": np.ascontiguousarray(inp["a_w_out"][0], dtype=np.float32),
        "b_w_in": np.ascontiguousarray(inp["b_w_in"][0], dtype=np.float32),
        "b_w_out": np.ascontiguousarray(inp["b_w_out"][0], dtype=np.float32),
        "f_w_up": np.ascontiguousarray(inp["f_w_up"], dtype=np.float32),
        "f_w_down": np.ascontiguousarray(inp["f_w_down"], dtype=np.float32),
    }
    x = np.asarray(inp["x"], dtype=np.float32)
    maps = []
    for c in range(ncores):
        m = dict(shared)
        m["x"] = np.ascontiguousarray(x[c])
        maps.append(m)
    return maps


ALL_STAGES = ("gla", "ffn0", "sconv", "ffn1", "fnorm")
_PROG_CACHE = {}


def get_prog(stages=ALL_STAGES):
    key = tuple(stages)
    if key not in _PROG_CACHE:
        _PROG_CACHE[key] = Prog(key)
    return _PROG_CACHE[key]


def kernel(**inputs):
    prog = get_prog(ALL_STAGES)
    in_maps = make_in_maps(inputs)
    res = run_bass_kernel_spmd(prog.nc, in_maps, core_ids=list(range(NCORES)))
    return np.stack([np.asarray(r["out"], dtype=np.float32) for r in res.results], axis=0)
```

```python
from contextlib import ExitStack

import numpy as np
import concourse.bass as bass
import concourse.mybir as mybir
from concourse.bass_utils import run_bass_kernel_spmd

F32 = mybir.dt.float32
BF16 = mybir.dt.bfloat16
AF = mybir.ActivationFunctionType
ALU = mybir.AluOpType

D = 1024
T = 2048
NCORES = 8
KC = 8
DFF = 2816
NJ = 22
EPS = 1e-6
HEADS = 4
DK = 128
DV = 256
CH = 64
PROJ_A = 3088
NSLOT = 16

VC = {}
_c = 0
for _name, _n in [("a_norm", 8), ("b_norm", 8), ("f_norm0", 8), ("f_norm1", 8), ("final_norm", 8),
                  ("a_gn", 8), ("b_conv0", 8), ("b_conv1", 8), ("b_conv2", 8),
                  ("f_conv00", 22), ("f_conv01", 22), ("f_conv02", 22),
                  ("f_conv10", 22), ("f_conv11", 22), ("f_conv12", 22)]:
    VC[_name] = _c
    _c += _n
NV = _c
NCONST = 128 + 128 + 2


class Tok:
    __slots__ = ("sem", "val")

    def __init__(self, sem, val):
        self.sem = sem
        self.val = val


class Buf:
    __slots__ = ("name", "w", "r")

    def __init__(self, name=""):
        self.name = name
        self.w = None
        self.r = {}


class Q:
    def __init__(self, nc, eng, name, is_pe=False):
        self.eng = eng
        self.sem = nc.alloc_semaphore("q_" + name)
        self.n = 0
        self.seen = {}
        self.is_pe = is_pe
        self.name = name

    def wait(self, tok):
        if tok is None:
            return
        if self.seen.get(tok.sem, 0) >= tok.val:
            return
        self.eng.wait_ge(tok.sem, tok.val)
        self.seen[tok.sem] = tok.val

    def deps(self, reads, writes):
        for b in reads:
            self.wait(b.w)
        for b in writes:
            if b.w is not None and not (self.is_pe and b.w.sem is self.sem):
                self.wait(b.w)
            for s, t in b.r.items():
                if s is self.sem:
                    continue
                self.wait(t)

    def done(self, ins, reads, writes):
        self.n += 1
        ins.then_inc(self.sem, 1)
        tok = Tok(self.sem, self.n)
        for b in reads:
            b.r[self.sem] = tok
        for b in writes:
            b.w = tok
            b.r = {}
        return tok

    def op(self, reads, writes, fn):
        self.deps(reads, writes)
        return self.done(fn(), reads, writes)


class DmaQ:
    def __init__(self, nc, eng, name, nsem):
        self.eng = eng
        self.sems = [nc.alloc_semaphore("d_%s%d" % (name, i)) for i in range(nsem)]
        self.cnt = [0] * nsem
        self.i = 0
        self.seen = {}

    def wait(self, tok):
        if tok is None:
            return
        if self.seen.get(tok.sem, 0) >= tok.val:
            return
        self.eng.wait_ge(tok.sem, tok.val)
        self.seen[tok.sem] = tok.val

    def dma(self, out, in_, reads, writes, extra_waits=()):
        for t in extra_waits:
            self.wait(t)
        for b in reads:
            self.wait(b.w)
        for b in writes:
            self.wait(b.w)
            for t in b.r.values():
                self.wait(t)
        k = self.i % len(self.sems)
        self.i += 1
        self.cnt[k] += 16
        sem = self.sems[k]
        self.eng.dma_start(out=out, in_=in_).then_inc(sem, 16)
        tok = Tok(sem, self.cnt[k])
        for b in reads:
            b.r[sem] = tok
        for b in writes:
            b.w = tok
            b.r = {}
        return tok


class WBlock:
    def __init__(self, ap, buf, slots):
        self.ap = ap
        self.buf = buf
        self.slots = slots


class Ring:
    def __init__(self, nc, dq, nslot):
        self.nc = nc
        self.dq = dq
        self.nslot = nslot
        self.t = nc.alloc_sbuf_tensor("wring", [128, nslot * 1024], BF16)
        self.bufs = [Buf("ring%d" % i) for i in range(nslot)]
        self.head = 0

    def align(self):
        half = self.nslot // 2
        if self.head % half:
            self.head = (self.head // half + 1) * half
        if self.head >= self.nslot:
            self.head = 0

    def load(self, src, nparts, shape_free):
        nel = int(np.prod(shape_free))
        ns = (nel + 1023) // 1024
        assert ns <= self.nslot
        if self.head + ns > self.nslot:
            self.head = 0
        s0 = self.head
        self.head += ns
        slots = self.bufs[s0:s0 + ns]
        flat = self.t[0:nparts, s0 * 1024: s0 * 1024 + nel]
        if len(shape_free) == 2:
            dst = flat.rearrange("p (a b) -> p a b", b=shape_free[1])
        else:
            dst = flat
        self.dq.dma(dst, src, [], slots)
        return WBlock(dst, slots, slots)


class Prog:
    def __init__(self, stages, dbg=None):
        self.stages = stages
        self.dbg = dbg
        self.dumped = set()
        nc = bass.Bass("TRN2", target_bir_lowering=False)
        self.nc = nc
        dt = nc.dram_tensor
        self.x = dt("x", [T, D], F32, kind="ExternalInput").ap()
        self.vecs_d = dt("vecs", [128, NV], F32, kind="ExternalInput").ap()
        self.consts_d = dt("consts", [128, NCONST], F32, kind="ExternalInput").ap()
        self.a_w_in = dt("a_w_in", [D, PROJ_A], F32, kind="ExternalInput").ap()
        self.a_wgu = dt("a_wgu", [17, 512], F32, kind="ExternalInput").ap()
        self.a_w_out = dt("a_w_out", [D, D], F32, kind="ExternalInput").ap()
        self.b_w_in = dt("b_w_in", [D, 3 * D], F32, kind="ExternalInput").ap()
        self.b_w_out = dt("b_w_out", [D, D], F32, kind="ExternalInput").ap()
        self.f_w_up = dt("f_w_up", [2, D, 2 * DFF], F32, kind="ExternalInput").ap()
        self.f_w_down = dt("f_w_down", [2, DFF, D], F32, kind="ExternalInput").ap()
        self.out = dt("out", [T, D], F32, kind="ExternalOutput").ap()

        self.pe = Q(nc, nc.tensor, "pe", is_pe=True)
        self.act = Q(nc, nc.scalar, "act")
        self.dve = Q(nc, nc.vector, "dve")
        self.gp = Q(nc, nc.gpsimd, "gp")
        self.sp = DmaQ(nc, nc.sync, "sp", 4)
        self.pool = DmaQ(nc, nc.gpsimd, "pool", NSLOT)
        self.ring = Ring(nc, self.pool, NSLOT)

        self.XT = nc.alloc_sbuf_tensor("XT", [128, KC, T], F32)
        self.bXT = [[Buf("XT%d_%d" % (k, t)) for t in range(4)] for k in range(KC)]
        self.vecs = nc.alloc_sbuf_tensor("vecs_sb", [128, NV], F32)
        self.consts = nc.alloc_sbuf_tensor("consts_sb", [128, NCONST], F32)
        self.onesD = nc.alloc_sbuf_tensor("onesD", [128, 128], BF16)
        self.onesV = nc.alloc_sbuf_tensor("onesV", [128, 128], BF16)
        self.bconst = Buf("const")
        self.cst = nc.alloc_sbuf_tensor("cst", [128, 2], F32)
        self.PS = nc.alloc_psum_tensor("PS", [128, 8, 512], F32)
        self.bPS = [Buf("ps%d" % i) for i in range(8)]
        self.ident = self.consts[:, 0:128]
        self.M1 = self.consts[:, 128:256]
        self.M2 = self.consts[:, 256:258]

        self.build()

    def sbt(self, name, shape, dtype):
        self._uid = getattr(self, "_uid", 0) + 1
        return self.nc.sbuf_tensor("%s_u%d" % (name, self._uid), shape, dtype)

    def dump(self, name, ap, shape, dtype, bufs):
        if not self.dbg or name in self.dumped:
            return
        self.dumped.add(name)
        d = self.nc.dram_tensor("dbg_" + name, list(shape), dtype, kind="ExternalOutput").ap()
        self.barrier()
        for q in (self.pe, self.act, self.dve):
            if q.n > 0:
                self.sp.wait(Tok(q.sem, q.n))
        t = self.sp.dma(d, ap, bufs, [])
        for q in (self.pe, self.act, self.dve):
            q.wait(t)

    def vcol(self, name, j=0):
        c = VC[name] + j
        return self.vecs[:, c:c + 1]

    def barrier(self):
        qs = [self.pe, self.act, self.dve]
        for q in qs:
            for p in qs + [self.gp]:
                if p is not q and p.n > 0:
                    q.wait(Tok(p.sem, p.n))

    def evac_alt(self, idx):
        return self.act if idx % 2 == 0 else self.dve

    def copy_on(self, q, out, in_, reads, writes):
        if q is self.act:
            return q.op(reads, writes, lambda: self.nc.scalar.copy(out, in_))
        return q.op(reads, writes, lambda: self.nc.vector.tensor_copy(out, in_))

    def build(self):
        nc = self.nc
        st = self.stages
        self.sp.dma(self.vecs[:], self.vecs_d, [], [self.bconst])
        self.sp.dma(self.consts[:], self.consts_d, [], [self.bconst])
        self.dve.op([], [self.bconst], lambda: nc.vector.memset(self.onesD[:], 1.0 / D))
        self.dve.op([], [self.bconst], lambda: nc.vector.memset(self.onesV[:], 1.0 / DV))
        self.dve.op([], [self.bconst], lambda: nc.vector.memset(self.cst[:, 0:1], EPS))
        self.dve.op([], [self.bconst], lambda: nc.vector.memset(self.cst[:, 1:2], 1.0))
        self.load_x()
        if "gla" in st:
            self.gla()
        with ExitStack() as eso:
            self.XN = eso.enter_context(self.sbt("xn_shared", [128, KC, T], BF16))
            self.bXNs = [[Buf("xn") for _ in range(4)] for _ in range(KC)]
            self.scr = self.norm_scratch(eso)
            self.ostg = [eso.enter_context(self.sbt("ostage%d" % i, [128, D], F32)) for i in range(2)]
            self.bost = [Buf("ost%d" % i) for i in range(2)]
            self.out_toks = []
            phases = [p for p in ("ffn0", "sconv", "ffn1") if p in st]
            gname = {"ffn0": "f_norm0", "sconv": "b_norm", "ffn1": "f_norm1"}
            do_fnorm = "fnorm" in st

            def norm_tile_for(ph):
                return lambda tt: self.norm_tile_to(self.scr, self.XN, self.bXNs, gname[ph], tt, tt)

            def final_tile(tt):
                if do_fnorm:
                    self.final_norm_tile(tt)
                if tt >= 1:
                    self.store_tile(tt - 1)

            if phases:
                for tt in range(4):
                    norm_tile_for(phases[0])(tt)
            for k, ph in enumerate(phases):
                nxt = norm_tile_for(phases[k + 1]) if k + 1 < len(phases) else final_tile
                if ph == "sconv":
                    self.sconv(nxt)
                else:
                    self.ffn(int(ph[-1]), nxt)
            if not phases:
                for tt in range(4):
                    final_tile(tt)
            self.store_tile(3)
            for t in self.out_toks[-4:]:
                self.sp.wait(t)
            for t in self.out_toks[-4:]:
                self.pe.wait(t)

    def load_x(self):
        nc = self.nc
        with ExitStack() as es:
            stg = [es.enter_context(self.sbt("xstage%d" % i, [128, 4, D], F32)) for i in range(2)]
            bst = [Buf("xst0"), Buf("xst1")]
            xv = self.x.rearrange("(g i p) d -> g p i d", i=4, p=128)
            n = 0
            for tg in range(4):
                s = tg % 2
                self.sp.dma(stg[s][:], xv[tg], [], [bst[s]])
                for kc in range(KC):
                    b = n % 8
                    n += 1

                    def tr():
                        ins = None
                        for i in range(4):
                            ins = nc.tensor.transpose(self.PS[:, b, i * 128:(i + 1) * 128],
                                                      stg[s][:, i, kc * 128:(kc + 1) * 128], self.ident)
                        return ins
                    self.pe.op([bst[s], self.bconst], [self.bPS[b]], tr)
                    self.copy_on(self.evac_alt(n), self.XT[:, kc, tg * 512:(tg + 1) * 512], self.PS[:, b, :],
                                 [self.bPS[b]], [self.bXT[kc][tg]])
            self.barrier()

    def norm_scratch(self, es):
        SQ = [es.enter_context(self.sbt("sq%d" % i, [128, KC, 512], BF16)) for i in range(2)]
        RST = [es.enter_context(self.sbt("rstd%d" % i, [128, 512], F32)) for i in range(2)]
        return (SQ, [Buf("sq0"), Buf("sq1")], RST, [Buf("rs0"), Buf("rs1")])

    def norm_stats_tile(self, scr, tt):
        nc = self.nc
        SQ, bSQ, RST, bRS = scr
        s = tt % 2
        c0 = tt * 512
        bank = 6 + s
        self.act.op([self.bXT[kc][tt] for kc in range(KC)], [bSQ[s]],
                    lambda: nc.scalar.activation(SQ[s][:], self.XT[:, :, c0:c0 + 512], AF.Square))

        def mm():
            ins = None
            for kc in range(KC):
                ins = nc.tensor.matmul(self.PS[:, bank, :], self.onesD[:], SQ[s][:, kc, :],
                                       start=(kc == 0), stop=(kc == KC - 1))
            return ins
        self.pe.op([bSQ[s], self.bconst], [self.bPS[bank]], mm)
        self.rstd_from_psum(RST[s][:], self.PS[:, bank, :], [self.bPS[bank]], bRS[s])
        return RST[s], bRS[s]

    def norm_tile_to(self, scr, XN, bXN, gname, t, tt):
        nc = self.nc
        rst, brs = self.norm_stats_tile(scr, tt)
        c0 = tt * 512
        for kc in range(KC):
            self.dve.op([brs, self.bXT[kc][tt]], [bXN[kc][t]],
                        lambda: nc.vector.scalar_tensor_tensor(
                            XN[:, kc, t * 512:(t + 1) * 512], self.XT[:, kc, c0:c0 + 512], self.vcol(gname, kc),
                            rst[:], ALU.mult, ALU.mult))

    def final_norm_tile(self, tt):
        nc = self.nc
        rst, brs = self.norm_stats_tile(self.scr, tt)
        c0 = tt * 512
        for kc in range(KC):
            self.dve.op([brs, self.bXT[kc][tt]], [self.bXT[kc][tt]],
                        lambda: nc.vector.scalar_tensor_tensor(
                            self.XT[:, kc, c0:c0 + 512], self.XT[:, kc, c0:c0 + 512],
                            self.vcol("final_norm", kc), rst[:], ALU.mult, ALU.mult))

    def store_tile(self, tt):
        nc = self.nc
        ov = self.out.rearrange("(i p) d -> i p d", p=128)
        for i in range(tt * 4, tt * 4 + 4):
            s = i % 2
            pb = (i % 3) * 2

            def tr():
                ins = None
                for kc in range(KC):
                    ins = nc.tensor.transpose(self.PS[:, pb + kc // 4, (kc % 4) * 128:(kc % 4 + 1) * 128],
                                              self.XT[:, kc, i * 128:(i + 1) * 128], self.ident)
                return ins
            self.pe.op([self.bXT[kc][tt] for kc in range(KC)] + [self.bconst],
                       [self.bPS[pb], self.bPS[pb + 1]], tr)
            self.copy_on(self.evac_alt(i), self.ostg[s][:], self.PS[:, pb:pb + 2, :],
                         [self.bPS[pb], self.bPS[pb + 1]], [self.bost[s]])
            self.out_toks.append(self.sp.dma(ov[i], self.ostg[s][:], [self.bost[s]], []))

    def rstd_from_psum(self, out, ps, bps, bout):
        nc = self.nc
        self.act.op(bps + [self.bconst], [bout],
                    lambda: nc.scalar.activation(out, ps, AF.Ln, bias=self.cst[:, 0:1]))
        self.act.op([bout], [bout], lambda: nc.scalar.activation(out, out, AF.Exp, scale=-0.5))

    def rmsnorm_to(self, es_tmp, XN, bXN, gname, t0=0, nt=4):
        scr = self.norm_scratch(es_tmp)
        for t in range(nt):
            self.norm_tile_to(scr, XN, bXN, gname, t, t0 + t)

    def ffn(self, l, next_tile):
        nc = self.nc
        groups = [(0, 8), (8, 7), (15, 7)]
        GMAX = 8
        with ExitStack() as es:
            XN = self.XN
            bXN = self.bXNs
            H = es.enter_context(self.sbt("ffn_h", [128, GMAX, T], BF16))
            bH = [[Buf("h%d_%d" % (j, t)) for t in range(2)] for j in range(GMAX)]
            Gb = [es.enter_context(self.sbt("ffn_g%d" % i, [128, 2 + 1024], F32)) for i in range(2)]
            A = [es.enter_context(self.sbt("ffn_a%d" % i, [128, 1024], F32)) for i in range(2)]
            bG = [Buf("g0"), Buf("g1")]
            bA = [Buf("a0"), Buf("a1")]
            self.dve.op([], [bG[0]], lambda: nc.vector.memset(Gb[0][:, 0:2], 0.0))
            wup = self.f_w_up[l].rearrange("(kc p) c -> p kc c", p=128)
            wdn = self.f_w_down[l].rearrange("(j p) c -> j p c", p=128)
            cw = lambda tap, j: self.vcol("f_conv%d%d" % (l, tap), j)
            unit = 0
            nbank = 0
            for (g0, gn) in groups:
                jj = 0
                while jj < gn:
                    nb = min(4, gn - jj)
                    j0 = g0 + jj
                    self.ring.align()
                    wg = self.ring.load(wup[:, :, j0 * 128:(j0 + nb) * 128], 128, [KC, nb * 128])
                    wu = self.ring.load(wup[:, :, DFF + j0 * 128:DFF + (j0 + nb) * 128], 128, [KC, nb * 128])
                    for jb in range(nb):
                        j = j0 + jb
                        for hh in range(2):
                            pb = (unit % 2) * 4
                            s = hh
                            unit += 1

                            def mm(w, bank0):
                                ins = None
                                for kc in range(KC):
                                    for t in range(2):
                                        ins = nc.tensor.matmul(
                                            self.PS[:, bank0 + t, :], w.ap[:, kc, jb * 128:(jb + 1) * 128],
                                            XN[:, kc, (hh * 2 + t) * 512:(hh * 2 + t + 1) * 512],
                                            start=(kc == 0), stop=(kc == KC - 1))
                                return ins
                            rxn = [bXN[kc][hh * 2 + t] for kc in range(KC) for t in range(2)]
                            self.pe.op(rxn + wg.buf, [self.bPS[pb], self.bPS[pb + 1]], lambda: mm(wg, pb))
                            self.pe.op(rxn + wu.buf, [self.bPS[pb + 2], self.bPS[pb + 3]], lambda: mm(wu, pb + 2))
                            psg = self.PS[:, pb:pb + 2, :]
                            psu = self.PS[:, pb + 2:pb + 4, :]
                            bg = [self.bPS[pb], self.bPS[pb + 1]]
                            bu = [self.bPS[pb + 2], self.bPS[pb + 3]]
                            self.act.op(bg, [bA[s]], lambda: nc.scalar.activation(
                                A[s][:], psg, AF.Copy, scale=cw(2, j)))
                            self.act.op(bg, [bG[s]], lambda: nc.scalar.copy(Gb[s][:, 2:2 + 1024], psg))
                            if hh == 1:
                                self.act.op([bG[0]], [bG[1]], lambda: nc.scalar.copy(
                                    Gb[1][:, 0:2], Gb[0][:, 1024:1026]))
                            self.dve.op([bG[s], bA[s]], [bA[s]], lambda: nc.vector.scalar_tensor_tensor(
                                A[s][:], Gb[s][:, 1:1025], cw(1, j), A[s][:], ALU.mult, ALU.add))
                            self.dve.op([bG[s], bA[s]], [bA[s]], lambda: nc.vector.scalar_tensor_tensor(
                                A[s][:], Gb[s][:, 0:1024], cw(0, j), A[s][:], ALU.mult, ALU.add))
                            self.act.op([bA[s]], [bA[s]], lambda: nc.scalar.activation(A[s][:], A[s][:], AF.Silu))
                            self.dve.op([bA[s]] + bu, [bH[jj + jb][hh]], lambda: nc.vector.tensor_tensor(
                                H[:, jj + jb, hh * 1024:(hh + 1) * 1024], A[s][:], psu, ALU.mult))
                    jj += nb
                self.ring.align()
                wd = [self.ring.load(wdn[g0 + k], 128, [D]) for k in range(gn)]
                for tt in range(4):
                    for m in range(KC):
                        b = nbank % 8
                        nbank += 1

                        def mmd():
                            ins = None
                            for k in range(gn):
                                ins = nc.tensor.matmul(self.PS[:, b, :], wd[k].ap[:, m * 128:(m + 1) * 128],
                                                       H[:, k, tt * 512:(tt + 1) * 512],
                                                       start=(k == 0), stop=(k == gn - 1))
                            return ins
                        rd = [bH[k][tt // 2] for k in range(gn)]
                        for k in range(gn):
                            rd += wd[k].buf
                        self.pe.op(rd, [self.bPS[b]], mmd)
                        self.dve.op([self.bPS[b], self.bXT[m][tt]], [self.bXT[m][tt]],
                                    lambda: nc.vector.tensor_tensor(
                                        self.XT[:, m, tt * 512:(tt + 1) * 512],
                                        self.XT[:, m, tt * 512:(tt + 1) * 512], self.PS[:, b, :], ALU.add))
                    if g0 == groups[-1][0] and tt >= 1:
                        next_tile(tt - 1)
            next_tile(3)

    def sconv(self, next_tile):
        nc = self.nc
        with ExitStack() as es:
            XN = self.XN
            bXN = self.bXNs
            Y = es.enter_context(self.sbt("sc_y", [128, KC, T], BF16))
            bY = [[Buf("y%d_%d" % (j, t)) for t in range(4)] for j in range(KC)]
            Cs = [es.enter_context(self.sbt("sc_c%d" % i, [128, 512], F32)) for i in range(2)]
            Zb = [es.enter_context(self.sbt("sc_z%d" % i, [128, 2 + 512], F32)) for i in range(2)]
            A = [es.enter_context(self.sbt("sc_a%d" % i, [128, 512], F32)) for i in range(2)]
            Bs = [es.enter_context(self.sbt("sc_b%d" % i, [128, 512], F32)) for i in range(2)]
            bB = [Buf("b0"), Buf("b1")]
            bC = [Buf("c0"), Buf("c1")]
            bZ = [Buf("z0"), Buf("z1")]
            bA = [Buf("a0"), Buf("a1")]
            win = self.b_w_in.rearrange("(kc p) c -> p kc c", p=128)
            wout = self.b_w_out.rearrange("(j p) c -> j p c", p=128)
            cw = lambda tap, j: self.vcol("b_conv%d" % tap, j)
            NB = 2
            for j0 in range(0, KC, NB):
                self.ring.align()
                ws = [self.ring.load(win[:, :, sg * D + j0 * 128: sg * D + (j0 + NB) * 128], 128, [KC, NB * 128])
                      for sg in range(3)]
                for jb in range(NB):
                    j = j0 + jb
                    for tt in range(4):
                        s = tt % 2
                        pb = s * 3

                        def mm(w, bank):
                            ins = None
                            for kc in range(KC):
                                ins = nc.tensor.matmul(self.PS[:, bank, :], w.ap[:, kc, jb * 128:(jb + 1) * 128],
                                                       XN[:, kc, tt * 512:(tt + 1) * 512],
                                                       start=(kc == 0), stop=(kc == KC - 1))
                            return ins
                        for sg in range(3):
                            self.pe.op([bXN[kc][tt] for kc in range(KC)] + ws[sg].buf, [self.bPS[pb + sg]],
                                       lambda: mm(ws[sg], pb + sg))
                        self.act.op([self.bPS[pb + 1]], [bC[s]], lambda: nc.scalar.copy(Cs[s][:], self.PS[:, pb + 1, :]))
                        self.act.op([self.bPS[pb]], [bB[s]], lambda: nc.scalar.copy(Bs[s][:], self.PS[:, pb, :]))
                        if tt == 0:
                            self.dve.op([], [bZ[s]], lambda: nc.vector.memset(Zb[s][:, 0:2], 0.0))
                        else:
                            self.act.op([bZ[1 - s]], [bZ[s]], lambda: nc.scalar.copy(
                                Zb[s][:, 0:2], Zb[1 - s][:, 512:514]))
                        self.dve.op([bC[s], self.bPS[pb + 2]], [bZ[s]], lambda: nc.vector.tensor_tensor(
                            Zb[s][:, 2:514], Cs[s][:], self.PS[:, pb + 2, :], ALU.mult))
                        self.act.op([bZ[s]], [bA[s]], lambda: nc.scalar.activation(
                            A[s][:], Zb[s][:, 2:514], AF.Copy, scale=cw(2, j)))
                        self.dve.op([bZ[s], bA[s]], [bA[s]], lambda: nc.vector.scalar_tensor_tensor(
                            A[s][:], Zb[s][:, 1:513], cw(1, j), A[s][:], ALU.mult, ALU.add))
                        self.dve.op([bZ[s], bA[s]], [bA[s]], lambda: nc.vector.scalar_tensor_tensor(
                            A[s][:], Zb[s][:, 0:512], cw(0, j), A[s][:], ALU.mult, ALU.add))
                        self.dve.op([bA[s], bB[s]], [bY[j][tt]], lambda: nc.vector.tensor_tensor(
                            Y[:, j, tt * 512:(tt + 1) * 512], A[s][:], Bs[s][:], ALU.mult))
            self.out_proj(wout, Y, lambda kc, tg: [bY[kc][tg]], range(4), lambda tg: tg, next_tile)
            next_tile(3)

    def out_proj(self, wout, Y, ybufs, tiles, gtile, next_tile=None):
        nc = self.nc
        self.ring.align()
        wo = [self.ring.load(wout[kc], 128, [D]) for kc in range(KC)]
        nb = 0
        for t in tiles:
            tg = gtile(t)
            for m in range(KC):
                b = 6 + nb % 2
                nb += 1

                def mmo():
                    ins = None
                    for kc in range(KC):
                        ins = nc.tensor.matmul(self.PS[:, b, :], wo[kc].ap[:, m * 128:(m + 1) * 128],
                                               Y[:, kc, t * 512:(t + 1) * 512],
                                               start=(kc == 0), stop=(kc == KC - 1))
                    return ins
                rd = []
                for kc in range(KC):
                    rd += ybufs(kc, t) + wo[kc].buf
                self.pe.op(rd, [self.bPS[b]], mmo)
                self.dve.op([self.bPS[b], self.bXT[m][tg]], [self.bXT[m][tg]],
                            lambda: nc.vector.tensor_tensor(
                                self.XT[:, m, tg * 512:(tg + 1) * 512],
                                self.XT[:, m, tg * 512:(tg + 1) * 512], self.PS[:, b, :], ALU.add))
            if next_tile is not None and t >= 1:
                next_tile(t - 1)

    def gla(self):
        nc = self.nc
        PS = self.PS
        bPS = self.bPS
        win = self.a_w_in.rearrange("(kc p) c -> p kc c", p=128)
        wout = self.a_w_out.rearrange("(j p) c -> j p c", p=128)
        nbank = [0]

        def nextbank():
            b = nbank[0] % 8
            nbank[0] += 1
            return b

        with ExitStack() as es:
            S = [es.enter_context(self.sbt("gl_s%d" % i, [128, HEADS, DV], F32)) for i in range(2)]
            bS = [[Buf("s") for _ in range(HEADS)] for _ in range(2)]
            WGU = es.enter_context(self.sbt("gl_wgu", [32, 512], BF16))
            bWGU = Buf("wgu")
            self.dve.op([], [bWGU], lambda: nc.vector.memset(WGU[:], 0.0))
            self.pool.dma(WGU[0:17, :], self.a_wgu, [], [bWGU])
            self.dve.op([], bS[1], lambda: nc.vector.memset(S[1][:], 0.0))
            for hf in range(2):
                with ExitStack() as esh:
                    QT = esh.enter_context(self.sbt("gl_qt", [128, HEADS, 1024], BF16))
                    bQT = [[Buf("qt") for _ in range(2)] for _ in range(HEADS)]
                    GL = esh.enter_context(self.sbt("gl_gl", [32, 1024], BF16))
                    bGL = Buf("gl")
                    KD = esh.enter_context(self.sbt("gl_kd", [128, 8, 512], BF16))
                    bKD = [Buf("kd") for _ in range(8)]
                    V = esh.enter_context(self.sbt("gl_v", [128, 8, 1024], BF16))
                    bV = [Buf("v") for _ in range(8)]
                    SR = esh.enter_context(self.sbt("gl_sr", [128, KC, 1024], BF16))
                    bSR = [[Buf("sr") for _ in range(2)] for _ in range(KC)]
                    DEC = esh.enter_context(self.sbt("gl_dec", [128, HEADS, 16], F32))
                    bDEC = [Buf("dec") for _ in range(8)]
                    with ExitStack() as esa:
                        XN = esa.enter_context(self.sbt("gl_xn", [128, KC, 1024], BF16))
                        bXN = [[Buf("xn") for _ in range(2)] for _ in range(KC)]
                        SPt = [esa.enter_context(self.sbt("gl_sp%d" % i, [128, 512], F32)) for i in range(2)]
                        E = [esa.enter_context(self.sbt("gl_e%d" % i, [128, 512], F32)) for i in range(2)]
                        bSP = [Buf("sp0"), Buf("sp1")]
                        bE = [Buf("e0"), Buf("e1")]
                        with ExitStack() as es2:
                            self.rmsnorm_to(es2, XN, bXN, "a_norm", t0=hf * 2, nt=2)
                        self.dve.op([], [bGL], lambda: nc.vector.memset(GL[:], 0.0))
                        self.dve.op([], [bGL], lambda: nc.vector.memset(GL[0:17, :], 1.0))
                        self.ring.align()
                        wgl = self.ring.load(win[:, :, 3072:3088], 128, [KC, 16])
                        for tt in range(2):
                            b = nextbank()

                            def mmg():
                                ins = None
                                for kc in range(KC):
                                    ins = nc.tensor.matmul(PS[0:16, b, :], wgl.ap[:, kc, :],
                                                           XN[:, kc, tt * 512:(tt + 1) * 512],
                                                           start=(kc == 0), stop=(kc == KC - 1))
                                return ins
                            self.pe.op([bXN[kc][tt] for kc in range(KC)] + wgl.buf, [bPS[b]], mmg)
                            self.act.op([bPS[b]], [bGL], lambda: nc.scalar.copy(
                                GL[0:16, tt * 512:(tt + 1) * 512], PS[0:16, b, :]))
                        wq = self.ring.load(win[:, :, 0:512], 128, [KC, 512])
                        for j in range(HEADS):
                            for tt in range(2):
                                b = nextbank()

                                def mmq():
                                    ins = None
                                    for kc in range(KC):
                                        ins = nc.tensor.matmul(PS[:, b, :], wq.ap[:, kc, j * 128:(j + 1) * 128],
                                                               XN[:, kc, tt * 512:(tt + 1) * 512],
                                                               start=(kc == 0), stop=(kc == KC - 1))
                                    return ins
                                self.pe.op([bXN[kc][tt] for kc in range(KC)] + wq.buf, [bPS[b]], mmq)
                                self.act.op([bPS[b]], [bQT[j][tt]], lambda: nc.scalar.activation(
                                    QT[:, j, tt * 512:(tt + 1) * 512], PS[:, b, :], AF.Copy, scale=float(DK) ** -0.5))
                        self.ring.align()
                        wk = self.ring.load(win[:, :, 512:1024], 128, [KC, 512])

                        def gate_pre(i):
                            s = i % 2
                            b1 = nextbank()
                            self.pe.op([bGL, bWGU], [bPS[b1]], lambda: nc.tensor.matmul(
                                PS[:, b1, :], GL[:, i * 128:(i + 1) * 128], WGU[:, :], start=True, stop=True))
                            self.act.op([bPS[b1]], [bSP[s]], lambda: nc.scalar.activation(
                                SPt[s][:], PS[:, b1, :], AF.Exp, scale=-1.0))
                            self.act.op([bSP[s], self.bconst], [bSP[s]], lambda: nc.scalar.activation(
                                SPt[s][:], SPt[s][:], AF.Ln, bias=self.cst[:, 1:2]))

                        gate_pre(0)
                        for i in range(8):
                            s = i % 2
                            if i + 1 < 8:
                                gate_pre(i + 1)
                            b4 = nextbank()

                            def mmk():
                                ins = None
                                for kc in range(KC):
                                    ins = nc.tensor.matmul(PS[:, b4, :], XN[:, kc, i * 128:(i + 1) * 128], wk.ap[:, kc, :],
                                                           start=(kc == 0), stop=(kc == KC - 1))
                                return ins
                            self.pe.op([bXN[kc][i // 4] for kc in range(KC)] + wk.buf, [bPS[b4]], mmk)
                            b2 = nextbank()
                            self.pe.op([bSP[s], self.bconst], [bPS[b2]], lambda: nc.tensor.matmul(
                                PS[:, b2, :], self.M1, SPt[s][:], start=True, stop=True))
                            b3 = nextbank()

                            def mmt():
                                ins = None
                                for h in range(HEADS):
                                    ins = nc.tensor.matmul(PS[:, b3, h * 2:h * 2 + 2], SPt[s][:, h * 128:(h + 1) * 128],
                                                           self.M2, start=True, stop=True)
                                return ins
                            self.pe.op([bSP[s], self.bconst], [bPS[b3]], mmt)
                            self.act.op([bPS[b2]], [bE[s]], lambda: nc.scalar.activation(E[s][:], PS[:, b2, :], AF.Exp))
                            self.act.op([bPS[b3]], [bDEC[i]], lambda: nc.scalar.activation(
                                DEC[:, :, 2 * i:2 * i + 2],
                                PS[:, b3, 0:8].rearrange("p (h c) -> p h c", c=2), AF.Exp))
                            self.dve.op([bPS[b4], bE[s]], [bKD[i]], lambda: nc.vector.tensor_tensor(
                                KD[:, i, :], PS[:, b4, :], E[s][:], ALU.mult))
                        self.ring.align()
                        for vb in range(2):
                            wv = self.ring.load(win[:, :, 1024 + vb * 512:1024 + (vb + 1) * 512], 128, [KC, 512])
                            for i in range(8):
                                b = nextbank()

                                def mmv():
                                    ins = None
                                    for kc in range(KC):
                                        ins = nc.tensor.matmul(PS[:, b, :], XN[:, kc, i * 128:(i + 1) * 128],
                                                               wv.ap[:, kc, :], start=(kc == 0), stop=(kc == KC - 1))
                                    return ins
                                self.pe.op([bXN[kc][i // 4] for kc in range(KC)] + wv.buf, [bPS[b]], mmv)
                                self.copy_on(self.evac_alt(i), V[:, i, vb * 512:(vb + 1) * 512], PS[:, b, :],
                                             [bPS[b]], [bV[i]])
                        self.ring.align()
                        for rb in range(2):
                            wr = self.ring.load(win[:, :, 2048 + rb * 512:2048 + (rb + 1) * 512], 128, [KC, 512])
                            for jb in range(4):
                                j = rb * 4 + jb
                                for tt in range(2):
                                    b = nextbank()

                                    def mmr():
                                        ins = None
                                        for kc in range(KC):
                                            ins = nc.tensor.matmul(PS[:, b, :], wr.ap[:, kc, jb * 128:(jb + 1) * 128],
                                                                   XN[:, kc, tt * 512:(tt + 1) * 512],
                                                                   start=(kc == 0), stop=(kc == KC - 1))
                                        return ins
                                    self.pe.op([bXN[kc][tt] for kc in range(KC)] + wr.buf, [bPS[b]], mmr)
                                    self.act.op([bPS[b]], [bSR[j][tt]], lambda: nc.scalar.activation(
                                        SR[:, j, tt * 512:(tt + 1) * 512], PS[:, b, :], AF.Silu))
                    self.dump("GL", GL[:], [32, 1024], BF16, [])
                    self.dump("QT", QT[:], [128, HEADS, 1024], BF16, [])
                    self.dump("KD", KD[:], [128, 8, 512], BF16, [])
                    self.dump("V", V[:], [128, 8, 1024], BF16, [])
                    self.dump("SR", SR[:], [128, KC, 1024], BF16, [])
                    self.dump("DEC", DEC[:], [128, HEADS, 16], F32, [])
                    self.barrier()
                    with ExitStack() as esb:
                        TQ = 4
                        TW = TQ * CH
                        SB = [esb.enter_context(self.sbt("gl_sb%d" % i, [128, HEADS, DV], BF16)) for i in range(2)]
                        bSB = [Buf("sb0"), Buf("sb1")]
                        OT2 = [esb.enter_context(self.sbt("gl_ot%d" % i, [128, KC, TW], F32)) for i in range(2)]
                        OSQ2 = [esb.enter_context(self.sbt("gl_osq%d" % i, [128, KC, TW], BF16)) for i in range(2)]
                        RS = esb.enter_context(self.sbt("gl_rs", [128, HEADS, TW], F32))
                        OF2 = [esb.enter_context(self.sbt("gl_of%d" % i, [128, KC, TW], BF16)) for i in range(2)]
                        TMP = [esb.enter_context(self.sbt("gl_tmp%d" % i, [128, TW], F32)) for i in range(2)]
                        bTMP = [Buf("tmp0"), Buf("tmp1")]
                        bOF2 = [[Buf("of") for _ in range(KC)] for _ in range(2)]
                        bOT2 = [[Buf("ot") for _ in range(TQ)] for _ in range(2)]
                        bOSQ2 = [[Buf("osq") for _ in range(TQ)] for _ in range(2)]
                        bRS = [Buf("rs") for _ in range(HEADS)]

                        def emit_upd(c):
                            i = c // 2
                            part = (c % 2) * 64
                            su = (c % 2) * 2

                            def mmu():
                                ins = None
                                for h in range(HEADS):
                                    ins = nc.tensor.matmul(
                                        PS[:, su + h // 2, (h % 2) * 256:(h % 2 + 1) * 256],
                                        KD[part:part + 64, i, h * 128:(h + 1) * 128],
                                        V[part:part + 64, i, h * 256:(h + 1) * 256], start=True, stop=True)
                                return ins
                            self.pe.op([bKD[i], bV[i]], [bPS[su], bPS[su + 1]], mmu)

                        def emit_state(c):
                            i = c // 2
                            su = (c % 2) * 2
                            sn = (hf * 16 + c) % 2
                            so = 1 - sn
                            for h in range(HEADS):
                                self.dve.op([bS[so][h], bDEC[i], bPS[su + h // 2]], [bS[sn][h]],
                                            lambda: nc.vector.scalar_tensor_tensor(
                                                S[sn][:, h, :], S[so][:, h, :], DEC[:, h, c:c + 1],
                                                PS[:, su + h // 2, (h % 2) * 256:(h % 2 + 1) * 256],
                                                ALU.mult, ALU.add))
                            self.act.op(bS[sn], [bSB[c % 2]], lambda: nc.scalar.copy(SB[c % 2][:], S[sn][:]))

                        def emit_o(c):
                            tq = c // TQ
                            cl = c % TQ
                            bo = 4 + c % 2

                            def mmo():
                                ins = None
                                for j in range(KC):
                                    h, hv = j // 2, j % 2
                                    ins = nc.tensor.matmul(PS[:, bo, j * 64:(j + 1) * 64],
                                                           SB[c % 2][:, h, hv * 128:(hv + 1) * 128],
                                                           QT[:, h, c * 64:(c + 1) * 64], start=True, stop=True)
                                return ins
                            self.pe.op([bSB[c % 2]] + [bQT[h][c // 8] for h in range(HEADS)], [bPS[bo]], mmo)
                            pso = PS[:, bo, :].rearrange("p (j l) -> p j l", l=64)
                            self.act.op([bPS[bo]], [bOT2[tq % 2][cl]], lambda: nc.scalar.copy(
                                OT2[tq % 2][:, :, cl * 64:(cl + 1) * 64], pso))
                            self.act.op([bPS[bo]], [bOSQ2[tq % 2][cl]], lambda: nc.scalar.activation(
                                OSQ2[tq % 2][:, :, cl * 64:(cl + 1) * 64], pso, AF.Square))

                        def emit_post(tq):
                            ot, osq, of = OT2[tq % 2], OSQ2[tq % 2], OF2[tq % 2]
                            bot, bosq, bof = bOT2[tq % 2], bOSQ2[tq % 2], bOF2[tq % 2]
                            for h in range(HEADS):
                                bn = 6 + h % 2

                                def mmn():
                                    nc.tensor.matmul(PS[:, bn, 0:TW], self.onesV[:], osq[:, 2 * h, :], start=True, stop=False)
                                    return nc.tensor.matmul(PS[:, bn, 0:TW], self.onesV[:], osq[:, 2 * h + 1, :],
                                                            start=False, stop=True)
                                self.pe.op(bosq + [self.bconst], [bPS[bn]], mmn)
                                self.rstd_from_psum(RS[:, h, :], PS[:, bn, 0:TW], [bPS[bn]], bRS[h])
                            self.ring.align()
                            wo = [self.ring.load(wout[kc], 128, [D]) for kc in range(KC)]
                            for j in range(KC):
                                self.dve.op(bot + [bRS[j // 2]], [bTMP[j % 2]], lambda: nc.vector.scalar_tensor_tensor(
                                    TMP[j % 2][:], ot[:, j, :], self.vcol("a_gn", j), RS[:, j // 2, :],
                                    ALU.mult, ALU.mult))
                                self.dve.op([bTMP[j % 2], bSR[j][tq // 2]], [bof[j]], lambda: nc.vector.tensor_tensor(
                                    of[:, j, :], TMP[j % 2][:], SR[:, j, tq * TW:(tq + 1) * TW], ALU.mult))
                            tg = hf * 2 + tq // 2
                            c0 = tg * 512 + (tq % 2) * TW
                            groups = []
                            for m in range(KC):
                                def grp(m=m):
                                    b = 6 + m % 2

                                    def mmo2():
                                        ins = None
                                        for kc in range(KC):
                                            ins = nc.tensor.matmul(PS[:, b, 0:TW], wo[kc].ap[:, m * 128:(m + 1) * 128],
                                                                   of[:, kc, :], start=(kc == 0), stop=(kc == KC - 1))
                                        return ins
                                    rd = list(bof)
                                    for kc in range(KC):
                                        rd += wo[kc].buf
                                    self.pe.op(rd, [bPS[b]], mmo2)
                                    self.dve.op([bPS[b], self.bXT[m][tg]], [self.bXT[m][tg]],
                                                lambda: nc.vector.tensor_tensor(
                                                    self.XT[:, m, c0:c0 + TW],
                                                    self.XT[:, m, c0:c0 + TW], PS[:, b, 0:TW], ALU.add))
                                groups.append(grp)
                            return groups

                        pending = []
                        emit_upd(0)
                        emit_state(0)
                        emit_upd(1)
                        for c in range(16):
                            if c + 2 < 16:
                                emit_upd(c + 2)
                            if c + 1 < 16:
                                emit_state(c + 1)
                            emit_o(c)
                            for _ in range(2):
                                if pending:
                                    pending.pop(0)()
                            if c % TQ == TQ - 1:
                                while pending:
                                    pending.pop(0)()
                                pending = emit_post(c // TQ)
                        while pending:
                            pending.pop(0)()
                    self.barrier()


def _chunkcols(v):
    v = np.asarray(v, dtype=np.float32)
    return np.ascontiguousarray(v.reshape(-1, 128).T)


def make_consts():
    c = np.zeros((128, NCONST), dtype=np.float32)
    c[:, 0:128] = np.eye(128, dtype=np.float32)
    lp = np.arange(128)[:, None]
    l = np.arange(128)[None, :]
    c[:, 128:256] = np.where((lp > l) & (lp // CH == l // CH), -1.0 / 16.0, 0.0)
    c[:, 256:258] = np.where(lp // CH == np.arange(2)[None, :], -1.0 / 16.0, 0.0)
    return c


def make_in_maps(inp, ncores=NCORES):
    vec = np.zeros((128, NV), dtype=np.float32)

    def put(name, v):
        a = _chunkcols(v)
        vec[:, VC[name]:VC[name] + a.shape[1]] = a
    put("a_norm", inp["a_norm"][0])
    put("b_norm", inp["b_norm"][0])
    put("f_norm0", inp["f_norm"][0])
    put("f_norm1", inp["f_norm"][1])
    put("final_norm", inp["final_norm"])
    put("a_gn", inp["a_gn"][0])
    for t in range(3):
        put("b_conv%d" % t, inp["b_conv"][0][t])
        put("f_conv0%d" % t, inp["f_conv"][0][t])
        put("f_conv1%d" % t, inp["f_conv"][1][t])
    wgu = np.ascontiguousarray(np.concatenate(
        [np.asarray(inp["a_w_gate_up"][0], dtype=np.float32),
         np.asarray(inp["a_b_gate"][0], dtype=np.float32)[None, :]], axis=0))
    shared = {
        "vecs": vec,
        "consts": make_consts(),
        "a_w_in": np.ascontiguousarray(inp["a_w_in"][0], dtype=np.float32),
        "a_wgu": wgu,
        "a_w_out": np.ascontiguousarray(inp["a_w_out"][0], dtype=np.float32),
        "b_w_in": np.ascontiguousarray(inp["b_w_in"][0], dtype=np.float32),
        "b_w_out": np.ascontiguousarray(inp["b_w_out"][0], dtype=np.float32),
        "f_w_up": np.ascontiguousarray(inp["f_w_up"], dtype=np.float32),
        "f_w_down": np.ascontiguousarray(inp["f_w_down"], dtype=np.float32),
    }
    x = np.asarray(inp["x"], dtype=np.float32)
    maps = []
    for c in range(ncores):
        m = dict(shared)
        m["x"] = np.ascontiguousarray(x[c])
        maps.append(m)
    return maps


ALL_STAGES = ("gla", "ffn0", "sconv", "ffn1", "fnorm")
_PROG_CACHE = {}


def get_prog(stages=ALL_STAGES):
    key = tuple(stages)
    if key not in _PROG_CACHE:
        _PROG_CACHE[key] = Prog(key)
    return _PROG_CACHE[key]


def kernel(**inputs):
    prog = get_prog(ALL_STAGES)
    in_maps = make_in_maps(inputs)
    res = run_bass_kernel_spmd(prog.nc, in_maps, core_ids=list(range(NCORES)))
    return np.stack([np.asarray(r["out"], dtype=np.float32) for r in res.results], axis=0)
```

```python
from contextlib import ExitStack

import numpy as np
import concourse.bass as bass
import concourse.mybir as mybir
from concourse.bass_utils import run_bass_kernel_spmd

F32 = mybir.dt.float32
BF16 = mybir.dt.bfloat16
AF = mybir.ActivationFunctionType
ALU = mybir.AluOpType

D = 1024
T = 2048
NCORES = 8
KC = 8
DFF = 2816
NJ = 22
EPS = 1e-6
HEADS = 4
DK = 128
DV = 256
CH = 64
PROJ_A = 3088
NSLOT = 16

VC = {}
_c = 0
for _name, _n in [("a_norm", 8), ("b_norm", 8), ("f_norm0", 8), ("f_norm1", 8), ("final_norm", 8),
                  ("a_gn", 8), ("b_conv0", 8), ("b_conv1", 8), ("b_conv2", 8),
                  ("f_conv00", 22), ("f_conv01", 22), ("f_conv02", 22),
                  ("f_conv10", 22), ("f_conv11", 22), ("f_conv12", 22)]:
    VC[_name] = _c
    _c += _n
NV = _c
NCONST = 128 + 128 + 2


class Tok:
    __slots__ = ("sem", "val")

    def __init__(self, sem, val):
        self.sem = sem
        self.val = val


class Buf:
    __slots__ = ("name", "w", "r")

    def __init__(self, name=""):
        self.name = name
        self.w = None
        self.r = {}


class Q:
    def __init__(self, nc, eng, name, is_pe=False):
        self.eng = eng
        self.sem = nc.alloc_semaphore("q_" + name)
        self.n = 0
        self.seen = {}
        self.is_pe = is_pe
        self.name = name

    def wait(self, tok):
        if tok is None:
            return
        if self.seen.get(tok.sem, 0) >= tok.val:
            return
        self.eng.wait_ge(tok.sem, tok.val)
        self.seen[tok.sem] = tok.val

    def deps(self, reads, writes):
        for b in reads:
            self.wait(b.w)
        for b in writes:
            if b.w is not None and not (self.is_pe and b.w.sem is self.sem):
                self.wait(b.w)
            for s, t in b.r.items():
                if s is self.sem:
                    continue
                self.wait(t)

    def done(self, ins, reads, writes):
        self.n += 1
        ins.then_inc(self.sem, 1)
        tok = Tok(self.sem, self.n)
        for b in reads:
            b.r[self.sem] = tok
        for b in writes:
            b.w = tok
            b.r = {}
        return tok

    def op(self, reads, writes, fn):
        self.deps(reads, writes)
        return self.done(fn(), reads, writes)


class DmaQ:
    def __init__(self, nc, eng, name, nsem):
        self.eng = eng
        self.sems = [nc.alloc_semaphore("d_%s%d" % (name, i)) for i in range(nsem)]
        self.cnt = [0] * nsem
        self.i = 0
        self.seen = {}

    def wait(self, tok):
        if tok is None:
            return
        if self.seen.get(tok.sem, 0) >= tok.val:
            return
        self.eng.wait_ge(tok.sem, tok.val)
        self.seen[tok.sem] = tok.val

    def dma(self, out, in_, reads, writes, extra_waits=()):
        for t in extra_waits:
            self.wait(t)
        for b in reads:
            self.wait(b.w)
        for b in writes:
            self.wait(b.w)
            for t in b.r.values():
                self.wait(t)
        k = self.i % len(self.sems)
        self.i += 1
        self.cnt[k] += 16
        sem = self.sems[k]
        self.eng.dma_start(out=out, in_=in_).then_inc(sem, 16)
        tok = Tok(sem, self.cnt[k])
        for b in reads:
            b.r[sem] = tok
        for b in writes:
            b.w = tok
            b.r = {}
        return tok


class WBlock:
    def __init__(self, ap, buf, slots):
        self.ap = ap
        self.buf = buf
        self.slots = slots


class Ring:
    def __init__(self, nc, dq, nslot):
        self.nc = nc
        self.dq = dq
        self.nslot = nslot
        self.t = nc.alloc_sbuf_tensor("wring", [128, nslot * 1024], BF16)
        self.bufs = [Buf("ring%d" % i) for i in range(nslot)]
        self.head = 0

    def align(self):
        half = self.nslot // 2
        if self.head % half:
            self.head = (self.head // half + 1) * half
        if self.head >= self.nslot:
            self.head = 0

    def load(self, src, nparts, shape_free):
        nel = int(np.prod(shape_free))
        ns = (nel + 1023) // 1024
        assert ns <= self.nslot
        if self.head + ns > self.nslot:
            self.head = 0
        s0 = self.head
        self.head += ns
        slots = self.bufs[s0:s0 + ns]
        flat = self.t[0:nparts, s0 * 1024: s0 * 1024 + nel]
        if len(shape_free) == 2:
            dst = flat.rearrange("p (a b) -> p a b", b=shape_free[1])
        else:
            dst = flat
        self.dq.dma(dst, src, [], slots)
        return WBlock(dst, slots, slots)


class Prog:
    def __init__(self, stages, dbg=None):
        self.stages = stages
        self.dbg = dbg
        self.dumped = set()
        nc = bass.Bass("TRN2", target_bir_lowering=False)
        self.nc = nc
        dt = nc.dram_tensor
        self.x = dt("x", [T, D], F32, kind="ExternalInput").ap()
        self.vecs_d = dt("vecs", [128, NV], F32, kind="ExternalInput").ap()
        self.consts_d = dt("consts", [128, NCONST], F32, kind="ExternalInput").ap()
        self.a_w_in = dt("a_w_in", [D, PROJ_A], F32, kind="ExternalInput").ap()
        self.a_wgu = dt("a_wgu", [17, 512], F32, kind="ExternalInput").ap()
        self.a_w_out = dt("a_w_out", [D, D], F32, kind="ExternalInput").ap()
        self.b_w_in = dt("b_w_in", [D, 3 * D], F32, kind="ExternalInput").ap()
        self.b_w_out = dt("b_w_out", [D, D], F32, kind="ExternalInput").ap()
        self.f_w_up = dt("f_w_up", [2, D, 2 * DFF], F32, kind="ExternalInput").ap()
        self.f_w_down = dt("f_w_down", [2, DFF, D], F32, kind="ExternalInput").ap()
        self.out = dt("out", [T, D], F32, kind="ExternalOutput").ap()

        self.pe = Q(nc, nc.tensor, "pe", is_pe=True)
        self.act = Q(nc, nc.scalar, "act")
        self.dve = Q(nc, nc.vector, "dve")
        self.gp = Q(nc, nc.gpsimd, "gp")
        self.sp = DmaQ(nc, nc.sync, "sp", 4)
        self.pool = DmaQ(nc, nc.gpsimd, "pool", NSLOT)
        self.ring = Ring(nc, self.pool, NSLOT)

        self.XT = nc.alloc_sbuf_tensor("XT", [128, KC, T], F32)
        self.bXT = [[Buf("XT%d_%d" % (k, t)) for t in range(4)] for k in range(KC)]
        self.vecs = nc.alloc_sbuf_tensor("vecs_sb", [128, NV], F32)
        self.consts = nc.alloc_sbuf_tensor("consts_sb", [128, NCONST], F32)
        self.onesD = nc.alloc_sbuf_tensor("onesD", [128, 128], BF16)
        self.onesV = nc.alloc_sbuf_tensor("onesV", [128, 128], BF16)
        self.bconst = Buf("const")
        self.cst = nc.alloc_sbuf_tensor("cst", [128, 2], F32)
        self.PS = nc.alloc_psum_tensor("PS", [128, 8, 512], F32)
        self.bPS = [Buf("ps%d" % i) for i in range(8)]
        self.ident = self.consts[:, 0:128]
        self.M1 = self.consts[:, 128:256]
        self.M2 = self.consts[:, 256:258]

        self.build()

    def sbt(self, name, shape, dtype):
        self._uid = getattr(self, "_uid", 0) + 1
        return self.nc.sbuf_tensor("%s_u%d" % (name, self._uid), shape, dtype)

    def dump(self, name, ap, shape, dtype, bufs):
        if not self.dbg or name in self.dumped:
            return
        self.dumped.add(name)
        d = self.nc.dram_tensor("dbg_" + name, list(shape), dtype, kind="ExternalOutput").ap()
        self.barrier()
        for q in (self.pe, self.act, self.dve):
            if q.n > 0:
                self.sp.wait(Tok(q.sem, q.n))
        t = self.sp.dma(d, ap, bufs, [])
        for q in (self.pe, self.act, self.dve):
            q.wait(t)

    def vcol(self, name, j=0):
        c = VC[name] + j
        return self.vecs[:, c:c + 1]

    def barrier(self):
        qs = [self.pe, self.act, self.dve]
        for q in qs:
            for p in qs + [self.gp]:
                if p is not q and p.n > 0:
                    q.wait(Tok(p.sem, p.n))

    def evac_alt(self, idx):
        return self.act if idx % 2 == 0 else self.dve

    def copy_on(self, q, out, in_, reads, writes):
        if q is self.act:
            return q.op(reads, writes, lambda: self.nc.scalar.copy(out, in_))
        return q.op(reads, writes, lambda: self.nc.vector.tensor_copy(out, in_))

    def build(self):
        nc = self.nc
        st = self.stages
        self.sp.dma(self.vecs[:], self.vecs_d, [], [self.bconst])
        self.sp.dma(self.consts[:], self.consts_d, [], [self.bconst])
        self.dve.op([], [self.bconst], lambda: nc.vector.memset(self.onesD[:], 1.0 / D))
        self.dve.op([], [self.bconst], lambda: nc.vector.memset(self.onesV[:], 1.0 / DV))
        self.dve.op([], [self.bconst], lambda: nc.vector.memset(self.cst[:, 0:1], EPS))
        self.dve.op([], [self.bconst], lambda: nc.vector.memset(self.cst[:, 1:2], 1.0))
        self.load_x()
        if "gla" in st:
            self.gla()
        with ExitStack() as eso:
            self.XN = eso.enter_context(self.sbt("xn_shared", [128, KC, T], BF16))
            self.bXNs = [[Buf("xn") for _ in range(4)] for _ in range(KC)]
            self.scr = self.norm_scratch(eso)
            self.ostg = [eso.enter_context(self.sbt("ostage%d" % i, [128, D], F32)) for i in range(2)]
            self.bost = [Buf("ost%d" % i) for i in range(2)]
            self.out_toks = []
            phases = [p for p in ("ffn0", "sconv", "ffn1") if p in st]
            gname = {"ffn0": "f_norm0", "sconv": "b_norm", "ffn1": "f_norm1"}
            do_fnorm = "fnorm" in st

            def norm_tile_for(ph):
                return lambda tt: self.norm_tile_to(self.scr, self.XN, self.bXNs, gname[ph], tt, tt)

            def final_tile(tt):
                if do_fnorm:
                    self.final_norm_tile(tt)
                if tt >= 1:
                    self.store_tile(tt - 1)

            if phases:
                for tt in range(4):
                    norm_tile_for(phases[0])(tt)
            for k, ph in enumerate(phases):
                nxt = norm_tile_for(phases[k + 1]) if k + 1 < len(phases) else final_tile
                if ph == "sconv":
                    self.sconv(nxt)
                else:
                    self.ffn(int(ph[-1]), nxt)
            if not phases:
                for tt in range(4):
                    final_tile(tt)
            self.store_tile(3)
            for t in self.out_toks[-4:]:
                self.sp.wait(t)
            for t in self.out_toks[-4:]:
                self.pe.wait(t)

    def load_x(self):
        nc = self.nc
        with ExitStack() as es:
            stg = [es.enter_context(self.sbt("xstage%d" % i, [128, 4, D], F32)) for i in range(2)]
            bst = [Buf("xst0"), Buf("xst1")]
            xv = self.x.rearrange("(g i p) d -> g p i d", i=4, p=128)
            n = 0
            for tg in range(4):
                s = tg % 2
                self.sp.dma(stg[s][:], xv[tg], [], [bst[s]])
                for kc in range(KC):
                    b = n % 8
                    n += 1

                    def tr():
                        ins = None
                        for i in range(4):
                            ins = nc.tensor.transpose(self.PS[:, b, i * 128:(i + 1) * 128],
                                                      stg[s][:, i, kc * 128:(kc + 1) * 128], self.ident)
                        return ins
                    self.pe.op([bst[s], self.bconst], [self.bPS[b]], tr)
                    self.copy_on(self.evac_alt(n), self.XT[:, kc, tg * 512:(tg + 1) * 512], self.PS[:, b, :],
                                 [self.bPS[b]], [self.bXT[kc][tg]])
            self.barrier()

    def norm_scratch(self, es):
        SQ = [es.enter_context(self.sbt("sq%d" % i, [128, KC, 512], BF16)) for i in range(2)]
        RST = [es.enter_context(self.sbt("rstd%d" % i, [128, 512], F32)) for i in range(2)]
        return (SQ, [Buf("sq0"), Buf("sq1")], RST, [Buf("rs0"), Buf("rs1")])

    def norm_stats_tile(self, scr, tt):
        nc = self.nc
        SQ, bSQ, RST, bRS = scr
        s = tt % 2
        c0 = tt * 512
        bank = 6 + s
        self.act.op([self.bXT[kc][tt] for kc in range(KC)], [bSQ[s]],
                    lambda: nc.scalar.activation(SQ[s][:], self.XT[:, :, c0:c0 + 512], AF.Square))

        def mm():
            ins = None
            for kc in range(KC):
                ins = nc.tensor.matmul(self.PS[:, bank, :], self.onesD[:], SQ[s][:, kc, :],
                                       start=(kc == 0), stop=(kc == KC - 1))
            return ins
        self.pe.op([bSQ[s], self.bconst], [self.bPS[bank]], mm)
        self.rstd_from_psum(RST[s][:], self.PS[:, bank, :], [self.bPS[bank]], bRS[s])
        return RST[s], bRS[s]

    def norm_tile_to(self, scr, XN, bXN, gname, t, tt):
        nc = self.nc
        rst, brs = self.norm_stats_tile(scr, tt)
        c0 = tt * 512
        for kc in range(KC):
            self.dve.op([brs, self.bXT[kc][tt]], [bXN[kc][t]],
                        lambda: nc.vector.scalar_tensor_tensor(
                            XN[:, kc, t * 512:(t + 1) * 512], self.XT[:, kc, c0:c0 + 512], self.vcol(gname, kc),
                            rst[:], ALU.mult, ALU.mult))

    def final_norm_tile(self, tt):
        nc = self.nc
        rst, brs = self.norm_stats_tile(self.scr, tt)
        c0 = tt * 512
        for kc in range(KC):
            self.dve.op([brs, self.bXT[kc][tt]], [self.bXT[kc][tt]],
                        lambda: nc.vector.scalar_tensor_tensor(
                            self.XT[:, kc, c0:c0 + 512], self.XT[:, kc, c0:c0 + 512],
                            self.vcol("final_norm", kc), rst[:], ALU.mult, ALU.mult))

    def store_tile(self, tt):
        nc = self.nc
        ov = self.out.rearrange("(i p) d -> i p d", p=128)
        for i in range(tt * 4, tt * 4 + 4):
            s = i % 2
            pb = (i % 3) * 2

            def tr():
                ins = None
                for kc in range(KC):
                    ins = nc.tensor.transpose(self.PS[:, pb + kc // 4, (kc % 4) * 128:(kc % 4 + 1) * 128],
                                              self.XT[:, kc, i * 128:(i + 1) * 128], self.ident)
                return ins
            self.pe.op([self.bXT[kc][tt] for kc in range(KC)] + [self.bconst],
                       [self.bPS[pb], self.bPS[pb + 1]], tr)
            self.copy_on(self.evac_alt(i), self.ostg[s][:], self.PS[:, pb:pb + 2, :],
                         [self.bPS[pb], self.bPS[pb + 1]], [self.bost[s]])
            self.out_toks.append(self.sp.dma(ov[i], self.ostg[s][:], [self.bost[s]], []))

    def rstd_from_psum(self, out, ps, bps, bout):
        nc = self.nc
        self.act.op(bps + [self.bconst], [bout],
                    lambda: nc.scalar.activation(out, ps, AF.Ln, bias=self.cst[:, 0:1]))
        self.act.op([bout], [bout], lambda: nc.scalar.activation(out, out, AF.Exp, scale=-0.5))

    def rmsnorm_to(self, es_tmp, XN, bXN, gname, t0=0, nt=4):
        scr = self.norm_scratch(es_tmp)
        for t in range(nt):
            self.norm_tile_to(scr, XN, bXN, gname, t, t0 + t)

    def ffn(self, l, next_tile):
        nc = self.nc
        groups = [(0, 8), (8, 7), (15, 7)]
        GMAX = 8
        with ExitStack() as es:
            XN = self.XN
            bXN = self.bXNs
            H = es.enter_context(self.sbt("ffn_h", [128, GMAX, T], BF16))
            bH = [[Buf("h%d_%d" % (j, t)) for t in range(2)] for j in range(GMAX)]
            Gb = [es.enter_context(self.sbt("ffn_g%d" % i, [128, 2 + 1024], F32)) for i in range(2)]
            A = [es.enter_context(self.sbt("ffn_a%d" % i, [128, 1024], F32)) for i in range(2)]
            bG = [Buf("g0"), Buf("g1")]
            bA = [Buf("a0"), Buf("a1")]
            self.dve.op([], [bG[0]], lambda: nc.vector.memset(Gb[0][:, 0:2], 0.0))
            wup = self.f_w_up[l].rearrange("(kc p) c -> p kc c", p=128)
            wdn = self.f_w_down[l].rearrange("(j p) c -> j p c", p=128)
            cw = lambda tap, j: self.vcol("f_conv%d%d" % (l, tap), j)
            unit = 0
            nbank = 0
            for (g0, gn) in groups:
                jj = 0
                while jj < gn:
                    nb = min(4, gn - jj)
                    j0 = g0 + jj
                    self.ring.align()
                    wg = self.ring.load(wup[:, :, j0 * 128:(j0 + nb) * 128], 128, [KC, nb * 128])
                    wu = self.ring.load(wup[:, :, DFF + j0 * 128:DFF + (j0 + nb) * 128], 128, [KC, nb * 128])
                    for jb in range(nb):
                        j = j0 + jb
                        for hh in range(2):
                            pb = (unit % 2) * 4
                            s = hh
                            unit += 1

                            def mm(w, bank0):
                                ins = None
                                for kc in range(KC):
                                    for t in range(2):
                                        ins = nc.tensor.matmul(
                                            self.PS[:, bank0 + t, :], w.ap[:, kc, jb * 128:(jb + 1) * 128],
                                            XN[:, kc, (hh * 2 + t) * 512:(hh * 2 + t + 1) * 512],
                                            start=(kc == 0), stop=(kc == KC - 1))
                                return ins
                            rxn = [bXN[kc][hh * 2 + t] for kc in range(KC) for t in range(2)]
                            self.pe.op(rxn + wg.buf, [self.bPS[pb], self.bPS[pb + 1]], lambda: mm(wg, pb))
                            self.pe.op(rxn + wu.buf, [self.bPS[pb + 2], self.bPS[pb + 3]], lambda: mm(wu, pb + 2))
                            psg = self.PS[:, pb:pb + 2, :]
                            psu = self.PS[:, pb + 2:pb + 4, :]
                            bg = [self.bPS[pb], self.bPS[pb + 1]]
                            bu = [self.bPS[pb + 2], self.bPS[pb + 3]]
                            self.act.op(bg, [bA[s]], lambda: nc.scalar.activation(
                                A[s][:], psg, AF.Copy, scale=cw(2, j)))
                            self.act.op(bg, [bG[s]], lambda: nc.scalar.copy(Gb[s][:, 2:2 + 1024], psg))
                            if hh == 1:
                                self.act.op([bG[0]], [bG[1]], lambda: nc.scalar.copy(
                                    Gb[1][:, 0:2], Gb[0][:, 1024:1026]))
                            self.dve.op([bG[s], bA[s]], [bA[s]], lambda: nc.vector.scalar_tensor_tensor(
                                A[s][:], Gb[s][:, 1:1025], cw(1, j), A[s][:], ALU.mult, ALU.add))
                            self.dve.op([bG[s], bA[s]], [bA[s]], lambda: nc.vector.scalar_tensor_tensor(
                                A[s][:], Gb[s][:, 0:1024], cw(0, j), A[s][:], ALU.mult, ALU.add))
                            self.act.op([bA[s]], [bA[s]], lambda: nc.scalar.activation(A[s][:], A[s][:], AF.Silu))
                            self.dve.op([bA[s]] + bu, [bH[jj + jb][hh]], lambda: nc.vector.tensor_tensor(
                                H[:, jj + jb, hh * 1024:(hh + 1) * 1024], A[s][:], psu, ALU.mult))
                    jj += nb
                self.ring.align()
                wd = [self.ring.load(wdn[g0 + k], 128, [D]) for k in range(gn)]
                for tt in range(4):
                    for m in range(KC):
                        b = nbank % 8
                        nbank += 1

                        def mmd():
                            ins = None
                            for k in range(gn):
                                ins = nc.tensor.matmul(self.PS[:, b, :], wd[k].ap[:, m * 128:(m + 1) * 128],
                                                       H[:, k, tt * 512:(tt + 1) * 512],
                                                       start=(k == 0), stop=(k == gn - 1))
                            return ins
                        rd = [bH[k][tt // 2] for k in range(gn)]
                        for k in range(gn):
                            rd += wd[k].buf
                        self.pe.op(rd, [self.bPS[b]], mmd)
                        self.dve.op([self.bPS[b], self.bXT[m][tt]], [self.bXT[m][tt]],
                                    lambda: nc.vector.tensor_tensor(
                                        self.XT[:, m, tt * 512:(tt + 1) * 512],
                                        self.XT[:, m, tt * 512:(tt + 1) * 512], self.PS[:, b, :], ALU.add))
                    if g0 == groups[-1][0] and tt >= 1:
                        next_tile(tt - 1)
            next_tile(3)

    def sconv(self, next_tile):
        nc = self.nc
        with ExitStack() as es:
            XN = self.XN
            bXN = self.bXNs
            Y = es.enter_context(self.sbt("sc_y", [128, KC, T], BF16))
            bY = [[Buf("y%d_%d" % (j, t)) for t in range(4)] for j in range(KC)]
            Cs = [es.enter_context(self.sbt("sc_c%d" % i, [128, 512], F32)) for i in range(2)]
            Zb = [es.enter_context(self.sbt("sc_z%d" % i, [128, 2 + 512], F32)) for i in range(2)]
            A = [es.enter_context(self.sbt("sc_a%d" % i, [128, 512], F32)) for i in range(2)]
            Bs = [es.enter_context(self.sbt("sc_b%d" % i, [128, 512], F32)) for i in range(2)]
            bB = [Buf("b0"), Buf("b1")]
            bC = [Buf("c0"), Buf("c1")]
            bZ = [Buf("z0"), Buf("z1")]
            bA = [Buf("a0"), Buf("a1")]
            win = self.b_w_in.rearrange("(kc p) c -> p kc c", p=128)
            wout = self.b_w_out.rearrange("(j p) c -> j p c", p=128)
            cw = lambda tap, j: self.vcol("b_conv%d" % tap, j)
            NB = 2
            for j0 in range(0, KC, NB):
                self.ring.align()
                ws = [self.ring.load(win[:, :, sg * D + j0 * 128: sg * D + (j0 + NB) * 128], 128, [KC, NB * 128])
                      for sg in range(3)]
                for jb in range(NB):
                    j = j0 + jb
                    for tt in range(4):
                        s = tt % 2
                        pb = s * 3

                        def mm(w, bank):
                            ins = None
                            for kc in range(KC):
                                ins = nc.tensor.matmul(self.PS[:, bank, :], w.ap[:, kc, jb * 128:(jb + 1) * 128],
                                                       XN[:, kc, tt * 512:(tt + 1) * 512],
                                                       start=(kc == 0), stop=(kc == KC - 1))
                            return ins
                        for sg in range(3):
                            self.pe.op([bXN[kc][tt] for kc in range(KC)] + ws[sg].buf, [self.bPS[pb + sg]],
                                       lambda: mm(ws[sg], pb + sg))
                        self.act.op([self.bPS[pb + 1]], [bC[s]], lambda: nc.scalar.copy(Cs[s][:], self.PS[:, pb + 1, :]))
                        self.act.op([self.bPS[pb]], [bB[s]], lambda: nc.scalar.copy(Bs[s][:], self.PS[:, pb, :]))
                        if tt == 0:
                            self.dve.op([], [bZ[s]], lambda: nc.vector.memset(Zb[s][:, 0:2], 0.0))
                        else:
                            self.act.op([bZ[1 - s]], [bZ[s]], lambda: nc.scalar.copy(
                                Zb[s][:, 0:2], Zb[1 - s][:, 512:514]))
                        self.dve.op([bC[s], self.bPS[pb + 2]], [bZ[s]], lambda: nc.vector.tensor_tensor(
                            Zb[s][:, 2:514], Cs[s][:], self.PS[:, pb + 2, :], ALU.mult))
                        self.act.op([bZ[s]], [bA[s]], lambda: nc.scalar.activation(
                            A[s][:], Zb[s][:, 2:514], AF.Copy, scale=cw(2, j)))
                        self.dve.op([bZ[s], bA[s]], [bA[s]], lambda: nc.vector.scalar_tensor_tensor(
                            A[s][:], Zb[s][:, 1:513], cw(1, j), A[s][:], ALU.mult, ALU.add))
                        self.dve.op([bZ[s], bA[s]], [bA[s]], lambda: nc.vector.scalar_tensor_tensor(
                            A[s][:], Zb[s][:, 0:512], cw(0, j), A[s][:], ALU.mult, ALU.add))
                        self.dve.op([bA[s], bB[s]], [bY[j][tt]], lambda: nc.vector.tensor_tensor(
                            Y[:, j, tt * 512:(tt + 1) * 512], A[s][:], Bs[s][:], ALU.mult))
            self.out_proj(wout, Y, lambda kc, tg: [bY[kc][tg]], range(4), lambda tg: tg, next_tile)
            next_tile(3)

    def out_proj(self, wout, Y, ybufs, tiles, gtile, next_tile=None):
        nc = self.nc
        self.ring.align()
        wo = [self.ring.load(wout[kc], 128, [D]) for kc in range(KC)]
        nb = 0
        for t in tiles:
            tg = gtile(t)
            for m in range(KC):
                b = 6 + nb % 2
                nb += 1

                def mmo():
                    ins = None
                    for kc in range(KC):
                        ins = nc.tensor.matmul(self.PS[:, b, :], wo[kc].ap[:, m * 128:(m + 1) * 128],
                                               Y[:, kc, t * 512:(t + 1) * 512],
                                               start=(kc == 0), stop=(kc == KC - 1))
                    return ins
                rd = []
                for kc in range(KC):
                    rd += ybufs(kc, t) + wo[kc].buf
                self.pe.op(rd, [self.bPS[b]], mmo)
                self.dve.op([self.bPS[b], self.bXT[m][tg]], [self.bXT[m][tg]],
                            lambda: nc.vector.tensor_tensor(
                                self.XT[:, m, tg * 512:(tg + 1) * 512],
                                self.XT[:, m, tg * 512:(tg + 1) * 512], self.PS[:, b, :], ALU.add))
            if next_tile is not None and t >= 1:
                next_tile(t - 1)

    def gla(self):
        nc = self.nc
        PS = self.PS
        bPS = self.bPS
        win = self.a_w_in.rearrange("(kc p) c -> p kc c", p=128)
        wout = self.a_w_out.rearrange("(j p) c -> j p c", p=128)
        nbank = [0]

        def nextbank():
            b = nbank[0] % 8
            nbank[0] += 1
            return b

        with ExitStack() as es:
            S = [es.enter_context(self.sbt("gl_s%d" % i, [128, HEADS, DV], F32)) for i in range(2)]
            bS = [[Buf("s") for _ in range(HEADS)] for _ in range(2)]
            WGU = es.enter_context(self.sbt("gl_wgu", [32, 512], BF16))
            bWGU = Buf("wgu")
            self.dve.op([], [bWGU], lambda: nc.vector.memset(WGU[:], 0.0))
            self.pool.dma(WGU[0:17, :], self.a_wgu, [], [bWGU])
            self.dve.op([], bS[1], lambda: nc.vector.memset(S[1][:], 0.0))
            for hf in range(2):
                with ExitStack() as esh:
                    QT = esh.enter_context(self.sbt("gl_qt", [128, HEADS, 1024], BF16))
                    bQT = [[Buf("qt") for _ in range(2)] for _ in range(HEADS)]
                    GL = esh.enter_context(self.sbt("gl_gl", [32, 1024], BF16))
                    bGL = Buf("gl")
                    KD = esh.enter_context(self.sbt("gl_kd", [128, 8, 512], BF16))
                    bKD = [Buf("kd") for _ in range(8)]
                    V = esh.enter_context(self.sbt("gl_v", [128, 8, 1024], BF16))
                    bV = [Buf("v") for _ in range(8)]
                    SR = esh.enter_context(self.sbt("gl_sr", [128, KC, 1024], BF16))
                    bSR = [[Buf("sr") for _ in range(2)] for _ in range(KC)]
                    DEC = esh.enter_context(self.sbt("gl_dec", [128, HEADS, 16], F32))
                    bDEC = [Buf("dec") for _ in range(8)]
                    with ExitStack() as esa:
                        XN = esa.enter_context(self.sbt("gl_xn", [128, KC, 1024], BF16))
                        bXN = [[Buf("xn") for _ in range(2)] for _ in range(KC)]
                        SPt = [esa.enter_context(self.sbt("gl_sp%d" % i, [128, 512], F32)) for i in range(2)]
                        E = [esa.enter_context(self.sbt("gl_e%d" % i, [128, 512], F32)) for i in range(2)]
                        bSP = [Buf("sp0"), Buf("sp1")]
                        bE = [Buf("e0"), Buf("e1")]
                        with ExitStack() as es2:
                            self.rmsnorm_to(es2, XN, bXN, "a_norm", t0=hf * 2, nt=2)
                        self.dve.op([], [bGL], lambda: nc.vector.memset(GL[:], 0.0))
                        self.dve.op([], [bGL], lambda: nc.vector.memset(GL[0:17, :], 1.0))
                        self.ring.align()
                        wgl = self.ring.load(win[:, :, 3072:3088], 128, [KC, 16])
                        for tt in range(2):
                            b = nextbank()

                            def mmg():
                                ins = None
                                for kc in range(KC):
                                    ins = nc.tensor.matmul(PS[0:16, b, :], wgl.ap[:, kc, :],
                                                           XN[:, kc, tt * 512:(tt + 1) * 512],
                                                           start=(kc == 0), stop=(kc == KC - 1))
                                return ins
                            self.pe.op([bXN[kc][tt] for kc in range(KC)] + wgl.buf, [bPS[b]], mmg)
                            self.act.op([bPS[b]], [bGL], lambda: nc.scalar.copy(
                                GL[0:16, tt * 512:(tt + 1) * 512], PS[0:16, b, :]))
                        wq = self.ring.load(win[:, :, 0:512], 128, [KC, 512])
                        for j in range(HEADS):
                            for tt in range(2):
                                b = nextbank()

                                def mmq():
                                    ins = None
                                    for kc in range(KC):
                                        ins = nc.tensor.matmul(PS[:, b, :], wq.ap[:, kc, j * 128:(j + 1) * 128],
                                                               XN[:, kc, tt * 512:(tt + 1) * 512],
                                                               start=(kc == 0), stop=(kc == KC - 1))
                                    return ins
                                self.pe.op([bXN[kc][tt] for kc in range(KC)] + wq.buf, [bPS[b]], mmq)
                                self.act.op([bPS[b]], [bQT[j][tt]], lambda: nc.scalar.activation(
                                    QT[:, j, tt * 512:(tt + 1) * 512], PS[:, b, :], AF.Copy, scale=float(DK) ** -0.5))
                        self.ring.align()
                        wk = self.ring.load(win[:, :, 512:1024], 128, [KC, 512])

                        def gate_pre(i):
                            s = i % 2
                            b1 = nextbank()
                            self.pe.op([bGL, bWGU], [bPS[b1]], lambda: nc.tensor.matmul(
                                PS[:, b1, :], GL[:, i * 128:(i + 1) * 128], WGU[:, :], start=True, stop=True))
                            self.act.op([bPS[b1]], [bSP[s]], lambda: nc.scalar.activation(
                                SPt[s][:], PS[:, b1, :], AF.Exp, scale=-1.0))
                            self.act.op([bSP[s], self.bconst], [bSP[s]], lambda: nc.scalar.activation(
                                SPt[s][:], SPt[s][:], AF.Ln, bias=self.cst[:, 1:2]))

                        gate_pre(0)
                        for i in range(8):
                            s = i % 2
                            if i + 1 < 8:
                                gate_pre(i + 1)
                            b4 = nextbank()

                            def mmk():
                                ins = None
                                for kc in range(KC):
                                    ins = nc.tensor.matmul(PS[:, b4, :], XN[:, kc, i * 128:(i + 1) * 128], wk.ap[:, kc, :],
                                                           start=(kc == 0), stop=(kc == KC - 1))
                                return ins
                            self.pe.op([bXN[kc][i // 4] for kc in range(KC)] + wk.buf, [bPS[b4]], mmk)
                            b2 = nextbank()
                            self.pe.op([bSP[s], self.bconst], [bPS[b2]], lambda: nc.tensor.matmul(
                                PS[:, b2, :], self.M1, SPt[s][:], start=True, stop=True))
                            b3 = nextbank()

                            def mmt():
                                ins = None
                                for h in range(HEADS):
                                    ins = nc.tensor.matmul(PS[:, b3, h * 2:h * 2 + 2], SPt[s][:, h * 128:(h + 1) * 128],
                                                           self.M2, start=True, stop=True)
                                return ins
                            self.pe.op([bSP[s], self.bconst], [bPS[b3]], mmt)
                            self.act.op([bPS[b2]], [bE[s]], lambda: nc.scalar.activation(E[s][:], PS[:, b2, :], AF.Exp))
                            self.act.op([bPS[b3]], [bDEC[i]], lambda: nc.scalar.activation(
                                DEC[:, :, 2 * i:2 * i + 2],
                                PS[:, b3, 0:8].rearrange("p (h c) -> p h c", c=2), AF.Exp))
                            self.dve.op([bPS[b4], bE[s]], [bKD[i]], lambda: nc.vector.tensor_tensor(
                                KD[:, i, :], PS[:, b4, :], E[s][:], ALU.mult))
                        self.ring.align()
                        for vb in range(2):
                            wv = self.ring.load(win[:, :, 1024 + vb * 512:1024 + (vb + 1) * 512], 128, [KC, 512])
                            for i in range(8):
                                b = nextbank()

                                def mmv():
                                    ins = None
                                    for kc in range(KC):
                                        ins = nc.tensor.matmul(PS[:, b, :], XN[:, kc, i * 128:(i + 1) * 128],
                                                               wv.ap[:, kc, :], start=(kc == 0), stop=(kc == KC - 1))
                                    return ins
                                self.pe.op([bXN[kc][i // 4] for kc in range(KC)] + wv.buf, [bPS[b]], mmv)
                                self.copy_on(self.evac_alt(i), V[:, i, vb * 512:(vb + 1) * 512], PS[:, b, :],
                                             [bPS[b]], [bV[i]])
                        self.ring.align()
                        for rb in range(2):
                            wr = self.ring.load(win[:, :, 2048 + rb * 512:2048 + (rb + 1) * 512], 128, [KC, 512])
                            for jb in range(4):
                                j = rb * 4 + jb
                                for tt in range(2):
                                    b = nextbank()

                                    def mmr():
                                        ins = None
                                        for kc in range(KC):
                                            ins = nc.tensor.matmul(PS[:, b, :], wr.ap[:, kc, jb * 128:(jb + 1) * 128],
                                                                   XN[:, kc, tt * 512:(tt + 1) * 512],
                                                                   start=(kc == 0), stop=(kc == KC - 1))
                                        return ins
                                    self.pe.op([bXN[kc][tt] for kc in range(KC)] + wr.buf, [bPS[b]], mmr)
                                    self.act.op([bPS[b]], [bSR[j][tt]], lambda: nc.scalar.activation(
                                        SR[:, j, tt * 512:(tt + 1) * 512], PS[:, b, :], AF.Silu))
                    self.dump("GL", GL[:], [32, 1024], BF16, [])
                    self.dump("QT", QT[:], [128, HEADS, 1024], BF16, [])
                    self.dump("KD", KD[:], [128, 8, 512], BF16, [])
                    self.dump("V", V[:], [128, 8, 1024], BF16, [])
                    self.dump("SR", SR[:], [128, KC, 1024], BF16, [])
                    self.dump("DEC", DEC[:], [128, HEADS, 16], F32, [])
                    self.barrier()
                    with ExitStack() as esb:
                        TQ = 4
                        TW = TQ * CH
                        SB = [esb.enter_context(self.sbt("gl_sb%d" % i, [128, HEADS, DV], BF16)) for i in range(2)]
                        bSB = [Buf("sb0"), Buf("sb1")]
                        OT2 = [esb.enter_context(self.sbt("gl_ot%d" % i, [128, KC, TW], F32)) for i in range(2)]
                        OSQ2 = [esb.enter_context(self.sbt("gl_osq%d" % i, [128, KC, TW], BF16)) for i in range(2)]
                        RS = esb.enter_context(self.sbt("gl_rs", [128, HEADS, TW], F32))
                        OF2 = [esb.enter_context(self.sbt("gl_of%d" % i, [128, KC, TW], BF16)) for i in range(2)]
                        TMP = [esb.enter_context(self.sbt("gl_tmp%d" % i, [128, TW], F32)) for i in range(2)]
                        bTMP = [Buf("tmp0"), Buf("tmp1")]
                        bOF2 = [[Buf("of") for _ in range(KC)] for _ in range(2)]
                        bOT2 = [[Buf("ot") for _ in range(TQ)] for _ in range(2)]
                        bOSQ2 = [[Buf("osq") for _ in range(TQ)] for _ in range(2)]
                        bRS = [Buf("rs") for _ in range(HEADS)]

                        def emit_upd(c):
                            i = c // 2
                            part = (c % 2) * 64
                            su = (c % 2) * 2

                            def mmu():
                                ins = None
                                for h in range(HEADS):
                                    ins = nc.tensor.matmul(
                                        PS[:, su + h // 2, (h % 2) * 256:(h % 2 + 1) * 256],
                                        KD[part:part + 64, i, h * 128:(h + 1) * 128],
                                        V[part:part + 64, i, h * 256:(h + 1) * 256], start=True, stop=True)
                                return ins
                            self.pe.op([bKD[i], bV[i]], [bPS[su], bPS[su + 1]], mmu)

                        def emit_state(c):
                            i = c // 2
                            su = (c % 2) * 2
                            sn = (hf * 16 + c) % 2
                            so = 1 - sn
                            for h in range(HEADS):
                                self.dve.op([bS[so][h], bDEC[i], bPS[su + h // 2]], [bS[sn][h]],
                                            lambda: nc.vector.scalar_tensor_tensor(
                                                S[sn][:, h, :], S[so][:, h, :], DEC[:, h, c:c + 1],
                                                PS[:, su + h // 2, (h % 2) * 256:(h % 2 + 1) * 256],
                                                ALU.mult, ALU.add))
                            self.act.op(bS[sn], [bSB[c % 2]], lambda: nc.scalar.copy(SB[c % 2][:], S[sn][:]))

                        def emit_o(c):
                            tq = c // TQ
                            cl = c % TQ
                            bo = 4 + c % 2

                            def mmo():
                                ins = None
                                for j in range(KC):
                                    h, hv = j // 2, j % 2
                                    ins = nc.tensor.matmul(PS[:, bo, j * 64:(j + 1) * 64],
                                                           SB[c % 2][:, h, hv * 128:(hv + 1) * 128],
                                                           QT[:, h, c * 64:(c + 1) * 64], start=True, stop=True)
                                return ins
                            self.pe.op([bSB[c % 2]] + [bQT[h][c // 8] for h in range(HEADS)], [bPS[bo]], mmo)
                            pso = PS[:, bo, :].rearrange("p (j l) -> p j l", l=64)
                            self.act.op([bPS[bo]], [bOT2[tq % 2][cl]], lambda: nc.scalar.copy(
                                OT2[tq % 2][:, :, cl * 64:(cl + 1) * 64], pso))
                            self.act.op([bPS[bo]], [bOSQ2[tq % 2][cl]], lambda: nc.scalar.activation(
                                OSQ2[tq % 2][:, :, cl * 64:(cl + 1) * 64], pso, AF.Square))

                        def emit_post(tq):
                            ot, osq, of = OT2[tq % 2], OSQ2[tq % 2], OF2[tq % 2]
                            bot, bosq, bof = bOT2[tq % 2], bOSQ2[tq % 2], bOF2[tq % 2]
                            for h in range(HEADS):
                                bn = 6 + h % 2

                                def mmn():
                                    nc.tensor.matmul(PS[:, bn, 0:TW], self.onesV[:], osq[:, 2 * h, :], start=True, stop=False)
                                    return nc.tensor.matmul(PS[:, bn, 0:TW], self.onesV[:], osq[:, 2 * h + 1, :],
                                                            start=False, stop=True)
                                self.pe.op(bosq + [self.bconst], [bPS[bn]], mmn)
                                self.rstd_from_psum(RS[:, h, :], PS[:, bn, 0:TW], [bPS[bn]], bRS[h])
                            wo = wo_once
                            for j in range(KC):
                                self.dve.op(bot + [bRS[j // 2]], [bTMP[j % 2]], lambda: nc.vector.scalar_tensor_tensor(
                                    TMP[j % 2][:], ot[:, j, :], self.vcol("a_gn", j), RS[:, j // 2, :],
                                    ALU.mult, ALU.mult))
                                self.dve.op([bTMP[j % 2], bSR[j][tq // 2]], [bof[j]], lambda: nc.vector.tensor_tensor(
                                    of[:, j, :], TMP[j % 2][:], SR[:, j, tq * TW:(tq + 1) * TW], ALU.mult))
                            tg = hf * 2 + tq // 2
                            c0 = tg * 512 + (tq % 2) * TW
                            groups = []
                            for m in range(KC):
                                def grp(m=m):
                                    b = 6 + m % 2

                                    def mmo2():
                                        ins = None
                                        for kc in range(KC):
                                            ins = nc.tensor.matmul(PS[:, b, 0:TW], wo[kc].ap[:, m * 128:(m + 1) * 128],
                                                                   of[:, kc, :], start=(kc == 0), stop=(kc == KC - 1))
                                        return ins
                                    rd = list(bof)
                                    for kc in range(KC):
                                        rd += wo[kc].buf
                                    self.pe.op(rd, [bPS[b]], mmo2)
                                    self.dve.op([bPS[b], self.bXT[m][tg]], [self.bXT[m][tg]],
                                                lambda: nc.vector.tensor_tensor(
                                                    self.XT[:, m, c0:c0 + TW],
                                                    self.XT[:, m, c0:c0 + TW], PS[:, b, 0:TW], ALU.add))
                                groups.append(grp)
                            return groups

                        self.ring.align()
                        wo_once = [self.ring.load(wout[kc], 128, [D]) for kc in range(KC)]
                        pending = []
                        emit_upd(0)
                        emit_state(0)
                        emit_upd(1)
                        for c in range(16):
                            if c + 2 < 16:
                                emit_upd(c + 2)
                            if c + 1 < 16:
                                emit_state(c + 1)
                            emit_o(c)
                            for _ in range(2):
                                if pending:
                                    pending.pop(0)()
                            if c % TQ == TQ - 1:
                                while pending:
                                    pending.pop(0)()
                                pending = emit_post(c // TQ)
                        while pending:
                            pending.pop(0)()
                    self.barrier()


def _chunkcols(v):
    v = np.asarray(v, dtype=np.float32)
    return np.ascontiguousarray(v.reshape(-1, 128).T)


def make_consts():
    c = np.zeros((128, NCONST), dtype=np.float32)
    c[:, 0:128] = np.eye(128, dtype=np.float32)
    lp = np.arange(128)[:, None]
    l = np.arange(128)[None, :]
    c[:, 128:256] = np.where((lp > l) & (lp // CH == l // CH), -1.0 / 16.0, 0.0)
    c[:, 256:258] = np.where(lp // CH == np.arange(2)[None, :], -1.0 / 16.0, 0.0)
    return c


def make_in_maps(inp, ncores=NCORES):
    vec = np.zeros((128, NV), dtype=np.float32)

    def put(name, v):
        a = _chunkcols(v)
        vec[:, VC[name]:VC[name] + a.shape[1]] = a
    put("a_norm", inp["a_norm"][0])
    put("b_norm", inp["b_norm"][0])
    put("f_norm0", inp["f_norm"][0])
    put("f_norm1", inp["f_norm"][1])
    put("final_norm", inp["final_norm"])
    put("a_gn", inp["a_gn"][0])
    for t in range(3):
        put("b_conv%d" % t, inp["b_conv"][0][t])
        put("f_conv0%d" % t, inp["f_conv"][0][t])
        put("f_conv1%d" % t, inp["f_conv"][1][t])
    wgu = np.ascontiguousarray(np.concatenate(
        [np.asarray(inp["a_w_gate_up"][0], dtype=np.float32),
         np.asarray(inp["a_b_gate"][0], dtype=np.float32)[None, :]], axis=0))
    shared = {
        "vecs": vec,
        "consts": make_consts(),
        "a_w_in": np.ascontiguousarray(inp["a_w_in"][0], dtype=np.float32),
        "a_wgu": wgu,
        "a_w_out": np.ascontiguousarray(inp["a_w_out"][0], dtype=np.float32),
        "b_w_in": np.ascontiguousarray(inp["b_w_in"][0], dtype=np.float32),
        "b_w_out": np.ascontiguousarray(inp["b_w_out"][0], dtype=np.float32),
        "f_w_up": np.ascontiguousarray(inp["f_w_up"], dtype=np.float32),
        "f_w_down": np.ascontiguousarray(inp["f_w_down"], dtype=np.float32),
    }
    x = np.asarray(inp["x"], dtype=np.float32)
    maps = []
    for c in range(ncores):
        m = dict(shared)
        m["x"] = np.ascontiguousarray(x[c])
        maps.append(m)
    return maps


ALL_STAGES = ("gla", "ffn0", "sconv", "ffn1", "fnorm")
_PROG_CACHE = {}


def get_prog(stages=ALL_STAGES):
    key = tuple(stages)
    if key not in _PROG_CACHE:
        _PROG_CACHE[key] = Prog(key)
    return _PROG_CACHE[key]


def kernel(**inputs):
    prog = get_prog(ALL_STAGES)
    in_maps = make_in_maps(inputs)
    res = run_bass_kernel_spmd(prog.nc, in_maps, core_ids=list(range(NCORES)))
    return np.stack([np.asarray(r["out"], dtype=np.float32) for r in res.results], axis=0)
```
